# Optimizing a Trainium2 kernel written in Bass

```python
import math
import jax, jax.numpy as jnp
from jax import lax
import numpy as np

D_MODEL = 1024
BATCH = 2
SEQ = 8192
DEPTH = 4

N_MIXERS = 3
MIXER_ORDER = ('gla', 'diff', 'sgu')
EPS = 1e-6
NEG_INF = -1e30
D_FF = 4 * D_MODEL

GLA_HEADS = 4
GLA_DK = D_MODEL // 2
GLA_DV = D_MODEL
GLA_HK = GLA_DK // GLA_HEADS
GLA_HV = GLA_DV // GLA_HEADS
GLA_RANK = 16
GLA_GATE_NORM = 16.0
GLA_CHUNK = 64
GLA_IN = 2 * GLA_DK + 2 * GLA_DV

DIFF_HEAD_DIM = 64
DIFF_HEADS = D_MODEL // (2 * DIFF_HEAD_DIM)
DIFF_QBLOCK = 128
DIFF_IN = 4 * DIFF_HEADS * DIFF_HEAD_DIM + 2 * DIFF_HEADS * DIFF_HEAD_DIM

SGU_CHUNK = 128
SGU_WIDTH = D_MODEL
SGU_GROUPS = 8
SGU_GROUP_DIM = SGU_WIDTH // SGU_GROUPS

kernel_name = 'hybrid_gla_diffattn_sgu_trunk'


def rmsnorm(x, g):
    xf = x.astype(jnp.float32)
    y = xf * lax.rsqrt(jnp.mean(xf * xf, axis=-1, keepdims=True) + EPS)
    return (y * g.astype(jnp.float32)).astype(x.dtype)


def head_rms(o):
    return o * lax.rsqrt(jnp.mean(o * o, axis=-1, keepdims=True) + EPS)


def alibi_slopes(n_heads):
    return 2.0 ** (-8.0 * jnp.arange(1, n_heads + 1, dtype=jnp.float32) / n_heads)


def gla_mixer(h, w_in, gate_w1, gate_w2, gate_b, head_norm, w_out):
    B, T, _ = h.shape
    C = GLA_CHUNK
    nC = T // C
    f32 = jnp.float32
    proj = h @ w_in
    q, k, v, g = jnp.split(proj, [GLA_DK, 2 * GLA_DK, 2 * GLA_DK + GLA_DV], axis=-1)
    log_a = jax.nn.log_sigmoid(((h @ gate_w1) @ gate_w2 + gate_b).astype(f32)) / GLA_GATE_NORM

    def chunked(t, d):
        return t.reshape(B, nC, C, GLA_HEADS, d).transpose(0, 3, 1, 2, 4).astype(f32)

    q = chunked(q, GLA_HK) * (GLA_HK ** -0.5)
    k = chunked(k, GLA_HK)
    v = chunked(v, GLA_HV)
    b = jnp.cumsum(chunked(log_a, GLA_HK), axis=3)
    b_last = b[:, :, :, -1:, :]
    q_dec = q * jnp.exp(b)
    k_inv = k * jnp.exp(-b)
    k_to_end = k * jnp.exp(b_last - b)
    causal = jnp.tril(jnp.ones((C, C), dtype=bool))
    att = jnp.where(causal, jnp.einsum('bhncd,bhnsd->bhncs', q_dec, k_inv), 0.0)
    o_intra = jnp.einsum('bhncs,bhnsv->bhncv', att, v)
    kv = jnp.einsum('bhncd,bhncv->bhndv', k_to_end, v)
    decay = jnp.exp(b_last[:, :, :, 0, :])

    def step(S, inp):
        kv_n, dec_n = inp
        return dec_n[..., None] * S + kv_n, S

    S0 = jnp.zeros((B, GLA_HEADS, GLA_HK, GLA_HV), f32)
    _, S_prev = lax.scan(step, S0, (kv.transpose(2, 0, 1, 3, 4), decay.transpose(2, 0, 1, 3)))
    S_prev = S_prev.transpose(1, 2, 0, 3, 4)
    o = o_intra + jnp.einsum('bhncd,bhndv->bhncv', q_dec, S_prev)
    o = head_rms(o) * head_norm.astype(f32)
    o = o.transpose(0, 2, 3, 1, 4).reshape(B, T, GLA_DV)
    o = o * jax.nn.silu(g.astype(f32))
    return o.astype(h.dtype) @ w_out


def diff_mixer(h, w_in, lq1, lk1, lq2, lk2, head_norm, w_out, layer_idx):
    B, T, _ = h.shape
    H, Dh, QB = DIFF_HEADS, DIFF_HEAD_DIM, DIFF_QBLOCK
    nQ = T // QB
    f32 = jnp.float32
    lambda_init = 0.8 - 0.6 * math.exp(-0.3 * layer_idx)
    lam = (jnp.exp(jnp.sum(lq1.astype(f32) * lk1.astype(f32)))
           - jnp.exp(jnp.sum(lq2.astype(f32) * lk2.astype(f32))) + lambda_init)
    proj = h @ w_in
    q, k, v = jnp.split(proj, [2 * H * Dh, 4 * H * Dh], axis=-1)
    q = q.reshape(B, T, H, 2, Dh) * (Dh ** -0.5)
    k = k.reshape(B, T, H, 2, Dh)
    v = v.reshape(B, T, H, 2 * Dh)
    slopes = alibi_slopes(H)[:, None, None, None]
    pos_k = jnp.arange(T)
    q_blocks = jnp.moveaxis(q.reshape(B, nQ, QB, H, 2, Dh), 1, 0)

    def attend_block(args):
        q_blk, blk = args
        pos_q = blk * QB + jnp.arange(QB)
        dist = pos_q[:, None] - pos_k[None, :]
        s = jnp.einsum('bqhrd,bkhrd->bhrqk', q_blk, k).astype(f32)
        s = jnp.where(dist >= 0, s - slopes * dist.astype(f32), NEG_INF)
        p = jax.nn.softmax(s, axis=-1)
        p = p[:, :, 0] - lam * p[:, :, 1]
        return jnp.einsum('bhqk,bkhv->bqhv', p.astype(v.dtype), v)

    o = lax.map(attend_block, (q_blocks, jnp.arange(nQ)))
    o = jnp.moveaxis(o, 0, 1).reshape(B, T, H, 2 * Dh).astype(f32)
    o = head_rms(o) * head_norm.astype(f32) * (1.0 - lambda_init)
    return o.reshape(B, T, H * 2 * Dh).astype(h.dtype) @ w_out


def sgu_mixer(h, w_in, b_in, v_norm, w_s, b_s, w_out):
    B, T, _ = h.shape
    C = SGU_CHUNK
    nC = T // C
    uv = jax.nn.gelu(h @ w_in + b_in)
    u, v = jnp.split(uv, 2, axis=-1)
    v = rmsnorm(v, v_norm)
    vc = v.reshape(B, nC, C, SGU_GROUPS, SGU_GROUP_DIM)
    w = w_s * jnp.tril(jnp.ones((C, C), w_s.dtype))
    s = jnp.einsum('gts,bnsgd->bntgd', w, vc) + b_s.T[:, :, None]
    return (u * s.reshape(B, T, SGU_WIDTH)) @ w_out


def sq_relu_mlp(h, w1, w2):
    a = jax.nn.relu(h @ w1)
    return (a * a) @ w2


def setup_inputs(seed: int = 0) -> dict:
    key = jax.random.key(seed)
    keys = iter(jax.random.split(key, 64))

    def nrm(shape, scale):
        return scale * jax.random.normal(next(keys), shape, jnp.float32)

    def gain(n):
        return 1.0 + 0.05 * jax.random.normal(next(keys), (n,), jnp.float32)

    inp = {'x': jax.random.normal(next(keys), (BATCH, SEQ, D_MODEL), jnp.float32)}
    for i in range(DEPTH):
        p = 'l%d_' % i
        kind = MIXER_ORDER[i % N_MIXERS]
        inp[p + 'norm1'] = gain(D_MODEL)
        if kind == 'gla':
            inp[p + 'w_in'] = nrm((D_MODEL, GLA_IN), D_MODEL ** -0.5)
            inp[p + 'gate_w1'] = nrm((D_MODEL, GLA_RANK), D_MODEL ** -0.5)
            inp[p + 'gate_w2'] = nrm((GLA_RANK, GLA_DK), GLA_RANK ** -0.5)
            inp[p + 'gate_b'] = nrm((GLA_DK,), 0.1)
            inp[p + 'head_norm'] = gain(GLA_HV)
            inp[p + 'w_out'] = nrm((GLA_DV, D_MODEL), GLA_DV ** -0.5)
        elif kind == 'diff':
            inp[p + 'w_in'] = nrm((D_MODEL, DIFF_IN), D_MODEL ** -0.5)
            for nm in ('lambda_q1', 'lambda_k1', 'lambda_q2', 'lambda_k2'):
                inp[p + nm] = nrm((DIFF_HEAD_DIM,), 0.1)
            inp[p + 'head_norm'] = gain(2 * DIFF_HEAD_DIM)
            inp[p + 'w_out'] = nrm((2 * DIFF_HEADS * DIFF_HEAD_DIM, D_MODEL), D_MODEL ** -0.5)
        else:
            inp[p + 'w_in'] = nrm((D_MODEL, 2 * SGU_WIDTH), D_MODEL ** -0.5)
            inp[p + 'b_in'] = nrm((2 * SGU_WIDTH,), 0.02)
            inp[p + 'v_norm'] = gain(SGU_WIDTH)
            inp[p + 'w_s'] = nrm((SGU_GROUPS, SGU_CHUNK, SGU_CHUNK), 0.5 * SGU_CHUNK ** -0.5)
            inp[p + 'b_s'] = 1.0 + nrm((SGU_GROUPS, SGU_CHUNK), 0.05)
            inp[p + 'w_out'] = nrm((SGU_WIDTH, D_MODEL), SGU_WIDTH ** -0.5)
        inp[p + 'norm2'] = gain(D_MODEL)
        inp[p + 'mlp_w1'] = nrm((D_MODEL, D_FF), D_MODEL ** -0.5)
        inp[p + 'mlp_w2'] = nrm((D_FF, D_MODEL), D_FF ** -0.5)
    inp['final_norm'] = gain(D_MODEL)
    return inp


def reference(x,
              l0_norm1, l0_w_in, l0_gate_w1, l0_gate_w2, l0_gate_b, l0_head_norm, l0_w_out,
              l0_norm2, l0_mlp_w1, l0_mlp_w2,
              l1_norm1, l1_w_in, l1_lambda_q1, l1_lambda_k1, l1_lambda_q2, l1_lambda_k2,
              l1_head_norm, l1_w_out, l1_norm2, l1_mlp_w1, l1_mlp_w2,
              l2_norm1, l2_w_in, l2_b_in, l2_v_norm, l2_w_s, l2_b_s, l2_w_out,
              l2_norm2, l2_mlp_w1, l2_mlp_w2,
              l3_norm1, l3_w_in, l3_gate_w1, l3_gate_w2, l3_gate_b, l3_head_norm, l3_w_out,
              l3_norm2, l3_mlp_w1, l3_mlp_w2,
              final_norm):
    layers = (
        (l0_norm1, lambda h: gla_mixer(h, l0_w_in, l0_gate_w1, l0_gate_w2, l0_gate_b, l0_head_norm, l0_w_out),
         l0_norm2, l0_mlp_w1, l0_mlp_w2),
        (l1_norm1, lambda h: diff_mixer(h, l1_w_in, l1_lambda_q1, l1_lambda_k1, l1_lambda_q2, l1_lambda_k2,
                                        l1_head_norm, l1_w_out, 1),
         l1_norm2, l1_mlp_w1, l1_mlp_w2),
        (l2_norm1, lambda h: sgu_mixer(h, l2_w_in, l2_b_in, l2_v_norm, l2_w_s, l2_b_s, l2_w_out),
         l2_norm2, l2_mlp_w1, l2_mlp_w2),
        (l3_norm1, lambda h: gla_mixer(h, l3_w_in, l3_gate_w1, l3_gate_w2, l3_gate_b, l3_head_norm, l3_w_out),
         l3_norm2, l3_mlp_w1, l3_mlp_w2),
    )
    for i in range(DEPTH):
        norm1, mixer, norm2, w1, w2 = layers[i]
        x = x + mixer(rmsnorm(x, norm1))
        x = x + sq_relu_mlp(rmsnorm(x, norm2), w1, w2)
    return rmsnorm(x, final_norm)
```

```python
import contextlib
import numpy as np
import ml_dtypes
import concourse.bass as bass
import concourse.mybir as mybir
from concourse.bass_utils import run_bass_kernel_spmd

F32 = mybir.dt.float32
BF16 = mybir.dt.bfloat16
AF = mybir.ActivationFunctionType
ALU = mybir.AluOpType
AX = mybir.AxisListType

NCORES = 8
D = 1024
KC = 8
DFF = 4096
EPS = 1e-6
T = 8192
TL = T // 4
NTT = TL // 512
GROUPS = [[0, 1, 2, 3], [4, 5, 6, 7]]


NL = 4
DEBUG_OUT = False


def configure(t, nl=4):
    global T, TL, NTT, NL
    NL = nl
    T = t
    TL = T // 4
    NTT = TL // 512


class Res:
    __slots__ = ("name", "w", "r")

    def __init__(self, name):
        self.name = name
        self.w = None
        self.r = []


class Prog:
    COMPUTE = ("act", "dve", "pool", "pe")

    def __init__(self, nc):
        self.nc = nc
        self.stack = contextlib.ExitStack()
        self.streams = {k: [] for k in ("sync", "act", "dve", "pool", "pe")}
        self.cnt = {k: 0 for k in self.COMPUTE}
        self.esem = {k: nc.alloc_semaphore(name="s_" + k) for k in self.COMPUTE}
        self.dsem = {}
        self.dcnt = {}
        self.seen = {k: {} for k in self.streams}
        self.nbuf = 0
        self.q4 = {}
        self.batch = {}
        self.cst = self.sb([128, 4], F32, "cst")
        self.cst_r = Res("cst")
        def init(e):
            e.memset(self.cst[:, 0:1], 0.0)
            e.memset(self.cst[:, 1:2], float(EPS))
            return e.memset(self.cst[:, 2:3], 1.0)
        self.op("pool", init, (), (self.cst_r,))
        self.zero = self.cst[:, 0:1]
        self.epsc = self.cst[:, 1:2]
        self.onec = self.cst[:, 2:3]

    def sb(self, shape, dtype, name=None):
        self.nbuf += 1
        return self.stack.enter_context(self.nc.sbuf_tensor("S_" + (name or f"sb{self.nbuf}"), list(shape), dtype))

    ARENA_BYTES = 136 * 1024

    def phase(self, keep=0):
        if not hasattr(self, "arena_t"):
            self.arena_t = self.sb([128, self.ARENA_BYTES // 4], F32, "arena")
            self.arena_bf = self.arena_t.bitcast(BF16)
        self.barrier()
        self.arena_off = keep
        self.batch = {}

    def tmp(self, shape, dtype):
        esz = 2 if dtype == BF16 else 4
        n = int(np.prod(shape[1:]))
        nbytes = (n * esz + 31) // 32 * 32
        assert self.arena_off + nbytes <= self.ARENA_BYTES, ("arena overflow", self.arena_off, nbytes)
        base = self.arena_bf if dtype != F32 else self.arena_t
        o = self.arena_off // esz
        ap = base[0:shape[0], o:o + n]
        self.arena_off += nbytes
        if len(shape) == 3:
            ap = ap.rearrange("p (a b) -> p a b", a=shape[1])
        return ap

    def barrier(self):
        toks = [("e", k, self.cnt[k]) for k in self.COMPUTE if self.cnt[k] > 0]
        toks += [("d", k, v) for k, v in self.dcnt.items() if v > 0 and not k.startswith("ag")]
        for stream in self.streams:
            waits = []
            for kind, key, val in toks:
                if self.seen[stream].get((kind, key), 0) >= val:
                    continue
                self.seen[stream][(kind, key)] = val
                waits.append(self._sem_of((kind, key, val)))
            self.streams[stream].append((waits, None, None, 0))

    def ps(self, shape, dtype, name=None):
        self.nbuf += 1
        return self.stack.enter_context(self.nc.psum_tensor("P_" + (name or f"ps{self.nbuf}"), list(shape), dtype))

    def _sem_of(self, tok):
        kind, key, val = tok
        return (self.esem[key] if kind == "e" else self.dsem[key]), val

    def _waits(self, stream, reads, writes):
        need = {}
        def add(tok, raw):
            if tok is None:
                return
            kind, key, val = tok
            if kind == "e" and key == stream and not raw:
                return
            k = (kind, key)
            if val > need.get(k, 0):
                need[k] = val
        for r in reads:
            add(r.w, True)
        for w in writes:
            add(w.w, False)
            for t in w.r:
                add(t, False)
        out = []
        for (kind, key), val in need.items():
            if self.seen[stream].get((kind, key), 0) >= val:
                continue
            self.seen[stream][(kind, key)] = val
            out.append(self._sem_of((kind, key, val)))
        return out

    def _commit(self, tok, reads, writes):
        for r in reads:
            r.r.append(tok)
        for w in writes:
            w.w = tok
            w.r = []

    def op(self, eng, fn, reads=(), writes=()):
        waits = self._waits(eng, reads, writes)
        self.cnt[eng] += 1
        tok = ("e", eng, self.cnt[eng])
        self.streams[eng].append((waits, fn, self.esem[eng], 1))
        self._commit(tok, reads, writes)
        return tok

    def call(self, eng, meth, *args, reads=(), writes=(), **kw):
        return self.op(eng, lambda e: getattr(e, meth)(*args, **kw), reads, writes)

    def group(self, eng, calls, reads=(), writes=()):
        calls = list(calls)
        def run(e):
            ins = None
            for meth, args, kw in calls:
                ins = getattr(e, meth)(*args, **kw)
            return ins
        return self.op(eng, run, reads, writes)

    def act(self, out, in_, func, reads, writes, bias=None, scale=1.0, accum=None, extra_reads=()):
        b = self.zero if bias is None else bias
        if b.shape[0] != out.shape[0]:
            b = b[0:out.shape[0], :]
        kw = {} if accum is None else {"accum_out": accum}
        return self.op("act", lambda e: e.activation(out, in_, func, bias=b, scale=scale, **kw),
                       tuple(reads) + (self.cst_r,) + tuple(extra_reads), writes)

    def dma(self, queue, out, in_, reads=(), writes=(), key=None, batch=False):
        key = key or ("dma_" + writes[0].name)
        if key not in self.dsem:
            self.dsem[key] = self.nc.alloc_semaphore(name="d_" + key)
            self.dcnt[key] = 0
        waits = self._waits(queue, reads, writes)
        self.dcnt[key] += 16
        tok = ["d", key, self.dcnt[key]]
        if batch:
            for t in self.batch.setdefault(key, []):
                t[2] = self.dcnt[key]
            self.batch[key].append(tok)
        self.streams[queue].append((waits, lambda e: e.dma_start(out=out, in_=in_), self.dsem[key], 16))
        self._commit(tok, reads, writes)
        return tok

    def dma_fn(self, queue, fn, reads=(), writes=(), key=None):
        key = key or ("dma_" + writes[0].name)
        if key not in self.dsem:
            self.dsem[key] = self.nc.alloc_semaphore(name="d_" + key)
            self.dcnt[key] = 0
        waits = self._waits(queue, reads, writes)
        self.dcnt[key] += 16
        tok = ("d", key, self.dcnt[key])
        def run(e):
            if queue not in self.q4:
                self.q4[queue] = e.partition_id() % 4
            o, i = fn(e, self.q4[queue])
            return e.dma_start(out=o, in_=i)
        self.streams[queue].append((waits, run, self.dsem[key], 16))
        self._commit(tok, reads, writes)
        return tok

    def collective(self, kind, src, dst, reads, writes, key):
        assert key not in self.dsem
        self.dsem[key] = self.nc.alloc_semaphore(name="c_" + key)
        self.dcnt[key] = 1
        waits = self._waits("pool", reads, writes)
        tok = ["d", key, 1]
        self.streams["pool"].append((waits, lambda e: e.collective_compute(
            kind, ALU.bypass, replica_groups=GROUPS, ins=[src], outs=[dst]), self.dsem[key], 1))
        self._commit(tok, reads, writes)
        return tok

    def wait_all(self, stream, ress):
        waits = self._waits(stream, ress, ())
        self.streams[stream].append((waits, None, None, 0))

    def emit(self):
        nc = self.nc

        def replay(name):
            def run(e):
                for waits, fn, sem, inc in self.streams[name]:
                    for s, v in waits:
                        e.wait_ge(s, v)
                    if fn is not None:
                        ins = fn(e)
                        ins.then_inc(sem, inc)
            return run

        with nc.Block() as block:
            block.sync(replay("sync"))
            block.scalar(replay("act"))
            block.vector(replay("dve"))
            block.gpsimd(replay("pool"))
            block.tensor(replay("pe"))
        self.stack.close()


class Ctx:
    def __init__(self, P, nc):
        self.P = P
        self.nc = nc
        self.xT = P.sb([128, KC, TL], F32, "xT")
        self.xT_r = [[Res(f"xT_{k}_{t}") for t in range(NTT)] for k in range(KC)]
        self.ones = P.sb([128, 128], BF16, "ones")
        self.ones_r = Res("ones")
        P.call("pool", "memset", self.ones[:], 1.0, writes=(self.ones_r,))
        self.psb = [P.ps([128, 512], F32, f"bank{i}") for i in range(8)]
        self.psb_r = [Res(f"bank{i}") for i in range(8)]
        self.sqi = 0
        self.rsi = 0

    def trunk_base(self):
        P = self.P
        self.bufA = P.tmp([128, KC, TL], BF16)
        self.bufA_r = [[Res(f"bufA_{k}_{t}") for t in range(NTT)] for k in range(KC)]
        self.wbuf = [P.tmp([128, KC, 512], BF16) for i in range(4)]
        self.wbuf_r = [Res(f"wbuf{i}") for i in range(4)]
        self.sq = [P.tmp([128, 512], BF16) for i in range(3)]
        self.sq_r = [Res(f"sq{i}") for i in range(3)]
        self.rstd = [P.tmp([128, 512], F32) for i in range(2)]
        self.rstd_r = [Res(f"rstd{i}") for i in range(2)]
        self.gv = [P.tmp([128, KC], F32) for i in range(3)]
        self.gvi = 0
        return P.arena_off

    def load_vec(self, dram_ap, name):
        P = self.P
        t = self.gv[self.gvi]
        self.gvi += 1
        r = Res(name)
        P.dma("sync", t, dram_ap, (), (r,))
        return t, r

    def rmsnorm(self, g, g_r, dst, dst_r, bank, tts=None, tcol0=None):
        P = self.P
        for tt in (range(NTT) if tts is None else tts):
            ts = slice(tt * 512, (tt + 1) * 512)
            ds_ = ts if tcol0 is None else slice(0, 512)
            pb, pbr = self.psb[bank], self.psb_r[bank]
            for kc in range(KC):
                i = self.sqi = (self.sqi + 1) % 3
                sq, sqr = self.sq[i], self.sq_r[i]
                P.act(sq, self.xT[:, kc, ts], AF.Square, (self.xT_r[kc][tt],), (sqr,))
                P.call("pe", "matmul", pb[:], self.ones[:], sq, start=(kc == 0), stop=(kc == KC - 1),
                       reads=(sqr, self.ones_r), writes=(pbr,))
            j = self.rsi = (self.rsi + 1) % 2
            rs, rsr = self.rstd[j], self.rstd_r[j]
            P.act(rs, pb[:], AF.Sqrt, (pbr,), (rsr,), bias=P.epsc, scale=1.0 / D)
            P.call("dve", "reciprocal", rs, rs, reads=(rsr,), writes=(rsr,))
            for kc in range(KC):
                P.call("dve", "scalar_tensor_tensor", dst[:, kc, ds_], self.xT[:, kc, ts], g[:, kc:kc + 1], rs,
                       ALU.mult, ALU.mult, reads=(self.xT_r[kc][tt], rsr, g_r), writes=(dst_r[kc][tt],))


def emit_load_x(P, cx, x, ident):
    P.phase()
    identf = P.tmp([128, 128], F32)
    identf_r = Res("identf")
    P.dma("sync", identf, ident, (), (identf_r,))
    xin = [P.tmp([128, D], F32) for i in range(4)]
    xin_r = [Res(f"xin{i}") for i in range(4)]
    xv = x.rearrange("(n p) d -> n p d", p=128)
    bi = 0
    for tt in range(NTT):
        for tb in range(4):
            P.dma("sync", xin[tb], xv[tt * 4 + tb], (), (xin_r[tb],))
        for kc in range(KC):
            b = bi = (bi + 1) % 4
            pb, pbr = cx.psb[b], cx.psb_r[b]
            P.group("pe", [("transpose", (pb[:, tb * 128:(tb + 1) * 128], xin[tb][:, kc * 128:(kc + 1) * 128], identf), {})
                           for tb in range(4)], tuple(xin_r) + (identf_r,), (pbr,))
            if kc % 2 == 0:
                P.act(cx.xT[:, kc, tt * 512:(tt + 1) * 512], pb[:], AF.Identity, (pbr,), (cx.xT_r[kc][tt],))
            else:
                P.call("dve", "tensor_copy", cx.xT[:, kc, tt * 512:(tt + 1) * 512], pb[:], reads=(pbr,), writes=(cx.xT_r[kc][tt],))


def emit_norm_store(P, cx, g_dram, h_loc, h_loc_r, name, h_all=None, h_all_r=None):
    g, g_r = cx.load_vec(g_dram, name)
    for tt in range(NTT):
        cx.rmsnorm(g, g_r, cx.bufA, cx.bufA_r, bank=7, tts=[tt])
        dv = h_loc[tt].rearrange("(kc p) t -> p kc t", p=128)
        P.dma("sync", dv, cx.bufA[:, :, tt * 512:(tt + 1) * 512], tuple(cx.bufA_r[kc][tt] for kc in range(KC)), (h_loc_r[tt],),
              key=f"st_h_{tt}")
        if h_all is not None:
            P.collective("AllGather", h_loc[tt], h_all[tt], (h_loc_r[tt],), (h_all_r[tt],), f"agh{name}_{tt}")


def emit_wout_mlp(P, cx, keep, w_out, g2, w1, w2, name):
    P.phase(keep)
    wo = P.tmp([128, KC, D], BF16)
    wo_r = Res("wo")
    P.dma("pool", wo, w_out.rearrange("(kc p) n -> p kc n", p=128), (), (wo_r,))
    bi = 0
    for tt in range(NTT):
        ts = slice(tt * 512, (tt + 1) * 512)
        for n in range(KC):
            b = bi = (bi + 1) % 4
            pb, pbr = cx.psb[b], cx.psb_r[b]
            P.group("pe", [("matmul", (pb[:], wo[:, fc, n * 128:(n + 1) * 128], cx.bufA[:, fc, ts]),
                            dict(start=(fc == 0), stop=(fc == KC - 1))) for fc in range(KC)],
                    (wo_r,) + tuple(cx.bufA_r[fc][tt] for fc in range(KC)), (pbr,))
            P.call("dve", "tensor_tensor", cx.xT[:, n, ts], cx.xT[:, n, ts], pb[:], ALU.add,
                   reads=(pbr, cx.xT_r[n][tt]), writes=(cx.xT_r[n][tt],))
    g2t, g2r = cx.load_vec(g2, "g2" + name)
    cx.rmsnorm(g2t, g2r, cx.bufA, cx.bufA_r, bank=7)
    P.phase(keep)
    FG = 512
    NFG = DFF // FG
    FC = FG // 128
    w1b = [cx.wbuf[0], cx.wbuf[1]]
    w1b_r = [cx.wbuf_r[0], cx.wbuf_r[1]]
    w2b = [cx.wbuf[2 + i].rearrange("p a b -> p (a b)").rearrange("p (c n) -> p c n", c=FC) for i in range(2)]
    w2b_r = [cx.wbuf_r[2], cx.wbuf_r[3]]
    a2 = [P.tmp([128, FC, 512], BF16) for i in range(2)]
    a2_r = [[Res(f"a2_{i}_{c}") for c in range(FC)] for i in range(2)]
    rl = [P.tmp([128, 512], F32) for i in range(3)]
    rl_r = [Res(f"rl{i}") for i in range(3)]
    w1v = w1.rearrange("(kc p) f -> p kc f", p=128)
    w2v = w2.rearrange("(fc p) n -> p fc n", p=128)

    def load_w(fg):
        s = fg % 2
        P.dma("pool", w1b[s], w1v[:, :, fg * FG:(fg + 1) * FG], (), (w1b_r[s],))
        P.dma("pool", w2b[s], w2v[:, fg * FC:(fg + 1) * FC, :], (), (w2b_r[s],))

    load_w(0)
    steps = [(fg, tt) for fg in range(NFG) for tt in range(NTT)]
    st = {"abank": 0, "ybank": 0}

    def stage1(i):
        fg, tt = steps[i]
        s = fg % 2
        ai = i % 2
        ts = slice(tt * 512, (tt + 1) * 512)
        for c in range(FC):
            st["abank"] = (st["abank"] + 1) % 4
            pb, pbr = cx.psb[st["abank"]], cx.psb_r[st["abank"]]
            P.group("pe", [("matmul", (pb[:], w1b[s][:, kc, c * 128:(c + 1) * 128], cx.bufA[:, kc, ts]),
                            dict(start=(kc == 0), stop=(kc == KC - 1))) for kc in range(KC)],
                    (w1b_r[s],) + tuple(cx.bufA_r[kc][tt] for kc in range(KC)), (pbr,))
            ri = (ai * FC + c) % 3
            P.act(rl[ri], pb[:], AF.Relu, (pbr,), (rl_r[ri],))
            P.call("pool", "tensor_tensor", a2[ai][:, c, :], rl[ri], rl[ri], ALU.mult, reads=(rl_r[ri],), writes=(a2_r[ai][c],))

    def stage2(i):
        fg, tt = steps[i]
        s = fg % 2
        ai = i % 2
        ts = slice(tt * 512, (tt + 1) * 512)
        for n in range(KC):
            st["ybank"] = 4 + (st["ybank"] + 1) % 4
            pb, pbr = cx.psb[st["ybank"]], cx.psb_r[st["ybank"]]
            P.group("pe", [("matmul", (pb[:], w2b[s][:, c, n * 128:(n + 1) * 128], a2[ai][:, c, :]),
                            dict(start=(c == 0), stop=(c == FC - 1))) for c in range(FC)],
                    (w2b_r[s],) + tuple(a2_r[ai]), (pbr,))
            P.call("dve", "tensor_tensor", cx.xT[:, n, ts], cx.xT[:, n, ts], pb[:], ALU.add,
                   reads=(pbr, cx.xT_r[n][tt]), writes=(cx.xT_r[n][tt],))

    stage1(0)
    for i in range(len(steps)):
        fg, tt = steps[i]
        if tt == 0 and fg + 1 < NFG:
            load_w(fg + 1)
        if i + 1 < len(steps):
            stage1(i + 1)
        stage2(i)


def emit_final(P, cx, keep, gf, y_o, ident):
    P.phase(keep)
    g3t, g3r = cx.load_vec(gf, "gf")
    identf = P.tmp([128, 128], F32)
    identf_r = Res("identf")
    P.dma("sync", identf, ident, (), (identf_r,))
    yT = P.tmp([128, KC, 512], F32)
    yT_r = [[Res(f"yT_{k}")] * NTT for k in range(KC)]
    yo = [P.tmp([128, D], F32) for i in range(2)]
    yo_r = [Res(f"yo{i}") for i in range(2)]
    yv = y_o.rearrange("(n p) d -> n p d", p=128)
    outr = Res("st_y")
    oi = 0
    bi = 0
    for tt in range(NTT):
        cx.rmsnorm(g3t, g3r, yT, yT_r, bank=7, tts=[tt], tcol0=0)
        for tb in range(4):
            oi ^= 1
            for half in range(2):
                b = bi = (bi + 1) % 4
                pb, pbr = cx.psb[b], cx.psb_r[b]
                P.group("pe", [("transpose", (pb[:, q * 128:(q + 1) * 128], yT[:, half * 4 + q, tb * 128:(tb + 1) * 128], identf), {})
                               for q in range(4)], tuple(yT_r[k][0] for k in range(KC)) + (identf_r,), (pbr,))
                if half == 0:
                    P.act(yo[oi][:, 0:512], pb[:], AF.Identity, (pbr,), (yo_r[oi],))
                else:
                    P.call("dve", "tensor_copy", yo[oi][:, 512:1024], pb[:], reads=(pbr,), writes=(yo_r[oi],))
            P.dma("sync", yv[tt * 4 + tb], yo[oi], (yo_r[oi],), (outr,), key="st_y")
    P.wait_all("sync", (outr,))


GLA_HK = 128
GLA_HV = 256


def emit_gla(P, cx, h_all, h_all_r, o_loc, o_loc_r, o_all, o_all_r, W, name):
    P.phase()

    def wload(ap, ncol):
        t = P.tmp([128, KC, ncol], BF16)
        r = Res("w")
        P.dma("pool", t, ap.rearrange("(kc p) n -> p kc n", p=128), (), (r,), key="glaw", batch=True)
        return t, r

    wq_s, wq_r = wload(W["wq"], 128)
    wk_s, wk_r = wload(W["wk"], 128)
    wv_s, wv_r = wload(W["wv"], 256)
    wg_s, wg_r = wload(W["wg"], 256)
    gw1_s, gw1_r = wload(W["gw1"], 16)
    gw2_s = P.tmp([17, 128], BF16); gw2_r = Res("gw2")
    P.dma("pool", gw2_s, W["gw2b"], (), (gw2_r,), key="glaw", batch=True)
    idb = P.tmp([128, 128], BF16); idb_r = Res("idb")
    P.dma("pool", idb, W["ident"], (), (idb_r,), key="glaw", batch=True)
    hn_s = P.tmp([128, 256], F32); hn_r = Res("hn")
    P.dma("sync", hn_s, W["hnb"], (), (hn_r,), key="glac", batch=True)
    ti_s = P.tmp([64, 64], F32); ti_r = Res("ti")
    P.dma("sync", ti_s, W["tinc"], (), (ti_r,), key="glac", batch=True)
    tu_s = P.tmp([64, 64], F32); tu_r = Res("tu")
    P.dma("sync", tu_s, W["tupp"], (), (tu_r,), key="glac", batch=True)
    ti8 = P.tmp([64, 8, 64], F32); ti8_r = Res("ti8")
    for j in range(8):
        P.dma("sync", ti8[:, j, :], W["tinc"], (), (ti8_r,), key="glac", batch=True)

    hb = [P.tmp([128, KC, 512], BF16) for i in range(2)]
    hb_r = [Res(f"hb{i}") for i in range(2)]
    hvs = [h_all[j].rearrange("(r kc p) t -> r p kc t", r=4, p=128) for j in range(NTT)]

    def h_load(dst, dst_r, tile):
        P.dma("sync", dst, hvs[tile % NTT][tile // NTT], (h_all_r[tile % NTT],), (dst_r,))

    S = P.tmp([128, 256], F32); S_r = Res("S")
    Sb = P.tmp([128, 256], BF16); Sb_r = Res("Sb")
    P.call("pool", "memset", S, 0.0, writes=(S_r,))
    P.call("pool", "memset", Sb, 0.0, writes=(Sb_r,))
    r17 = P.tmp([17, 512], BF16); r17_r = Res("r17")
    P.call("pool", "memset", r17, 1.0, writes=(r17_r,))

    bk, br = cx.psb, cx.psb_r
    bkT = bk[5].bitcast(BF16)[:, :].rearrange("p (a b) -> p a b", a=2)

    ez = P.tmp([64, 8, 128], F32); ez_r = Res("ez")
    lsp = P.tmp([64, 8, 128], F32); lsp_r = Res("lsp")
    E1 = P.tmp([128, 512], F32); E1_r = Res("E1")
    Ei = P.tmp([128, 512], F32); Ei_r = Res("Ei")
    eU = P.tmp([64, 8, 128], F32); eU_r = Res("eU")
    qd = P.tmp([128, 512], BF16); qd_r = Res("qd")
    ki = P.tmp([128, 512], BF16); ki_r = Res("ki")
    ke = P.tmp([64, 8, 128], BF16); ke_r = Res("ke")
    kf = [P.tmp([64, 8, 128], F32) for i in range(2)]; kf_r = [[Res(f"kf{i}_{j}") for j in range(8)] for i in range(2)]
    vb = [P.tmp([64, 8, 256], BF16) for i in range(2)]; vb_r = [[Res(f"vb{i}_{j}") for j in range(8)] for i in range(2)]
    sg = [P.tmp([64, 8, 256], F32) for i in range(2)]; sg_r = [[Res(f"sg{i}_{j}") for j in range(8)] for i in range(2)]
    at = P.tmp([64, 8, 64], BF16); at_r = Res("at")
    osb_ = P.tmp([64, 8, 256], F32); osb_r = [Res(f"os{i}") for i in range(8)]
    sq = [P.tmp([64, 256], F32) for i in range(2)]; sq_r = [Res(f"sq{i}") for i in range(2)]
    ss = P.tmp([64, 8], F32); ss_r = Res("ss")
    rs = P.tmp([64, 8], F32); rs_r = Res("rs")
    tt_ = [P.tmp([64, 256], F32) for i in range(2)]; tt_r = [Res(f"tt{i}") for i in range(2)]
    of = [P.tmp([64, 256], BF16) for i in range(2)]; of_r = [Res(f"of{i}") for i in range(2)]
    ob = [P.tmp([128, 2, 512], BF16) for i in range(2)]; ob_r = [Res(f"ob{i}") for i in range(2)]
    ovs = [o_loc[j].rearrange("(h p) t -> p h t", p=128) for j in range(4)]
    nq = TL // 512

    def mm8(dst, lhs_fn, rhs_fn, reads, dst_r):
        P.group("pe", [("matmul", (dst, lhs_fn(kc), rhs_fn(kc)), dict(start=(kc == 0), stop=(kc == KC - 1))) for kc in range(KC)],
                reads, (dst_r,))

    def proj_chunk(tile, j):
        p = tile % 2
        h, h_r = hb[p], hb_r[p]
        cs = slice(j * 64, (j + 1) * 64)
        mm8(bk[6][0:64, 0:128], lambda kc: h[:, kc, cs], lambda kc: wk_s[:, kc, :], (wk_r, h_r), br[6])
        mm8(bk[6][0:64, 128:384], lambda kc: h[:, kc, cs], lambda kc: wv_s[:, kc, :], (wv_r, h_r), br[6])
        P.act(kf[p][:, j, :], bk[6][0:64, 0:128], AF.Identity, (br[6],), (kf_r[p][j],))
        P.act(vb[p][:, j, :], bk[6][0:64, 128:384], AF.Identity, (br[6],), (vb_r[p][j],))
        mm8(bk[7][0:64, 0:256], lambda kc: h[:, kc, cs], lambda kc: wg_s[:, kc, :], (wg_r, h_r), br[7])
        P.act(sg[p][:, j, :], bk[7][0:64, 0:256], AF.Silu, (br[7],), (sg_r[p][j],))

    h_load(hb[0], hb_r[0], 0)
    for j in range(8):
        proj_chunk(0, j)
    for tile in range(T // 512):
        if tile + 1 < T // 512:
            h_load(hb[(tile + 1) % 2], hb_r[(tile + 1) % 2], tile + 1)
        h, h_r = hb[tile % 2], hb_r[tile % 2]
        mm8(bk[0][:], lambda kc: wq_s[:, kc, :], lambda kc: h[:, kc, :], (wq_r, h_r), br[0])
        mm8(bk[1][:], lambda kc: wk_s[:, kc, :], lambda kc: h[:, kc, :], (wk_r, h_r), br[1])
        mm8(bk[2][0:16, :], lambda kc: gw1_s[:, kc, :], lambda kc: h[:, kc, :], (gw1_r, h_r), br[2])
        P.act(r17[0:16, :], bk[2][0:16, :], AF.Identity, (br[2],), (r17_r,))
        for half in range(2):
            b = 3 + half
            P.group("pe", [("matmul", (bk[b][0:64, c * 128:(c + 1) * 128], r17[0:17, (half * 4 + c) * 64:(half * 4 + c + 1) * 64], gw2_s[0:17, :]),
                            dict(start=True, stop=True)) for c in range(4)], (r17_r, gw2_r), (br[b],))
            P.act(ez[:, half * 4:half * 4 + 4, :], bk[b][0:64, :].rearrange("p (a b) -> p a b", a=4), AF.Exp, (br[b],), (ez_r,), scale=-1.0)
        P.act(lsp[:, 0:4, :], ez[:, 0:4, :], AF.Ln, (ez_r,), (lsp_r,), bias=P.onec)
        P.act(lsp[:, 4:8, :], ez[:, 4:8, :], AF.Ln, (ez_r,), (lsp_r,), bias=P.onec)
        P.group("pe", [("matmul", (bk[5][:, c * 64:(c + 1) * 64], lsp[:, c, :], ti_s), dict(start=True, stop=True)) for c in range(8)],
                (lsp_r, ti_r), (br[5],))
        for half in range(2):
            b = 6 + half
            P.group("pe", [("matmul", (bk[b][0:64, c * 128:(c + 1) * 128], tu_s, lsp[:, half * 4 + c, :]), dict(start=True, stop=True))
                           for c in range(4)], (lsp_r, tu_r), (br[b],))
        P.act(E1, bk[5][:], AF.Exp, (br[5],), (E1_r,), scale=-1.0 / 16)
        P.act(Ei, bk[5][:], AF.Exp, (br[5],), (Ei_r,), scale=1.0 / 16)
        for half in range(2):
            b = 6 + half
            P.act(eU[:, half * 4:half * 4 + 4, :], bk[b][0:64, :].rearrange("p (a b) -> p a b", a=4), AF.Exp, (br[b],), (eU_r,), scale=-1.0 / 16)
        P.call("dve", "scalar_tensor_tensor", qd, bk[0][:], float(GLA_HK ** -0.5), E1, ALU.mult, ALU.mult,
               reads=(br[0], E1_r), writes=(qd_r,))
        P.call("dve", "tensor_tensor", ki, bk[1][:], Ei, ALU.mult, reads=(br[1], Ei_r), writes=(ki_r,))
        pp = tile % 2
        P.call("dve", "tensor_tensor", ke, kf[pp], eU, ALU.mult, reads=tuple(kf_r[pp]) + (eU_r,), writes=(ke_r,))
        P.group("pe", [("matmul", (bk[0][0:64, c * 64:(c + 1) * 64], ki[:, c * 64:(c + 1) * 64], qd[:, c * 64:(c + 1) * 64]),
                        dict(start=True, stop=True)) for c in range(8)], (ki_r, qd_r), (br[0],))
        P.call("dve", "tensor_tensor", at, bk[0][0:64, :].rearrange("p (a b) -> p a b", a=8), ti8, ALU.mult,
               reads=(br[0], ti8_r), writes=(at_r,))
        for j in range(8):
            cs = slice(j * 64, (j + 1) * 64)
            bo, bkv = 1 + j % 2, 3 + j % 2
            if tile + 1 < T // 512:
                proj_chunk(tile + 1, j)
            P.call("pe", "matmul", bk[bkv][:, 0:256], ke[:, j, :], vb[pp][:, j, :], start=True, stop=True,
                   reads=(ke_r, vb_r[pp][j]), writes=(br[bkv],))
            P.group("pe", [("matmul", (bk[bo][0:64, 0:256], qd[:, cs], Sb), dict(start=True, stop=False)),
                           ("matmul", (bk[bo][0:64, 0:256], at[:, j, :], vb[pp][:, j, :]), dict(start=False, stop=True))],
                    (qd_r, Sb_r, at_r, vb_r[pp][j]), (br[bo],))
            P.call("dve", "scalar_tensor_tensor", S, S, E1[:, j * 64 + 63:j * 64 + 64], bk[bkv][:, 0:256], ALU.mult, ALU.add,
                   reads=(S_r, E1_r, br[bkv]), writes=(S_r,))
            P.act(Sb, S, AF.Identity, (S_r,), (Sb_r,))
            P.call("dve", "tensor_copy", osb_[:, j, :], bk[bo][0:64, 0:256], reads=(br[bo],), writes=(osb_r[j],))
            P.act(sq[j % 2], osb_[:, j, :], AF.Square, (osb_r[j],), (sq_r[j % 2],))
            P.call("dve", "reduce_sum", ss[:, j:j + 1], sq[j % 2], AX.X, reads=(sq_r[j % 2],), writes=(ss_r,))
        P.act(rs, ss, AF.Sqrt, (ss_r,), (rs_r,), bias=P.epsc, scale=1.0 / GLA_HV)
        P.call("dve", "reciprocal", rs, rs, reads=(rs_r,), writes=(rs_r,))
        for j in range(8):
            cs = slice(j * 64, (j + 1) * 64)
            i = j % 2
            P.call("dve", "scalar_tensor_tensor", tt_[i], osb_[:, j, :], rs[:, j:j + 1], hn_s[0:64, :], ALU.mult, ALU.mult,
                   reads=(osb_r[j], rs_r, hn_r), writes=(tt_r[i],))
            P.call("dve", "tensor_tensor", of[i], tt_[i], sg[pp][:, j, :], ALU.mult, reads=(tt_r[i], sg_r[pp][j]), writes=(of_r[i],))
            P.group("pe", [("transpose", (bkT[:, 0, cs], of[i][:, 0:128], idb[0:64, 0:64]), {}),
                           ("transpose", (bkT[:, 1, cs], of[i][:, 128:256], idb[0:64, 0:64]), {})],
                    (of_r[i], idb_r), (br[5],))
        o_, o_r = ob[tile % 2], ob_r[tile % 2]
        P.act(o_[:, 0, :], bkT[:, 0, :], AF.Identity, (br[5],), (o_r,))
        P.call("dve", "tensor_copy", o_[:, 1, :], bkT[:, 1, :], reads=(br[5],), writes=(o_r,))
        P.dma("sync", ovs[tile // nq][:, :, (tile % nq) * 512:(tile % nq + 1) * 512], o_, (o_r,), (o_loc_r[tile // nq][tile % 2],),
              key=f"st_o{name}_{tile % 2}")
        if tile % nq == nq - 1:
            P.collective("AllGather", o_loc[tile // nq], o_all[tile // nq], tuple(o_loc_r[tile // nq]), (o_all_r[tile // nq],),
                         f"ago{name}_{tile // nq}")


DIFF_LAMBDA_INIT = 0.8 - 0.6 * float(np.exp(-0.3 * 1))


def emit_diff(P, cx, h_all, h_all_r, o_loc, o_loc_r, o_all, o_all_r, W, name):
    P.phase()
    NT = T // 512

    def wload(ap, ncol):
        t = P.tmp([128, KC, ncol], BF16)
        r = Res("w")
        P.dma("pool", t, ap.rearrange("(kc p) n -> p kc n", p=128), (), (r,), key="difw", batch=True)
        return t, r

    wq_s, wq_r = wload(W["wq"], 256)
    wk_s, wk_r = wload(W["wk"], 256)
    wv_s, wv_r = wload(W["wv"], 256)
    tri_s = P.tmp([128, 128], BF16); tri_r = Res("tri")
    P.dma("pool", tri_s, W["tri"], (), (tri_r,), key="difw", batch=True)
    ones, ones_r = cx.ones, cx.ones_r
    onef = P.tmp([1, 128], F32); onef_r = Res("onef")
    P.call("pool", "memset", onef, 1.0, writes=(onef_r,))
    bias_s = P.tmp([128, 2, 67], F32); bias_r = Res("bias")
    P.dma("sync", bias_s, W["biasT"].rearrange("h p n -> p h n"), (), (bias_r,), key="difc", batch=True)
    hn_s = P.tmp([128, 1], F32); hn_r = Res("hn")
    P.dma("sync", hn_s, W["hnc"], (), (hn_r,), key="difc", batch=True)
    bk, bk_r = cx.psb, cx.psb_r

    lp = P.tmp([1, 256], F32); lp_r = Res("lp")
    P.dma("sync", lp, W["lamp"], (), (lp_r,), key="difc", batch=True)
    P.call("dve", "tensor_scalar", hn_s, hn_s, float(1.0 - DIFF_LAMBDA_INIT), None, ALU.mult, reads=(hn_r,), writes=(hn_r,))
    lw = P.tmp([1, 136], F32); lw_r = Res("lw")
    P.call("dve", "tensor_tensor", lw[:, 0:64], lp[:, 0:64], lp[:, 64:128], ALU.mult, reads=(lp_r,), writes=(lw_r,))
    P.call("dve", "tensor_tensor", lw[:, 64:128], lp[:, 128:192], lp[:, 192:256], ALU.mult, reads=(lp_r, lw_r), writes=(lw_r,))
    P.call("dve", "reduce_sum", lw[:, 128:129], lw[:, 0:64], AX.X, reads=(lw_r,), writes=(lw_r,))
    P.call("dve", "reduce_sum", lw[:, 129:130], lw[:, 64:128], AX.X, reads=(lw_r,), writes=(lw_r,))
    P.act(lw[:, 130:132], lw[:, 128:130], AF.Exp, (lw_r,), (lw_r,))
    P.call("dve", "tensor_tensor", lw[:, 132:133], lw[:, 131:132], lw[:, 130:131], ALU.subtract, reads=(lw_r,), writes=(lw_r,))
    P.call("dve", "tensor_scalar", lw[:, 133:134], lw[:, 132:133], float(-DIFF_LAMBDA_INIT), None, ALU.add, reads=(lw_r,), writes=(lw_r,))
    P.call("pe", "matmul", bk[0][:, 0:1], onef[0:1, :], lw[0:1, 133:134], start=True, stop=True,
           reads=(onef_r, lw_r), writes=(bk_r[0],))
    nlam = P.tmp([128, 1], F32); nlam_r = Res("nlam")
    P.call("dve", "tensor_copy", nlam, bk[0][:, 0:1], reads=(bk_r[0],), writes=(nlam_r,))

    Qa = [P.tmp([67, T], BF16) for r in range(2)]
    Ka = [P.tmp([67, T], BF16) for r in range(2)]
    Qa_r = [[Res(f"Qa{r}_{t}") for t in range(NT)] for r in range(2)]
    Ka_r = [[Res(f"Ka{r}_{t}") for t in range(NT)] for r in range(2)]
    Qg_r = [Res(f"Qg{r}") for r in range(2)]
    Kg_r = [Res(f"Kg{r}") for r in range(2)]
    V = P.tmp([128, T // 128, 128], BF16)
    V_r = [Res(f"V{t}") for t in range(NT)]
    hb = [P.tmp([128, KC, 512], BF16) for i in range(2)]
    hb_r = [Res(f"hb{i}") for i in range(2)]
    hvs = [h_all[j].rearrange("(r kc p) t -> r p kc t", r=4, p=128) for j in range(NTT)]

    def h_load(dst, dst_r, tile):
        P.dma("sync", dst, hvs[tile % NTT][tile // NTT], (h_all_r[tile % NTT],), (dst_r,))

    Pt = [P.tmp([128, 512], BF16) for i in range(4)]
    Pt_r = [Res(f"Pt{i}") for i in range(4)]
    rec = [P.tmp([128, 512], F32) for i in range(2)]
    rec_r = [Res(f"rec{i}") for i in range(2)]
    on = [P.tmp([128, 512], F32) for i in range(2)]
    on_r = [Res(f"on{i}") for i in range(2)]
    oo = P.tmp([128, 512], F32); oo_r = Res("oo")
    sq = P.tmp([128, 512], BF16); sq_r = Res("sq")
    rs = P.tmp([128, 512], F32); rs_r = Res("rs")
    ofin = [P.tmp([128, 512], BF16) for i in range(2)]
    ofin_r = [Res(f"ofin{i}") for i in range(2)]
    hcnt = 0
    sbank = 0
    pti = 0
    for hh in range(2):
        for r in range(2):
            P.dma("pool", Qa[r][64:67, :], W["qaug"][hh], (), (Qg_r[r],), key="difa", batch=True)
            P.dma("pool", Ka[r][64:67, :], W["kaug"][hh], (), (Kg_r[r],), key="difa", batch=True)
        for tile in range(NT):
            s = hcnt % 2
            hcnt += 1
            h_load(hb[s], hb_r[s], tile)
            h, h_r = hb[s], hb_r[s]
            cols = slice(tile * 512, (tile + 1) * 512)
            for r in range(2):
                wc = slice((hh * 2 + r) * 64, (hh * 2 + r + 1) * 64)
                for (w_s, w_r, dst, dst_r, sc) in ((wq_s, wq_r, Qa, Qa_r, 0.125), (wk_s, wk_r, Ka, Ka_r, 1.0)):
                    sbank = (sbank + 1) % 4
                    pb, pbr = bk[sbank], bk_r[sbank]
                    P.group("pe", [("matmul", (pb[0:64, :], w_s[:, kc, wc], h[:, kc, :]), dict(start=(kc == 0), stop=(kc == KC - 1)))
                                   for kc in range(KC)], (w_r, h_r), (pbr,))
                    P.act(dst[r][0:64, cols], pb[0:64, :], AF.Identity, (pbr,), (dst_r[r][tile],), scale=sc)
            sbank = (sbank + 1) % 4
            pb, pbr = bk[sbank], bk_r[sbank]
            for tb in range(4):
                P.group("pe", [("matmul", (pb[:, tb * 128:(tb + 1) * 128], h[:, kc, tb * 128:(tb + 1) * 128], wv_s[:, kc, hh * 128:(hh + 1) * 128]),
                                dict(start=(kc == 0), stop=(kc == KC - 1))) for kc in range(KC)], (wv_r, h_r), (pbr,))
            P.call("dve", "tensor_copy", V[:, tile * 4:(tile + 1) * 4, :], pb[:].rearrange("p (a b) -> p a b", a=4),
                   reads=(pbr,), writes=(V_r[tile],))
        LA = 2
        for It in range(NT):
            nJ = 4 * It + 4
            pend = []

            def consume(item, It=It, nJ=nJ):
                Jt, r, c0, pt, ptr = item
                P.call("pe", "matmul", bk[4 + r][:, c0:512], V[:, Jt, :], pt[:, c0:512], start=(Jt == 0), stop=(Jt == nJ - 1),
                       reads=(V_r[Jt // 4], ptr), writes=(bk_r[4 + r],))
                P.call("pe", "matmul", bk[6 + r][:, c0:512], ones[:], pt[:, c0:512], start=(Jt == 0), stop=(Jt == nJ - 1),
                       reads=(ones_r, ptr), writes=(bk_r[6 + r],))

            for Jt in range(nJ):
                m = Jt - 4 * It
                c0 = 128 * m if m > 0 else 0
                idx = 4 * It - Jt + 3
                for r in range(2):
                    sbank = (sbank + 1) % 4
                    pb, pbr = bk[sbank], bk_r[sbank]
                    pti = (pti + 1) % 4
                    pt, ptr = Pt[pti], Pt_r[pti]
                    P.call("pe", "matmul", pb[:, c0:512], Ka[r][0:67, Jt * 128:(Jt + 1) * 128],
                           Qa[r][0:67, It * 512 + c0:(It + 1) * 512], start=True, stop=True,
                           reads=(Ka_r[r][Jt // 4], Kg_r[r], Qa_r[r][It], Qg_r[r]), writes=(pbr,))
                    P.act(pt[:, c0:512], pb[:, c0:512], AF.Exp, (pbr,), (ptr,), bias=bias_s[:, hh, idx:idx + 1], extra_reads=(bias_r,))
                    if m >= 0:
                        P.call("dve", "tensor_tensor", pt[:, c0:c0 + 128], pt[:, c0:c0 + 128], tri_s, ALU.mult,
                               reads=(ptr, tri_r), writes=(ptr,))
                    pend.append((Jt, r, c0, pt, ptr))
                    if len(pend) > LA:
                        consume(pend.pop(0))
            while pend:
                consume(pend.pop(0))
            for r in range(2):
                P.call("dve", "reciprocal", rec[r], bk[6 + r][:], reads=(bk_r[6 + r],), writes=(rec_r[r],))
                P.call("dve", "tensor_tensor", on[r], bk[4 + r][:], rec[r], ALU.mult, reads=(bk_r[4 + r], rec_r[r]), writes=(on_r[r],))
            P.call("dve", "scalar_tensor_tensor", oo, on[1], nlam[:, 0:1], on[0], ALU.mult, ALU.add,
                   reads=(on_r[0], on_r[1], nlam_r), writes=(oo_r,))
            P.act(sq, oo, AF.Square, (oo_r,), (sq_r,))
            sbank = (sbank + 1) % 4
            pb, pbr = bk[sbank], bk_r[sbank]
            P.call("pe", "matmul", pb[:], ones[:], sq, start=True, stop=True, reads=(ones_r, sq_r), writes=(pbr,))
            P.act(rs, pb[:], AF.Sqrt, (pbr,), (rs_r,), bias=P.epsc, scale=1.0 / 128)
            P.call("dve", "reciprocal", rs, rs, reads=(rs_r,), writes=(rs_r,))
            ob, ob_r = ofin[It % 2], ofin_r[It % 2]
            P.call("dve", "scalar_tensor_tensor", ob, oo, hn_s[:, 0:1], rs, ALU.mult, ALU.mult,
                   reads=(oo_r, hn_r, rs_r), writes=(ob_r,))
            nq = TL // 512
            P.dma("sync", o_loc[It // nq][hh * 128:(hh + 1) * 128, (It % nq) * 512:(It % nq + 1) * 512], ob, (ob_r,),
                  (o_loc_r[It // nq][It % 2],), key=f"st_o{name}_{It % 2}")
            if hh == 1 and It % nq == nq - 1:
                P.collective("AllGather", o_loc[It // nq], o_all[It // nq], tuple(o_loc_r[It // nq]), (o_all_r[It // nq],),
                             f"ago{name}_{It // nq}")


GELU_C = 0.044715
GELU_S = 1.5957691216057308


def gelu_tanh(P, out, x, x_r, out_r, t1, t1_r):
    P.act(t1, x, AF.Square, (x_r,), (t1_r,))
    P.call("dve", "tensor_scalar", t1, t1, GELU_C, 1.0, ALU.mult, ALU.add, reads=(t1_r,), writes=(t1_r,))
    P.call("dve", "tensor_tensor", t1, t1, x, ALU.mult, reads=(t1_r, x_r), writes=(t1_r,))
    P.act(t1, t1, AF.Sigmoid, (t1_r,), (t1_r,), scale=GELU_S)
    P.call("pool", "tensor_tensor", out, x, t1, ALU.mult, reads=(x_r, t1_r), writes=(out_r,))


def emit_sgu(P, cx, keep, h_loc, h_loc_r, W):
    P.phase(keep)
    wv_ = W["w_in"].rearrange("(kc p) n -> p kc n", p=128)
    for j in range(4):
        P.dma("pool", cx.wbuf[j], wv_[:, :, j * 512:(j + 1) * 512], (), (cx.wbuf_r[j],))
    bu_t = P.tmp([128, KC], F32); bu_r = Res("bu")
    P.dma("sync", bu_t, W["bu"], (), (bu_r,), key="sguc", batch=True)
    bv_t = P.tmp([1, D], BF16); bv_r = Res("bv")
    P.dma("pool", bv_t, W["bv"], (), (bv_r,), key="sguw", batch=True)
    bs_t = P.tmp([1, D], BF16); bs_r = Res("bs")
    P.dma("pool", bs_t, W["bs"], (), (bs_r,), key="sguw", batch=True)
    vnb_t = P.tmp([128, D], F32); vnb_r = Res("vnb")
    P.dma("sync", vnb_t, W["vnb"], (), (vnb_r,), key="sguc", batch=True)
    tri_t = P.tmp([128, 128], BF16); tri_r = Res("tri")
    P.dma("pool", tri_t, W["tri"], (), (tri_r,), key="sguw", batch=True)
    ws_t = P.tmp([128, 8, 128], BF16); ws_r = Res("ws")
    P.dma("pool", ws_t, W["wsT"].rearrange("g s t -> s g t"), (), (ws_r,), key="sguw", batch=True)
    for g in range(8):
        P.call("dve", "tensor_tensor", ws_t[:, g, :], ws_t[:, g, :], tri_t, ALU.mult, reads=(ws_r, tri_r), writes=(ws_r,))
    hb = [P.tmp([128, KC, 512], BF16) for i in range(2)]
    hb_r = [Res(f"shb{i}") for i in range(2)]
    hvl = [h_loc[tt].rearrange("(kc p) t -> p kc t", p=128) for tt in range(NTT)]
    vn = [P.tmp([128, D], BF16) for i in range(4)]
    vn_r = [Res(f"vn{i}") for i in range(4)]
    xv = [P.tmp([128, 512], F32) for i in range(2)]
    xv_r = [Res(f"xv{i}") for i in range(2)]
    t1 = [P.tmp([128, 512], F32) for i in range(2)]
    t1_r = [Res(f"t1{i}") for i in range(2)]
    gv = [P.tmp([128, D], F32) for i in range(2)]
    gv_r = [Res(f"gv{i}") for i in range(2)]
    sqv = P.tmp([128, D], F32); sqv_r = Res("sqv")
    ssv = [P.tmp([128, 2], F32) for i in range(2)]
    ssv_r = [Res(f"ssv{i}") for i in range(2)]
    ug = [P.tmp([128, 512], F32) for i in range(2)]
    ug_r = [Res(f"ug{i}") for i in range(2)]
    bank = 0
    xi = 0
    P.dma("sync", hb[0], hvl[0], (h_loc_r[0],), (hb_r[0],))
    for tt in range(NTT):
        if tt + 1 < NTT:
            P.dma("sync", hb[(tt + 1) % 2], hvl[tt + 1], (h_loc_r[tt + 1],), (hb_r[(tt + 1) % 2],))
        h, h_r = hb[tt % 2], hb_r[tt % 2]
        ts = slice(tt * 512, (tt + 1) * 512)
        for cb in range(4):
            tok = slice(cb * 128, (cb + 1) * 128)
            gi = cb % 2
            for half in range(2):
                bank = (bank + 1) % 4
                pb, pbr = cx.psb[bank], cx.psb_r[bank]
                calls = [("matmul", (pb[:], h[:, kc, tok], cx.wbuf[2 + half][:, kc, :]), dict(start=(kc == 0), stop=False))
                         for kc in range(KC)]
                calls.append(("matmul", (pb[:], cx.ones[0:1, :], bv_t[0:1, half * 512:(half + 1) * 512]), dict(start=False, stop=True)))
                P.group("pe", calls, (h_r, cx.wbuf_r[2 + half], cx.ones_r, bv_r), (pbr,))
                xi ^= 1
                P.act(xv[xi], pb[:], AF.Identity, (pbr,), (xv_r[xi],))
                gelu_tanh(P, gv[gi][:, half * 512:(half + 1) * 512], xv[xi], xv_r[xi], gv_r[gi], t1[xi], t1_r[xi])
            P.act(sqv, gv[gi], AF.Square, (gv_r[gi],), (sqv_r,))
            P.call("dve", "reduce_sum", ssv[gi][:, 0:1], sqv, AX.X, reads=(sqv_r,), writes=(ssv_r[gi],))
            P.act(ssv[gi][:, 1:2], ssv[gi][:, 0:1], AF.Sqrt, (ssv_r[gi],), (ssv_r[gi],), bias=P.epsc, scale=1.0 / D)
            P.call("dve", "reciprocal", ssv[gi][:, 0:1], ssv[gi][:, 1:2], reads=(ssv_r[gi],), writes=(ssv_r[gi],))
            P.call("dve", "scalar_tensor_tensor", vn[cb], gv[gi], ssv[gi][:, 0:1], vnb_t, ALU.mult, ALU.mult,
                   reads=(gv_r[gi], ssv_r[gi], vnb_r), writes=(vn_r[cb],))
        for g in range(8):
            bank = (bank + 1) % 4
            pb, pbr = cx.psb[bank], cx.psb_r[bank]
            P.group("pe", [("matmul", (pb[:], cx.wbuf[g // 4][:, kc, (g % 4) * 128:(g % 4 + 1) * 128], h[:, kc, :]),
                            dict(start=(kc == 0), stop=(kc == KC - 1))) for kc in range(KC)], (h_r, cx.wbuf_r[g // 4]), (pbr,))
            xi ^= 1
            P.act(xv[xi], pb[:], AF.Identity, (pbr,), (xv_r[xi],), bias=bu_t[:, g:g + 1], extra_reads=(bu_r,))
            gelu_tanh(P, ug[xi], xv[xi], xv_r[xi], ug_r[xi], t1[xi], t1_r[xi])
            sb_ = 4 + g % 3
            sp, spr = cx.psb[sb_], cx.psb_r[sb_]
            calls = []
            for cb in range(4):
                calls.append(("matmul", (sp[:, cb * 128:(cb + 1) * 128], vn[cb][:, g * 128:(g + 1) * 128], ws_t[:, g, :]),
                              dict(start=True, stop=False)))
                calls.append(("matmul", (sp[:, cb * 128:(cb + 1) * 128], cx.ones[0:1, :], bs_t[0:1, g * 128:(g + 1) * 128]),
                              dict(start=False, stop=True)))
            P.group("pe", calls, tuple(vn_r) + (ws_r, cx.ones_r, bs_r), (spr,))
            P.call("dve", "tensor_tensor", cx.bufA[:, g, ts], ug[xi], sp[:], ALU.mult, reads=(ug_r[xi], spr), writes=(cx.bufA_r[g][tt],))


KINDS = ("gla", "diff", "sgu", "gla")


def fused_input_specs():
    sp = {"x": ([TL, D], F32), "ident": ([128, 128], F32), "tinc": ([64, 64], F32), "tupp": ([64, 64], F32),
          "tri": ([128, 128], F32), "gfin": ([128, KC], F32)}
    for L, kind in enumerate(KINDS):
        p = "l%d_" % L
        sp[p + "g1"] = ([128, KC], F32)
        sp[p + "g2"] = ([128, KC], F32)
        sp[p + "w_out"] = ([D, D], F32)
        sp[p + "w1"] = ([D, DFF], F32)
        sp[p + "w2"] = ([DFF, D], F32)
        if kind == "gla":
            sp[p + "wq"] = ([D, 128], F32); sp[p + "wk"] = ([D, 128], F32)
            sp[p + "wv"] = ([D, 256], F32); sp[p + "wg"] = ([D, 256], F32)
            sp[p + "gw1"] = ([D, 16], F32); sp[p + "gw2b"] = ([17, 128], F32); sp[p + "hnb"] = ([128, 256], F32)
        elif kind == "diff":
            sp[p + "wq"] = ([D, 256], F32); sp[p + "wk"] = ([D, 256], F32); sp[p + "wv"] = ([D, 256], F32)
            sp[p + "qaug"] = ([2, 3, T], F32); sp[p + "kaug"] = ([2, 3, T], F32); sp[p + "biasT"] = ([2, 128, 67], F32)
            sp[p + "lamp"] = ([1, 256], F32); sp[p + "hnc"] = ([128, 1], F32)
        else:
            sp[p + "w_in"] = ([D, 2 * D], F32); sp[p + "bu"] = ([128, KC], F32); sp[p + "bv"] = ([1, D], F32)
            sp[p + "vnb"] = ([128, D], F32); sp[p + "wsT"] = ([8, 128, 128], F32); sp[p + "bs"] = ([1, D], F32)
    return sp


def build_fused(nc):
    I = {k: nc.dram_tensor(k, shp, dt, kind="ExternalInput").ap() for k, (shp, dt) in fused_input_specs().items()}
    y_o = nc.dram_tensor("y_o", [TL, D], F32, kind="ExternalOutput").ap()
    P = Prog(nc)
    cx = Ctx(P, nc)
    emit_load_x(P, cx, I["x"], I["ident"])
    P.phase()
    keep = cx.trunk_base()
    for L, kind in enumerate(KINDS[:NL]):
        p = "l%d_" % L
        nm = "L%d" % L
        h_loc = nc.dram_tensor("h_loc" + nm, [NTT, D, 512], BF16).ap()
        h_loc_r = [Res(f"h_loc{nm}_{t}") for t in range(NTT)]
        if kind == "sgu":
            emit_norm_store(P, cx, I[p + "g1"], h_loc, h_loc_r, "g1" + nm)
            W = {k: I[p + k] for k in ("w_in", "bu", "bv", "vnb", "wsT", "bs")}
            W["tri"] = I["tri"]
            emit_sgu(P, cx, keep, h_loc, h_loc_r, W)
        else:
            h_all = nc.dram_tensor("h_all" + nm, [NTT, 4 * D, 512], BF16).ap()
            h_all_r = [Res(f"h_all{nm}_{t}") for t in range(NTT)]
            emit_norm_store(P, cx, I[p + "g1"], h_loc, h_loc_r, "g1" + nm, h_all, h_all_r)
            o_loc = nc.dram_tensor("o_loc" + nm, [4, 256, TL], BF16).ap()
            o_loc_r = [[Res(f"o_loc{nm}_{j}_{p}") for p in range(2)] for j in range(4)]
            o_all = nc.dram_tensor("o_all" + nm, [4, D, TL], BF16).ap()
            o_all_r = [Res(f"o_all{nm}_{j}") for j in range(4)]
            if kind == "gla":
                W = {k: I[p + k] for k in ("wq", "wk", "wv", "wg", "gw1", "gw2b", "hnb")}
                W.update(tinc=I["tinc"], tupp=I["tupp"], ident=I["ident"])
                emit_gla(P, cx, h_all, h_all_r, o_loc, o_loc_r, o_all, o_all_r, W, nm)
            else:
                W = {k: I[p + k] for k in ("wq", "wk", "wv", "qaug", "kaug", "biasT", "lamp", "hnc")}
                W["tri"] = I["tri"]
                emit_diff(P, cx, h_all, h_all_r, o_loc, o_loc_r, o_all, o_all_r, W, nm)
            P.phase()
            keep = cx.trunk_base()
            for kc in range(KC):
                def src(e, q, dst=cx.bufA[:, kc, :], srcv=o_all[:, kc * 128:(kc + 1) * 128, :]):
                    return dst, srcv[bass.ds(q, 1), :, :].rearrange("o p t -> p (o t)")
                P.dma_fn("sync", src, tuple(o_all_r), tuple(cx.bufA_r[kc]), key=f"ldo_{kc}")
        emit_wout_mlp(P, cx, keep, I[p + "w_out"], I[p + "g2"], I[p + "w1"], I[p + "w2"], nm)
        P.phase(keep)
        cx.gvi = 0
    emit_final(P, cx, keep, I["gfin"], y_o, I["ident"])
    P.emit()


def _pl(v):
    return np.ascontiguousarray(np.asarray(v, np.float32).reshape(-1, 128).T)


def fused_inputs(inp):
    s = np.arange(64)[:, None]
    c = np.arange(64)[None, :]
    tok = np.arange(T)
    a = tok % 512
    a_lo = (a % 256).astype(np.float32)
    a_hi = (a - a % 256).astype(np.float32)
    cc = (tok % 128).astype(np.float32)
    shared = {"ident": np.eye(128, dtype=np.float32), "tinc": (s <= c).astype(np.float32), "tupp": (s > c).astype(np.float32),
              "tri": (np.arange(128)[:, None] <= np.arange(128)[None, :]).astype(np.float32),
              "gfin": _pl(inp["final_norm"])}
    x = inp["x"].reshape(NCORES, TL, D)
    maps = []
    for cidx in range(NCORES):
        g = cidx % 4
        m = dict(shared)
        m["x"] = np.ascontiguousarray(x[cidx])
        for L, kind in enumerate(KINDS):
            p = "l%d_" % L
            m[p + "g1"] = _pl(inp[p + "norm1"])
            m[p + "g2"] = _pl(inp[p + "norm2"])
            m[p + "w_out"] = inp[p + "w_out"]
            m[p + "w1"] = inp[p + "mlp_w1"]
            m[p + "w2"] = inp[p + "mlp_w2"]
            w_in = inp[p + "w_in"]
            if kind == "gla":
                m[p + "wq"] = np.ascontiguousarray(w_in[:, g * 128:(g + 1) * 128])
                m[p + "wk"] = np.ascontiguousarray(w_in[:, 512 + g * 128:512 + (g + 1) * 128])
                m[p + "wv"] = np.ascontiguousarray(w_in[:, 1024 + g * 256:1024 + (g + 1) * 256])
                m[p + "wg"] = np.ascontiguousarray(w_in[:, 2048 + g * 256:2048 + (g + 1) * 256])
                m[p + "gw1"] = inp[p + "gate_w1"]
                m[p + "gw2b"] = np.ascontiguousarray(np.concatenate(
                    [inp[p + "gate_w2"][:, g * 128:(g + 1) * 128], inp[p + "gate_b"][None, g * 128:(g + 1) * 128]], axis=0))
                m[p + "hnb"] = np.ascontiguousarray(np.broadcast_to(inp[p + "head_norm"][None, :], (128, 256)))
            elif kind == "diff":
                m[p + "wq"] = np.ascontiguousarray(w_in[:, g * 256:(g + 1) * 256])
                m[p + "wk"] = np.ascontiguousarray(w_in[:, 1024 + g * 256:1024 + (g + 1) * 256])
                m[p + "wv"] = np.ascontiguousarray(w_in[:, 2048 + g * 256:2048 + (g + 1) * 256])
                qa = np.zeros((2, 3, T), np.float32)
                ka = np.zeros((2, 3, T), np.float32)
                bt = np.zeros((2, 128, 67), np.float32)
                for hh in range(2):
                    slope = 2.0 ** (-(2 * g + hh + 1))
                    qa[hh, 0] = -slope * a_lo
                    qa[hh, 1] = -slope * a_hi
                    qa[hh, 2] = 1.0
                    ka[hh, 0] = 1.0
                    ka[hh, 1] = 1.0
                    ka[hh, 2] = slope * cc
                    bt[hh] = (-slope * 128.0 * (np.arange(67) - 3))[None, :]
                m[p + "qaug"] = qa
                m[p + "kaug"] = ka
                m[p + "biasT"] = bt
                m[p + "lamp"] = np.ascontiguousarray(np.concatenate(
                    [inp[p + "lambda_q1"], inp[p + "lambda_k1"], inp[p + "lambda_q2"], inp[p + "lambda_k2"]])[None, :].astype(np.float32))
                m[p + "hnc"] = np.ascontiguousarray(inp[p + "head_norm"][:, None])
            else:
                b_in = inp[p + "b_in"]
                m[p + "w_in"] = w_in
                m[p + "bu"] = np.ascontiguousarray(b_in[:D].reshape(KC, 128).T)
                m[p + "bv"] = np.ascontiguousarray(b_in[None, D:])
                m[p + "vnb"] = np.ascontiguousarray(np.broadcast_to(inp[p + "v_norm"][None, :], (128, D)))
                m[p + "wsT"] = np.ascontiguousarray(inp[p + "w_s"].transpose(0, 2, 1))
                m[p + "bs"] = np.ascontiguousarray(inp[p + "b_s"].reshape(1, D))
        maps.append(m)
    return maps


_NC = {}


def kernel(**inp):
    inp = {k: np.asarray(v) for k, v in inp.items()}
    if (T, NL) not in _NC:
        nc = bass.Bass("TRN2", target_bir_lowering=False)
        build_fused(nc)
        _NC[(T, NL)] = nc
    res = run_bass_kernel_spmd(_NC[(T, NL)], fused_inputs(inp), core_ids=list(range(NCORES))).results
    y = np.stack([r["y_o"] for r in res], axis=0)
    return np.ascontiguousarray(y.reshape(2, T, D).astype(np.float32))
```

```python
import contextlib
import numpy as np
import ml_dtypes
import concourse.bass as bass
import concourse.mybir as mybir
from concourse.bass_utils import run_bass_kernel_spmd

F32 = mybir.dt.float32
BF16 = mybir.dt.bfloat16
AF = mybir.ActivationFunctionType
ALU = mybir.AluOpType
AX = mybir.AxisListType

NCORES = 8
D = 1024
KC = 8
DFF = 4096
EPS = 1e-6
T = 8192
TL = T // 4
NTT = TL // 512
GROUPS = [[0, 1, 2, 3], [4, 5, 6, 7]]


NL = 4
DEBUG_OUT = False


def configure(t, nl=4):
    global T, TL, NTT, NL
    NL = nl
    T = t
    TL = T // 4
    NTT = TL // 512


class Res:
    __slots__ = ("name", "w", "r")

    def __init__(self, name):
        self.name = name
        self.w = None
        self.r = []


class Prog:
    COMPUTE = ("act", "dve", "pool", "pe")

    def __init__(self, nc):
        self.nc = nc
        self.stack = contextlib.ExitStack()
        self.streams = {k: [] for k in ("sync", "act", "dve", "pool", "pe")}
        self.cnt = {k: 0 for k in self.COMPUTE}
        self.esem = {k: nc.alloc_semaphore(name="s_" + k) for k in self.COMPUTE}
        self.dsem = {}
        self.dcnt = {}
        self.seen = {k: {} for k in self.streams}
        self.nbuf = 0
        self.q4 = {}
        self.batch = {}
        self.cst = self.sb([128, 4], F32, "cst")
        self.cst_r = Res("cst")
        def init(e):
            e.memset(self.cst[:, 0:1], 0.0)
            e.memset(self.cst[:, 1:2], float(EPS))
            return e.memset(self.cst[:, 2:3], 1.0)
        self.op("pool", init, (), (self.cst_r,))
        self.zero = self.cst[:, 0:1]
        self.epsc = self.cst[:, 1:2]
        self.onec = self.cst[:, 2:3]

    def sb(self, shape, dtype, name=None):
        self.nbuf += 1
        return self.stack.enter_context(self.nc.sbuf_tensor("S_" + (name or f"sb{self.nbuf}"), list(shape), dtype))

    ARENA_BYTES = 136 * 1024

    def phase(self, keep=0):
        if not hasattr(self, "arena_t"):
            self.arena_t = self.sb([128, self.ARENA_BYTES // 4], F32, "arena")
            self.arena_bf = self.arena_t.bitcast(BF16)
        self.barrier()
        self.arena_off = keep
        self.batch = {}

    def tmp(self, shape, dtype, top=False):
        esz = 2 if dtype == BF16 else 4
        n = int(np.prod(shape[1:]))
        nbytes = (n * esz + 31) // 32 * 32
        top_off = getattr(self, "top_off", self.ARENA_BYTES)
        if top:
            top_off -= nbytes
            self.top_off = top_off
            start = top_off
        else:
            start = self.arena_off
            self.arena_off += nbytes
        assert self.arena_off <= top_off, ("arena overflow", self.arena_off, top_off, nbytes)
        base = self.arena_bf if dtype != F32 else self.arena_t
        o = start // esz
        ap = base[0:shape[0], o:o + n]
        if len(shape) == 3:
            ap = ap.rearrange("p (a b) -> p a b", a=shape[1])
        return ap

    def release_top(self):
        self.top_off = self.ARENA_BYTES

    def barrier(self):
        toks = [("e", k, self.cnt[k]) for k in self.COMPUTE if self.cnt[k] > 0]
        toks += [("d", k, v) for k, v in self.dcnt.items() if v > 0 and not k.startswith("ag")]
        for stream in self.streams:
            waits = []
            for kind, key, val in toks:
                if self.seen[stream].get((kind, key), 0) >= val:
                    continue
                self.seen[stream][(kind, key)] = val
                waits.append(self._sem_of((kind, key, val)))
            self.streams[stream].append((waits, None, None, 0))

    def ps(self, shape, dtype, name=None):
        self.nbuf += 1
        return self.stack.enter_context(self.nc.psum_tensor("P_" + (name or f"ps{self.nbuf}"), list(shape), dtype))

    def _sem_of(self, tok):
        kind, key, val = tok
        return (self.esem[key] if kind == "e" else self.dsem[key]), val

    def _waits(self, stream, reads, writes):
        need = {}
        def add(tok, raw):
            if tok is None:
                return
            kind, key, val = tok
            if kind == "e" and key == stream and not raw:
                return
            k = (kind, key)
            if val > need.get(k, 0):
                need[k] = val
        for r in reads:
            add(r.w, True)
        for w in writes:
            add(w.w, False)
            for t in w.r:
                add(t, False)
        out = []
        for (kind, key), val in need.items():
            if self.seen[stream].get((kind, key), 0) >= val:
                continue
            self.seen[stream][(kind, key)] = val
            out.append(self._sem_of((kind, key, val)))
        return out

    def _commit(self, tok, reads, writes):
        for r in reads:
            r.r.append(tok)
        for w in writes:
            w.w = tok
            w.r = []

    def op(self, eng, fn, reads=(), writes=()):
        waits = self._waits(eng, reads, writes)
        self.cnt[eng] += 1
        tok = ("e", eng, self.cnt[eng])
        self.streams[eng].append((waits, fn, self.esem[eng], 1))
        self._commit(tok, reads, writes)
        return tok

    def call(self, eng, meth, *args, reads=(), writes=(), **kw):
        return self.op(eng, lambda e: getattr(e, meth)(*args, **kw), reads, writes)

    def group(self, eng, calls, reads=(), writes=()):
        calls = list(calls)
        def run(e):
            ins = None
            for meth, args, kw in calls:
                ins = getattr(e, meth)(*args, **kw)
            return ins
        return self.op(eng, run, reads, writes)

    def act(self, out, in_, func, reads, writes, bias=None, scale=1.0, accum=None, extra_reads=()):
        b = self.zero if bias is None else bias
        if b.shape[0] != out.shape[0]:
            b = b[0:out.shape[0], :]
        kw = {} if accum is None else {"accum_out": accum}
        return self.op("act", lambda e: e.activation(out, in_, func, bias=b, scale=scale, **kw),
                       tuple(reads) + (self.cst_r,) + tuple(extra_reads), writes)

    def dma(self, queue, out, in_, reads=(), writes=(), key=None, batch=False):
        key = key or ("dma_" + writes[0].name)
        if key not in self.dsem:
            self.dsem[key] = self.nc.alloc_semaphore(name="d_" + key)
            self.dcnt[key] = 0
        waits = self._waits(queue, reads, writes)
        self.dcnt[key] += 16
        tok = ["d", key, self.dcnt[key]]
        if batch:
            for t in self.batch.setdefault(key, []):
                t[2] = self.dcnt[key]
            self.batch[key].append(tok)
        self.streams[queue].append((waits, lambda e: e.dma_start(out=out, in_=in_), self.dsem[key], 16))
        self._commit(tok, reads, writes)
        return tok

    def dma_fn(self, queue, fn, reads=(), writes=(), key=None):
        key = key or ("dma_" + writes[0].name)
        if key not in self.dsem:
            self.dsem[key] = self.nc.alloc_semaphore(name="d_" + key)
            self.dcnt[key] = 0
        waits = self._waits(queue, reads, writes)
        self.dcnt[key] += 16
        tok = ("d", key, self.dcnt[key])
        def run(e):
            if queue not in self.q4:
                self.q4[queue] = e.partition_id() % 4
            o, i = fn(e, self.q4[queue])
            return e.dma_start(out=o, in_=i)
        self.streams[queue].append((waits, run, self.dsem[key], 16))
        self._commit(tok, reads, writes)
        return tok

    def collective(self, kind, src, dst, reads, writes, key):
        assert key not in self.dsem
        self.dsem[key] = self.nc.alloc_semaphore(name="c_" + key)
        self.dcnt[key] = 1
        waits = self._waits("pool", reads, writes)
        tok = ["d", key, 1]
        self.streams["pool"].append((waits, lambda e: e.collective_compute(
            kind, ALU.bypass, replica_groups=GROUPS, ins=[src], outs=[dst]), self.dsem[key], 1))
        self._commit(tok, reads, writes)
        return tok

    def wait_all(self, stream, ress):
        waits = self._waits(stream, ress, ())
        self.streams[stream].append((waits, None, None, 0))

    def emit(self):
        nc = self.nc

        def replay(name):
            def run(e):
                for waits, fn, sem, inc in self.streams[name]:
                    for s, v in waits:
                        e.wait_ge(s, v)
                    if fn is not None:
                        ins = fn(e)
                        ins.then_inc(sem, inc)
            return run

        with nc.Block() as block:
            block.sync(replay("sync"))
            block.scalar(replay("act"))
            block.vector(replay("dve"))
            block.gpsimd(replay("pool"))
            block.tensor(replay("pe"))
        self.stack.close()


class Ctx:
    def __init__(self, P, nc):
        self.P = P
        self.nc = nc
        self.xT = P.sb([128, KC, TL], F32, "xT")
        self.xT_r = [[Res(f"xT_{k}_{t}") for t in range(NTT)] for k in range(KC)]
        self.ones = P.sb([128, 128], BF16, "ones")
        self.ones_r = Res("ones")
        P.call("pool", "memset", self.ones[:], 1.0, writes=(self.ones_r,))
        self.psb = [P.ps([128, 512], F32, f"bank{i}") for i in range(8)]
        self.psb_r = [Res(f"bank{i}") for i in range(8)]
        self.sqi = 0
        self.rsi = 0

    def trunk_base(self):
        P = self.P
        self.bufA = P.tmp([128, KC, TL], BF16)
        self.bufA_r = [[Res(f"bufA_{k}_{t}") for t in range(NTT)] for k in range(KC)]
        self.wbuf = [P.tmp([128, KC, 512], BF16) for i in range(4)]
        self.wbuf_r = [Res(f"wbuf{i}") for i in range(4)]
        self.sq = [P.tmp([128, 512], BF16) for i in range(3)]
        self.sq_r = [Res(f"sq{i}") for i in range(3)]
        self.rstd = [P.tmp([128, 512], F32) for i in range(2)]
        self.rstd_r = [Res(f"rstd{i}") for i in range(2)]
        self.gv = [P.tmp([128, KC], F32) for i in range(3)]
        self.gvi = 0
        return P.arena_off

    def load_vec(self, dram_ap, name):
        P = self.P
        t = self.gv[self.gvi]
        self.gvi += 1
        r = Res(name)
        P.dma("sync", t, dram_ap, (), (r,))
        return t, r

    def rmsnorm(self, g, g_r, dst, dst_r, bank, tts=None, tcol0=None):
        P = self.P
        for tt in (range(NTT) if tts is None else tts):
            ts = slice(tt * 512, (tt + 1) * 512)
            ds_ = ts if tcol0 is None else slice(0, 512)
            pb, pbr = self.psb[bank], self.psb_r[bank]
            for kc in range(KC):
                i = self.sqi = (self.sqi + 1) % 3
                sq, sqr = self.sq[i], self.sq_r[i]
                P.act(sq, self.xT[:, kc, ts], AF.Square, (self.xT_r[kc][tt],), (sqr,))
                P.call("pe", "matmul", pb[:], self.ones[:], sq, start=(kc == 0), stop=(kc == KC - 1),
                       reads=(sqr, self.ones_r), writes=(pbr,))
            j = self.rsi = (self.rsi + 1) % 2
            rs, rsr = self.rstd[j], self.rstd_r[j]
            P.act(rs, pb[:], AF.Sqrt, (pbr,), (rsr,), bias=P.epsc, scale=1.0 / D)
            P.call("dve", "reciprocal", rs, rs, reads=(rsr,), writes=(rsr,))
            for kc in range(KC):
                P.call("dve", "scalar_tensor_tensor", dst[:, kc, ds_], self.xT[:, kc, ts], g[:, kc:kc + 1], rs,
                       ALU.mult, ALU.mult, reads=(self.xT_r[kc][tt], rsr, g_r), writes=(dst_r[kc][tt],))


def emit_load_x(P, cx, x, ident):
    P.phase()
    identf = P.tmp([128, 128], F32)
    identf_r = Res("identf")
    P.dma("sync", identf, ident, (), (identf_r,))
    xin = [P.tmp([128, D], F32) for i in range(4)]
    xin_r = [Res(f"xin{i}") for i in range(4)]
    xv = x.rearrange("(n p) d -> n p d", p=128)
    bi = 0
    for tt in range(NTT):
        for tb in range(4):
            P.dma("sync", xin[tb], xv[tt * 4 + tb], (), (xin_r[tb],))
        for kc in range(KC):
            b = bi = (bi + 1) % 4
            pb, pbr = cx.psb[b], cx.psb_r[b]
            P.group("pe", [("transpose", (pb[:, tb * 128:(tb + 1) * 128], xin[tb][:, kc * 128:(kc + 1) * 128], identf), {})
                           for tb in range(4)], tuple(xin_r) + (identf_r,), (pbr,))
            if kc % 2 == 0:
                P.act(cx.xT[:, kc, tt * 512:(tt + 1) * 512], pb[:], AF.Identity, (pbr,), (cx.xT_r[kc][tt],))
            else:
                P.call("dve", "tensor_copy", cx.xT[:, kc, tt * 512:(tt + 1) * 512], pb[:], reads=(pbr,), writes=(cx.xT_r[kc][tt],))


def emit_norm_store(P, cx, g_dram, h_loc, h_loc_r, name, h_all=None, h_all_r=None):
    g, g_r = cx.load_vec(g_dram, name)
    for tt in range(NTT):
        cx.rmsnorm(g, g_r, cx.bufA, cx.bufA_r, bank=7, tts=[tt])
        dv = h_loc[tt].rearrange("(kc p) t -> p kc t", p=128)
        P.dma("sync", dv, cx.bufA[:, :, tt * 512:(tt + 1) * 512], tuple(cx.bufA_r[kc][tt] for kc in range(KC)), (h_loc_r[tt],),
              key=f"st_h_{tt}")
        if h_all is not None:
            P.collective("AllGather", h_loc[tt], h_all[tt], (h_loc_r[tt],), (h_all_r[tt],), f"agh{name}_{tt}")


def emit_wout_mlp(P, cx, keep, w_out, g2, w1, w2, name):
    P.phase(keep)
    wo = P.tmp([128, KC, D], BF16)
    wo_r = Res("wo")
    P.dma("pool", wo, w_out.rearrange("(kc p) n -> p kc n", p=128), (), (wo_r,))
    bi = 0
    for tt in range(NTT):
        ts = slice(tt * 512, (tt + 1) * 512)
        for n in range(KC):
            b = bi = (bi + 1) % 4
            pb, pbr = cx.psb[b], cx.psb_r[b]
            P.group("pe", [("matmul", (pb[:], wo[:, fc, n * 128:(n + 1) * 128], cx.bufA[:, fc, ts]),
                            dict(start=(fc == 0), stop=(fc == KC - 1))) for fc in range(KC)],
                    (wo_r,) + tuple(cx.bufA_r[fc][tt] for fc in range(KC)), (pbr,))
            P.call("dve", "tensor_tensor", cx.xT[:, n, ts], cx.xT[:, n, ts], pb[:], ALU.add,
                   reads=(pbr, cx.xT_r[n][tt]), writes=(cx.xT_r[n][tt],))
    g2t, g2r = cx.load_vec(g2, "g2" + name)
    cx.rmsnorm(g2t, g2r, cx.bufA, cx.bufA_r, bank=7)
    P.phase(keep)
    FG = 512
    NFG = DFF // FG
    FC = FG // 128
    w1b = [cx.wbuf[0], cx.wbuf[1]]
    w1b_r = [cx.wbuf_r[0], cx.wbuf_r[1]]
    w2b = [cx.wbuf[2 + i].rearrange("p a b -> p (a b)").rearrange("p (c n) -> p c n", c=FC) for i in range(2)]
    w2b_r = [cx.wbuf_r[2], cx.wbuf_r[3]]
    a2 = [P.tmp([128, FC, 512], BF16) for i in range(2)]
    a2_r = [[Res(f"a2_{i}_{c}") for c in range(FC)] for i in range(2)]
    rl = [P.tmp([128, 512], F32) for i in range(3)]
    rl_r = [Res(f"rl{i}") for i in range(3)]
    w1v = w1.rearrange("(kc p) f -> p kc f", p=128)
    w2v = w2.rearrange("(fc p) n -> p fc n", p=128)

    def load_w(fg):
        s = fg % 2
        P.dma("pool", w1b[s], w1v[:, :, fg * FG:(fg + 1) * FG], (), (w1b_r[s],))
        P.dma("pool", w2b[s], w2v[:, fg * FC:(fg + 1) * FC, :], (), (w2b_r[s],))

    load_w(0)
    steps = [(fg, tt) for fg in range(NFG) for tt in range(NTT)]
    st = {"abank": 0, "ybank": 0}

    def stage1(i):
        fg, tt = steps[i]
        s = fg % 2
        ai = i % 2
        ts = slice(tt * 512, (tt + 1) * 512)
        for c in range(FC):
            st["abank"] = (st["abank"] + 1) % 4
            pb, pbr = cx.psb[st["abank"]], cx.psb_r[st["abank"]]
            P.group("pe", [("matmul", (pb[:], w1b[s][:, kc, c * 128:(c + 1) * 128], cx.bufA[:, kc, ts]),
                            dict(start=(kc == 0), stop=(kc == KC - 1))) for kc in range(KC)],
                    (w1b_r[s],) + tuple(cx.bufA_r[kc][tt] for kc in range(KC)), (pbr,))
            ri = (ai * FC + c) % 3
            P.act(rl[ri], pb[:], AF.Relu, (pbr,), (rl_r[ri],))
            P.call("pool", "tensor_tensor", a2[ai][:, c, :], rl[ri], rl[ri], ALU.mult, reads=(rl_r[ri],), writes=(a2_r[ai][c],))

    def stage2(i):
        fg, tt = steps[i]
        s = fg % 2
        ai = i % 2
        ts = slice(tt * 512, (tt + 1) * 512)
        for n in range(KC):
            st["ybank"] = 4 + (st["ybank"] + 1) % 4
            pb, pbr = cx.psb[st["ybank"]], cx.psb_r[st["ybank"]]
            P.group("pe", [("matmul", (pb[:], w2b[s][:, c, n * 128:(n + 1) * 128], a2[ai][:, c, :]),
                            dict(start=(c == 0), stop=(c == FC - 1))) for c in range(FC)],
                    (w2b_r[s],) + tuple(a2_r[ai]), (pbr,))
            P.call("dve", "tensor_tensor", cx.xT[:, n, ts], cx.xT[:, n, ts], pb[:], ALU.add,
                   reads=(pbr, cx.xT_r[n][tt]), writes=(cx.xT_r[n][tt],))

    stage1(0)
    for i in range(len(steps)):
        fg, tt = steps[i]
        if tt == 0 and fg + 1 < NFG:
            load_w(fg + 1)
        if i + 1 < len(steps):
            stage1(i + 1)
        stage2(i)


def emit_final(P, cx, keep, gf, y_o, ident):
    P.phase(keep)
    g3t, g3r = cx.load_vec(gf, "gf")
    identf = P.tmp([128, 128], F32)
    identf_r = Res("identf")
    P.dma("sync", identf, ident, (), (identf_r,))
    yT = P.tmp([128, KC, 512], F32)
    yT_r = [[Res(f"yT_{k}")] * NTT for k in range(KC)]
    yo = [P.tmp([128, D], F32) for i in range(2)]
    yo_r = [Res(f"yo{i}") for i in range(2)]
    yv = y_o.rearrange("(n p) d -> n p d", p=128)
    outr = Res("st_y")
    oi = 0
    bi = 0
    for tt in range(NTT):
        cx.rmsnorm(g3t, g3r, yT, yT_r, bank=7, tts=[tt], tcol0=0)
        for tb in range(4):
            oi ^= 1
            for half in range(2):
                b = bi = (bi + 1) % 4
                pb, pbr = cx.psb[b], cx.psb_r[b]
                P.group("pe", [("transpose", (pb[:, q * 128:(q + 1) * 128], yT[:, half * 4 + q, tb * 128:(tb + 1) * 128], identf), {})
                               for q in range(4)], tuple(yT_r[k][0] for k in range(KC)) + (identf_r,), (pbr,))
                if half == 0:
                    P.act(yo[oi][:, 0:512], pb[:], AF.Identity, (pbr,), (yo_r[oi],))
                else:
                    P.call("dve", "tensor_copy", yo[oi][:, 512:1024], pb[:], reads=(pbr,), writes=(yo_r[oi],))
            P.dma("sync", yv[tt * 4 + tb], yo[oi], (yo_r[oi],), (outr,), key="st_y")
    P.wait_all("sync", (outr,))


GLA_HK = 128
GLA_HV = 256


def preload_gla(P, W):
    L = {}

    def wload(ap, ncol):
        t = P.tmp([128, KC, ncol], BF16, top=True)
        r = Res("w")
        P.dma("pool", t, ap.rearrange("(kc p) n -> p kc n", p=128), (), (r,), key="glaw", batch=True)
        return t, r

    L["wq"] = wload(W["wq"], 128)
    L["wk"] = wload(W["wk"], 128)
    L["wv"] = wload(W["wv"], 256)
    L["wg"] = wload(W["wg"], 256)
    L["gw1"] = wload(W["gw1"], 16)
    gw2_s = P.tmp([17, 128], BF16, top=True); gw2_r = Res("gw2")
    P.dma("pool", gw2_s, W["gw2b"], (), (gw2_r,), key="glaw", batch=True)
    L["gw2"] = (gw2_s, gw2_r)
    idb = P.tmp([128, 128], BF16, top=True); idb_r = Res("idb")
    P.dma("pool", idb, W["ident"], (), (idb_r,), key="glaw", batch=True)
    L["idb"] = (idb, idb_r)
    hn_s = P.tmp([128, 256], F32, top=True); hn_r = Res("hn")
    P.dma("sync", hn_s, W["hnb"], (), (hn_r,), key="glac", batch=True)
    L["hn"] = (hn_s, hn_r)
    ti_s = P.tmp([64, 64], F32, top=True); ti_r = Res("ti")
    P.dma("sync", ti_s, W["tinc"], (), (ti_r,), key="glac", batch=True)
    L["ti"] = (ti_s, ti_r)
    tu_s = P.tmp([64, 64], F32, top=True); tu_r = Res("tu")
    P.dma("sync", tu_s, W["tupp"], (), (tu_r,), key="glac", batch=True)
    L["tu"] = (tu_s, tu_r)
    ti8 = P.tmp([64, 8, 64], F32, top=True); ti8_r = Res("ti8")
    for j in range(8):
        P.dma("sync", ti8[:, j, :], W["tinc"], (), (ti8_r,), key="glac", batch=True)
    L["ti8"] = (ti8, ti8_r)
    return L


def emit_gla(P, cx, h_all, h_all_r, o_loc, o_loc_r, o_all, o_all_r, L, name):
    P.phase()
    wq_s, wq_r = L["wq"]; wk_s, wk_r = L["wk"]; wv_s, wv_r = L["wv"]; wg_s, wg_r = L["wg"]
    gw1_s, gw1_r = L["gw1"]; gw2_s, gw2_r = L["gw2"]; idb, idb_r = L["idb"]
    hn_s, hn_r = L["hn"]; ti_s, ti_r = L["ti"]; tu_s, tu_r = L["tu"]; ti8, ti8_r = L["ti8"]

    hb = [P.tmp([128, KC, 512], BF16) for i in range(2)]
    hb_r = [Res(f"hb{i}") for i in range(2)]
    hvs = [h_all[j].rearrange("(r kc p) t -> r p kc t", r=4, p=128) for j in range(NTT)]

    def h_load(dst, dst_r, tile):
        P.dma("sync", dst, hvs[tile % NTT][tile // NTT], (h_all_r[tile % NTT],), (dst_r,))

    S = P.tmp([128, 256], F32); S_r = Res("S")
    Sb = P.tmp([128, 256], BF16); Sb_r = Res("Sb")
    P.call("pool", "memset", S, 0.0, writes=(S_r,))
    P.call("pool", "memset", Sb, 0.0, writes=(Sb_r,))
    r17 = P.tmp([17, 512], BF16); r17_r = Res("r17")
    P.call("pool", "memset", r17, 1.0, writes=(r17_r,))

    bk, br = cx.psb, cx.psb_r
    bkT = bk[5].bitcast(BF16)[:, :].rearrange("p (a b) -> p a b", a=2)

    ez = P.tmp([64, 8, 128], F32); ez_r = Res("ez")
    lsp = P.tmp([64, 8, 128], F32); lsp_r = Res("lsp")
    E1 = P.tmp([128, 512], F32); E1_r = Res("E1")
    Ei = P.tmp([128, 512], F32); Ei_r = Res("Ei")
    eU = P.tmp([64, 8, 128], F32); eU_r = Res("eU")
    qd = P.tmp([128, 512], BF16); qd_r = Res("qd")
    ki = P.tmp([128, 512], BF16); ki_r = Res("ki")
    ke = P.tmp([64, 8, 128], BF16); ke_r = Res("ke")
    kf = [P.tmp([64, 8, 128], F32) for i in range(2)]; kf_r = [[Res(f"kf{i}_{j}") for j in range(8)] for i in range(2)]
    vb = [P.tmp([64, 8, 256], BF16) for i in range(2)]; vb_r = [[Res(f"vb{i}_{j}") for j in range(8)] for i in range(2)]
    sg = [P.tmp([64, 8, 256], F32) for i in range(2)]; sg_r = [[Res(f"sg{i}_{j}") for j in range(8)] for i in range(2)]
    at = P.tmp([64, 8, 64], BF16); at_r = Res("at")
    osb_ = P.tmp([64, 8, 256], F32); osb_r = [Res(f"os{i}") for i in range(8)]
    sq = [P.tmp([64, 256], F32) for i in range(2)]; sq_r = [Res(f"sq{i}") for i in range(2)]
    ss = P.tmp([64, 8], F32); ss_r = Res("ss")
    rs = P.tmp([64, 8], F32); rs_r = Res("rs")
    tt_ = [P.tmp([64, 256], F32) for i in range(2)]; tt_r = [Res(f"tt{i}") for i in range(2)]
    of = [P.tmp([64, 256], BF16) for i in range(2)]; of_r = [Res(f"of{i}") for i in range(2)]
    ob = [P.tmp([128, 2, 512], BF16) for i in range(2)]; ob_r = [Res(f"ob{i}") for i in range(2)]
    ovs = [o_loc[j].rearrange("(h p) t -> p h t", p=128) for j in range(4)]
    nq = TL // 512

    def mm8(dst, lhs_fn, rhs_fn, reads, dst_r):
        P.group("pe", [("matmul", (dst, lhs_fn(kc), rhs_fn(kc)), dict(start=(kc == 0), stop=(kc == KC - 1))) for kc in range(KC)],
                reads, (dst_r,))

    def proj_chunk(tile, j):
        p = tile % 2
        h, h_r = hb[p], hb_r[p]
        cs = slice(j * 64, (j + 1) * 64)
        mm8(bk[6][0:64, 0:128], lambda kc: h[:, kc, cs], lambda kc: wk_s[:, kc, :], (wk_r, h_r), br[6])
        mm8(bk[6][0:64, 128:384], lambda kc: h[:, kc, cs], lambda kc: wv_s[:, kc, :], (wv_r, h_r), br[6])
        P.act(kf[p][:, j, :], bk[6][0:64, 0:128], AF.Identity, (br[6],), (kf_r[p][j],))
        P.act(vb[p][:, j, :], bk[6][0:64, 128:384], AF.Identity, (br[6],), (vb_r[p][j],))
        mm8(bk[7][0:64, 0:256], lambda kc: h[:, kc, cs], lambda kc: wg_s[:, kc, :], (wg_r, h_r), br[7])
        P.act(sg[p][:, j, :], bk[7][0:64, 0:256], AF.Silu, (br[7],), (sg_r[p][j],))

    h_load(hb[0], hb_r[0], 0)
    for j in range(8):
        proj_chunk(0, j)
    for tile in range(T // 512):
        if tile + 1 < T // 512:
            h_load(hb[(tile + 1) % 2], hb_r[(tile + 1) % 2], tile + 1)
        h, h_r = hb[tile % 2], hb_r[tile % 2]
        mm8(bk[0][:], lambda kc: wq_s[:, kc, :], lambda kc: h[:, kc, :], (wq_r, h_r), br[0])
        mm8(bk[1][:], lambda kc: wk_s[:, kc, :], lambda kc: h[:, kc, :], (wk_r, h_r), br[1])
        mm8(bk[2][0:16, :], lambda kc: gw1_s[:, kc, :], lambda kc: h[:, kc, :], (gw1_r, h_r), br[2])
        P.act(r17[0:16, :], bk[2][0:16, :], AF.Identity, (br[2],), (r17_r,))
        for half in range(2):
            b = 3 + half
            P.group("pe", [("matmul", (bk[b][0:64, c * 128:(c + 1) * 128], r17[0:17, (half * 4 + c) * 64:(half * 4 + c + 1) * 64], gw2_s[0:17, :]),
                            dict(start=True, stop=True)) for c in range(4)], (r17_r, gw2_r), (br[b],))
            P.act(ez[:, half * 4:half * 4 + 4, :], bk[b][0:64, :].rearrange("p (a b) -> p a b", a=4), AF.Exp, (br[b],), (ez_r,), scale=-1.0)
        P.act(lsp[:, 0:4, :], ez[:, 0:4, :], AF.Ln, (ez_r,), (lsp_r,), bias=P.onec)
        P.act(lsp[:, 4:8, :], ez[:, 4:8, :], AF.Ln, (ez_r,), (lsp_r,), bias=P.onec)
        P.group("pe", [("matmul", (bk[5][:, c * 64:(c + 1) * 64], lsp[:, c, :], ti_s), dict(start=True, stop=True)) for c in range(8)],
                (lsp_r, ti_r), (br[5],))
        for half in range(2):
            b = 6 + half
            P.group("pe", [("matmul", (bk[b][0:64, c * 128:(c + 1) * 128], tu_s, lsp[:, half * 4 + c, :]), dict(start=True, stop=True))
                           for c in range(4)], (lsp_r, tu_r), (br[b],))
        P.act(E1, bk[5][:], AF.Exp, (br[5],), (E1_r,), scale=-1.0 / 16)
        P.act(Ei, bk[5][:], AF.Exp, (br[5],), (Ei_r,), scale=1.0 / 16)
        for half in range(2):
            b = 6 + half
            P.act(eU[:, half * 4:half * 4 + 4, :], bk[b][0:64, :].rearrange("p (a b) -> p a b", a=4), AF.Exp, (br[b],), (eU_r,), scale=-1.0 / 16)
        P.call("dve", "scalar_tensor_tensor", qd, bk[0][:], float(GLA_HK ** -0.5), E1, ALU.mult, ALU.mult,
               reads=(br[0], E1_r), writes=(qd_r,))
        P.call("dve", "tensor_tensor", ki, bk[1][:], Ei, ALU.mult, reads=(br[1], Ei_r), writes=(ki_r,))
        pp = tile % 2
        P.call("dve", "tensor_tensor", ke, kf[pp], eU, ALU.mult, reads=tuple(kf_r[pp]) + (eU_r,), writes=(ke_r,))
        P.group("pe", [("matmul", (bk[0][0:64, c * 64:(c + 1) * 64], ki[:, c * 64:(c + 1) * 64], qd[:, c * 64:(c + 1) * 64]),
                        dict(start=True, stop=True)) for c in range(8)], (ki_r, qd_r), (br[0],))
        P.call("dve", "tensor_tensor", at, bk[0][0:64, :].rearrange("p (a b) -> p a b", a=8), ti8, ALU.mult,
               reads=(br[0], ti8_r), writes=(at_r,))
        for j in range(8):
            cs = slice(j * 64, (j + 1) * 64)
            bo, bkv = 1 + j % 2, 3 + j % 2
            if tile + 1 < T // 512:
                proj_chunk(tile + 1, j)
            P.call("pe", "matmul", bk[bkv][:, 0:256], ke[:, j, :], vb[pp][:, j, :], start=True, stop=True,
                   reads=(ke_r, vb_r[pp][j]), writes=(br[bkv],))
            P.group("pe", [("matmul", (bk[bo][0:64, 0:256], qd[:, cs], Sb), dict(start=True, stop=False)),
                           ("matmul", (bk[bo][0:64, 0:256], at[:, j, :], vb[pp][:, j, :]), dict(start=False, stop=True))],
                    (qd_r, Sb_r, at_r, vb_r[pp][j]), (br[bo],))
            P.call("dve", "scalar_tensor_tensor", S, S, E1[:, j * 64 + 63:j * 64 + 64], bk[bkv][:, 0:256], ALU.mult, ALU.add,
                   reads=(S_r, E1_r, br[bkv]), writes=(S_r,))
            P.act(Sb, S, AF.Identity, (S_r,), (Sb_r,))
            P.call("dve", "tensor_copy", osb_[:, j, :], bk[bo][0:64, 0:256], reads=(br[bo],), writes=(osb_r[j],))
            P.act(sq[j % 2], osb_[:, j, :], AF.Square, (osb_r[j],), (sq_r[j % 2],))
            P.call("dve", "reduce_sum", ss[:, j:j + 1], sq[j % 2], AX.X, reads=(sq_r[j % 2],), writes=(ss_r,))
        P.act(rs, ss, AF.Sqrt, (ss_r,), (rs_r,), bias=P.epsc, scale=1.0 / GLA_HV)
        P.call("dve", "reciprocal", rs, rs, reads=(rs_r,), writes=(rs_r,))
        for j in range(8):
            cs = slice(j * 64, (j + 1) * 64)
            i = j % 2
            P.call("dve", "scalar_tensor_tensor", tt_[i], osb_[:, j, :], rs[:, j:j + 1], hn_s[0:64, :], ALU.mult, ALU.mult,
                   reads=(osb_r[j], rs_r, hn_r), writes=(tt_r[i],))
            P.call("dve", "tensor_tensor", of[i], tt_[i], sg[pp][:, j, :], ALU.mult, reads=(tt_r[i], sg_r[pp][j]), writes=(of_r[i],))
            P.group("pe", [("transpose", (bkT[:, 0, cs], of[i][:, 0:128], idb[0:64, 0:64]), {}),
                           ("transpose", (bkT[:, 1, cs], of[i][:, 128:256], idb[0:64, 0:64]), {})],
                    (of_r[i], idb_r), (br[5],))
        o_, o_r = ob[tile % 2], ob_r[tile % 2]
        P.act(o_[:, 0, :], bkT[:, 0, :], AF.Identity, (br[5],), (o_r,))
        P.call("dve", "tensor_copy", o_[:, 1, :], bkT[:, 1, :], reads=(br[5],), writes=(o_r,))
        P.dma("sync", ovs[tile // nq][:, :, (tile % nq) * 512:(tile % nq + 1) * 512], o_, (o_r,), (o_loc_r[tile // nq][tile % 2],),
              key=f"st_o{name}_{tile % 2}")
        if tile % nq == nq - 1:
            P.collective("AllGather", o_loc[tile // nq], o_all[tile // nq], tuple(o_loc_r[tile // nq]), (o_all_r[tile // nq],),
                         f"ago{name}_{tile // nq}")


DIFF_LAMBDA_INIT = 0.8 - 0.6 * float(np.exp(-0.3 * 1))


def preload_diff(P, W):
    L = {}

    def wload(ap, ncol):
        t = P.tmp([128, KC, ncol], BF16, top=True)
        r = Res("w")
        P.dma("pool", t, ap.rearrange("(kc p) n -> p kc n", p=128), (), (r,), key="difw", batch=True)
        return t, r

    L["wq"] = wload(W["wq"], 256)
    L["wk"] = wload(W["wk"], 256)
    L["wv"] = wload(W["wv"], 256)
    tri_s = P.tmp([128, 128], BF16, top=True); tri_r = Res("tri")
    P.dma("pool", tri_s, W["tri"], (), (tri_r,), key="difw", batch=True)
    L["tri"] = (tri_s, tri_r)
    return L


def emit_diff(P, cx, h_all, h_all_r, o_loc, o_loc_r, o_all, o_all_r, W, L, name):
    P.phase()
    NT = T // 512
    wq_s, wq_r = L["wq"]; wk_s, wk_r = L["wk"]; wv_s, wv_r = L["wv"]; tri_s, tri_r = L["tri"]
    ones, ones_r = cx.ones, cx.ones_r
    onef = P.tmp([1, 128], F32); onef_r = Res("onef")
    P.call("pool", "memset", onef, 1.0, writes=(onef_r,))
    bias_s = P.tmp([128, 2, 67], F32); bias_r = Res("bias")
    P.dma("sync", bias_s, W["biasT"].rearrange("h p n -> p h n"), (), (bias_r,), key="difc", batch=True)
    hn_s = P.tmp([128, 1], F32); hn_r = Res("hn")
    P.dma("sync", hn_s, W["hnc"], (), (hn_r,), key="difc", batch=True)
    bk, bk_r = cx.psb, cx.psb_r

    lp = P.tmp([1, 256], F32); lp_r = Res("lp")
    P.dma("sync", lp, W["lamp"], (), (lp_r,), key="difc", batch=True)
    P.call("dve", "tensor_scalar", hn_s, hn_s, float(1.0 - DIFF_LAMBDA_INIT), None, ALU.mult, reads=(hn_r,), writes=(hn_r,))
    lw = P.tmp([1, 136], F32); lw_r = Res("lw")
    P.call("dve", "tensor_tensor", lw[:, 0:64], lp[:, 0:64], lp[:, 64:128], ALU.mult, reads=(lp_r,), writes=(lw_r,))
    P.call("dve", "tensor_tensor", lw[:, 64:128], lp[:, 128:192], lp[:, 192:256], ALU.mult, reads=(lp_r, lw_r), writes=(lw_r,))
    P.call("dve", "reduce_sum", lw[:, 128:129], lw[:, 0:64], AX.X, reads=(lw_r,), writes=(lw_r,))
    P.call("dve", "reduce_sum", lw[:, 129:130], lw[:, 64:128], AX.X, reads=(lw_r,), writes=(lw_r,))
    P.act(lw[:, 130:132], lw[:, 128:130], AF.Exp, (lw_r,), (lw_r,))
    P.call("dve", "tensor_tensor", lw[:, 132:133], lw[:, 131:132], lw[:, 130:131], ALU.subtract, reads=(lw_r,), writes=(lw_r,))
    P.call("dve", "tensor_scalar", lw[:, 133:134], lw[:, 132:133], float(-DIFF_LAMBDA_INIT), None, ALU.add, reads=(lw_r,), writes=(lw_r,))
    P.call("pe", "matmul", bk[0][:, 0:1], onef[0:1, :], lw[0:1, 133:134], start=True, stop=True,
           reads=(onef_r, lw_r), writes=(bk_r[0],))
    nlam = P.tmp([128, 1], F32); nlam_r = Res("nlam")
    P.call("dve", "tensor_copy", nlam, bk[0][:, 0:1], reads=(bk_r[0],), writes=(nlam_r,))

    Qa = [P.tmp([67, T], BF16) for r in range(2)]
    Ka = [P.tmp([67, T], BF16) for r in range(2)]
    Qa_r = [[Res(f"Qa{r}_{t}") for t in range(NT)] for r in range(2)]
    Ka_r = [[Res(f"Ka{r}_{t}") for t in range(NT)] for r in range(2)]
    Qg_r = [Res(f"Qg{r}") for r in range(2)]
    Kg_r = [Res(f"Kg{r}") for r in range(2)]
    V = P.tmp([128, T // 128, 128], BF16)
    V_r = [Res(f"V{t}") for t in range(NT)]
    hb = [P.tmp([128, KC, 512], BF16) for i in range(2)]
    hb_r = [Res(f"hb{i}") for i in range(2)]
    hvs = [h_all[j].rearrange("(r kc p) t -> r p kc t", r=4, p=128) for j in range(NTT)]

    def h_load(dst, dst_r, tile):
        P.dma("sync", dst, hvs[tile % NTT][tile // NTT], (h_all_r[tile % NTT],), (dst_r,))

    Pt = [P.tmp([128, 512], BF16) for i in range(4)]
    Pt_r = [Res(f"Pt{i}") for i in range(4)]
    rec = [P.tmp([128, 512], F32) for i in range(2)]
    rec_r = [Res(f"rec{i}") for i in range(2)]
    on = [P.tmp([128, 512], F32) for i in range(2)]
    on_r = [Res(f"on{i}") for i in range(2)]
    oo = P.tmp([128, 512], F32); oo_r = Res("oo")
    sq = P.tmp([128, 512], BF16); sq_r = Res("sq")
    rs = P.tmp([128, 512], F32); rs_r = Res("rs")
    ofin = [P.tmp([128, 512], BF16) for i in range(2)]
    ofin_r = [Res(f"ofin{i}") for i in range(2)]
    hcnt = 0
    sbank = 0
    pti = 0
    for hh in range(2):
        for r in range(2):
            P.dma("pool", Qa[r][64:67, :], W["qaug"][hh], (), (Qg_r[r],), key="difa", batch=True)
            P.dma("pool", Ka[r][64:67, :], W["kaug"][hh], (), (Kg_r[r],), key="difa", batch=True)
        for tile in range(NT):
            s = hcnt % 2
            hcnt += 1
            h_load(hb[s], hb_r[s], tile)
            h, h_r = hb[s], hb_r[s]
            cols = slice(tile * 512, (tile + 1) * 512)
            for r in range(2):
                wc = slice((hh * 2 + r) * 64, (hh * 2 + r + 1) * 64)
                for (w_s, w_r, dst, dst_r, sc) in ((wq_s, wq_r, Qa, Qa_r, 0.125), (wk_s, wk_r, Ka, Ka_r, 1.0)):
                    sbank = (sbank + 1) % 4
                    pb, pbr = bk[sbank], bk_r[sbank]
                    P.group("pe", [("matmul", (pb[0:64, :], w_s[:, kc, wc], h[:, kc, :]), dict(start=(kc == 0), stop=(kc == KC - 1)))
                                   for kc in range(KC)], (w_r, h_r), (pbr,))
                    P.act(dst[r][0:64, cols], pb[0:64, :], AF.Identity, (pbr,), (dst_r[r][tile],), scale=sc)
            sbank = (sbank + 1) % 4
            pb, pbr = bk[sbank], bk_r[sbank]
            for tb in range(4):
                P.group("pe", [("matmul", (pb[:, tb * 128:(tb + 1) * 128], h[:, kc, tb * 128:(tb + 1) * 128], wv_s[:, kc, hh * 128:(hh + 1) * 128]),
                                dict(start=(kc == 0), stop=(kc == KC - 1))) for kc in range(KC)], (wv_r, h_r), (pbr,))
            P.call("dve", "tensor_copy", V[:, tile * 4:(tile + 1) * 4, :], pb[:].rearrange("p (a b) -> p a b", a=4),
                   reads=(pbr,), writes=(V_r[tile],))
        LA = 2
        for It in range(NT):
            nJ = 4 * It + 4
            pend = []

            def consume(item, It=It, nJ=nJ):
                Jt, r, c0, pt, ptr = item
                P.call("pe", "matmul", bk[4 + r][:, c0:512], V[:, Jt, :], pt[:, c0:512], start=(Jt == 0), stop=(Jt == nJ - 1),
                       reads=(V_r[Jt // 4], ptr), writes=(bk_r[4 + r],))
                P.call("pe", "matmul", bk[6 + r][:, c0:512], ones[:], pt[:, c0:512], start=(Jt == 0), stop=(Jt == nJ - 1),
                       reads=(ones_r, ptr), writes=(bk_r[6 + r],))

            for Jt in range(nJ):
                m = Jt - 4 * It
                c0 = 128 * m if m > 0 else 0
                idx = 4 * It - Jt + 3
                for r in range(2):
                    sbank = (sbank + 1) % 4
                    pb, pbr = bk[sbank], bk_r[sbank]
                    pti = (pti + 1) % 4
                    pt, ptr = Pt[pti], Pt_r[pti]
                    P.call("pe", "matmul", pb[:, c0:512], Ka[r][0:67, Jt * 128:(Jt + 1) * 128],
                           Qa[r][0:67, It * 512 + c0:(It + 1) * 512], start=True, stop=True,
                           reads=(Ka_r[r][Jt // 4], Kg_r[r], Qa_r[r][It], Qg_r[r]), writes=(pbr,))
                    P.act(pt[:, c0:512], pb[:, c0:512], AF.Exp, (pbr,), (ptr,), bias=bias_s[:, hh, idx:idx + 1], extra_reads=(bias_r,))
                    if m >= 0:
                        P.call("dve", "tensor_tensor", pt[:, c0:c0 + 128], pt[:, c0:c0 + 128], tri_s, ALU.mult,
                               reads=(ptr, tri_r), writes=(ptr,))
                    pend.append((Jt, r, c0, pt, ptr))
                    if len(pend) > LA:
                        consume(pend.pop(0))
            while pend:
                consume(pend.pop(0))
            for r in range(2):
                P.call("dve", "reciprocal", rec[r], bk[6 + r][:], reads=(bk_r[6 + r],), writes=(rec_r[r],))
                P.call("dve", "tensor_tensor", on[r], bk[4 + r][:], rec[r], ALU.mult, reads=(bk_r[4 + r], rec_r[r]), writes=(on_r[r],))
            P.call("dve", "scalar_tensor_tensor", oo, on[1], nlam[:, 0:1], on[0], ALU.mult, ALU.add,
                   reads=(on_r[0], on_r[1], nlam_r), writes=(oo_r,))
            P.act(sq, oo, AF.Square, (oo_r,), (sq_r,))
            sbank = (sbank + 1) % 4
            pb, pbr = bk[sbank], bk_r[sbank]
            P.call("pe", "matmul", pb[:], ones[:], sq, start=True, stop=True, reads=(ones_r, sq_r), writes=(pbr,))
            P.act(rs, pb[:], AF.Sqrt, (pbr,), (rs_r,), bias=P.epsc, scale=1.0 / 128)
            P.call("dve", "reciprocal", rs, rs, reads=(rs_r,), writes=(rs_r,))
            ob, ob_r = ofin[It % 2], ofin_r[It % 2]
            P.call("dve", "scalar_tensor_tensor", ob, oo, hn_s[:, 0:1], rs, ALU.mult, ALU.mult,
                   reads=(oo_r, hn_r, rs_r), writes=(ob_r,))
            nq = TL // 512
            P.dma("sync", o_loc[It // nq][hh * 128:(hh + 1) * 128, (It % nq) * 512:(It % nq + 1) * 512], ob, (ob_r,),
                  (o_loc_r[It // nq][It % 2],), key=f"st_o{name}_{It % 2}")
            if hh == 1 and It % nq == nq - 1:
                P.collective("AllGather", o_loc[It // nq], o_all[It // nq], tuple(o_loc_r[It // nq]), (o_all_r[It // nq],),
                             f"ago{name}_{It // nq}")


GELU_C = 0.044715
GELU_S = 1.5957691216057308


def gelu_tanh(P, out, x, x_r, out_r, t1, t1_r):
    P.act(t1, x, AF.Square, (x_r,), (t1_r,))
    P.call("dve", "tensor_scalar", t1, t1, GELU_C, 1.0, ALU.mult, ALU.add, reads=(t1_r,), writes=(t1_r,))
    P.call("dve", "tensor_tensor", t1, t1, x, ALU.mult, reads=(t1_r, x_r), writes=(t1_r,))
    P.act(t1, t1, AF.Sigmoid, (t1_r,), (t1_r,), scale=GELU_S)
    P.call("pool", "tensor_tensor", out, x, t1, ALU.mult, reads=(x_r, t1_r), writes=(out_r,))


def emit_sgu(P, cx, keep, h_loc, h_loc_r, W):
    P.phase(keep)
    wv_ = W["w_in"].rearrange("(kc p) n -> p kc n", p=128)
    for j in range(4):
        P.dma("pool", cx.wbuf[j], wv_[:, :, j * 512:(j + 1) * 512], (), (cx.wbuf_r[j],))
    bu_t = P.tmp([128, KC], F32); bu_r = Res("bu")
    P.dma("sync", bu_t, W["bu"], (), (bu_r,), key="sguc", batch=True)
    bv_t = P.tmp([1, D], BF16); bv_r = Res("bv")
    P.dma("pool", bv_t, W["bv"], (), (bv_r,), key="sguw", batch=True)
    bs_t = P.tmp([1, D], BF16); bs_r = Res("bs")
    P.dma("pool", bs_t, W["bs"], (), (bs_r,), key="sguw", batch=True)
    vnb_t = P.tmp([128, D], F32); vnb_r = Res("vnb")
    P.dma("sync", vnb_t, W["vnb"], (), (vnb_r,), key="sguc", batch=True)
    tri_t = P.tmp([128, 128], BF16); tri_r = Res("tri")
    P.dma("pool", tri_t, W["tri"], (), (tri_r,), key="sguw", batch=True)
    ws_t = P.tmp([128, 8, 128], BF16); ws_r = Res("ws")
    P.dma("pool", ws_t, W["wsT"].rearrange("g s t -> s g t"), (), (ws_r,), key="sguw", batch=True)
    for g in range(8):
        P.call("dve", "tensor_tensor", ws_t[:, g, :], ws_t[:, g, :], tri_t, ALU.mult, reads=(ws_r, tri_r), writes=(ws_r,))
    hb = [P.tmp([128, KC, 512], BF16) for i in range(2)]
    hb_r = [Res(f"shb{i}") for i in range(2)]
    hvl = [h_loc[tt].rearrange("(kc p) t -> p kc t", p=128) for tt in range(NTT)]
    vn = [P.tmp([128, D], BF16) for i in range(4)]
    vn_r = [Res(f"vn{i}") for i in range(4)]
    xv = [P.tmp([128, 512], F32) for i in range(2)]
    xv_r = [Res(f"xv{i}") for i in range(2)]
    t1 = [P.tmp([128, 512], F32) for i in range(2)]
    t1_r = [Res(f"t1{i}") for i in range(2)]
    gv = [P.tmp([128, D], F32) for i in range(2)]
    gv_r = [Res(f"gv{i}") for i in range(2)]
    sqv = P.tmp([128, D], F32); sqv_r = Res("sqv")
    ssv = [P.tmp([128, 2], F32) for i in range(2)]
    ssv_r = [Res(f"ssv{i}") for i in range(2)]
    ug = [P.tmp([128, 512], F32) for i in range(2)]
    ug_r = [Res(f"ug{i}") for i in range(2)]
    bank = 0
    xi = 0
    P.dma("sync", hb[0], hvl[0], (h_loc_r[0],), (hb_r[0],))
    for tt in range(NTT):
        if tt + 1 < NTT:
            P.dma("sync", hb[(tt + 1) % 2], hvl[tt + 1], (h_loc_r[tt + 1],), (hb_r[(tt + 1) % 2],))
        h, h_r = hb[tt % 2], hb_r[tt % 2]
        ts = slice(tt * 512, (tt + 1) * 512)
        for cb in range(4):
            tok = slice(cb * 128, (cb + 1) * 128)
            gi = cb % 2
            for half in range(2):
                bank = (bank + 1) % 4
                pb, pbr = cx.psb[bank], cx.psb_r[bank]
                calls = [("matmul", (pb[:], h[:, kc, tok], cx.wbuf[2 + half][:, kc, :]), dict(start=(kc == 0), stop=False))
                         for kc in range(KC)]
                calls.append(("matmul", (pb[:], cx.ones[0:1, :], bv_t[0:1, half * 512:(half + 1) * 512]), dict(start=False, stop=True)))
                P.group("pe", calls, (h_r, cx.wbuf_r[2 + half], cx.ones_r, bv_r), (pbr,))
                xi ^= 1
                P.act(xv[xi], pb[:], AF.Identity, (pbr,), (xv_r[xi],))
                gelu_tanh(P, gv[gi][:, half * 512:(half + 1) * 512], xv[xi], xv_r[xi], gv_r[gi], t1[xi], t1_r[xi])
            P.act(sqv, gv[gi], AF.Square, (gv_r[gi],), (sqv_r,))
            P.call("dve", "reduce_sum", ssv[gi][:, 0:1], sqv, AX.X, reads=(sqv_r,), writes=(ssv_r[gi],))
            P.act(ssv[gi][:, 1:2], ssv[gi][:, 0:1], AF.Sqrt, (ssv_r[gi],), (ssv_r[gi],), bias=P.epsc, scale=1.0 / D)
            P.call("dve", "reciprocal", ssv[gi][:, 0:1], ssv[gi][:, 1:2], reads=(ssv_r[gi],), writes=(ssv_r[gi],))
            P.call("dve", "scalar_tensor_tensor", vn[cb], gv[gi], ssv[gi][:, 0:1], vnb_t, ALU.mult, ALU.mult,
                   reads=(gv_r[gi], ssv_r[gi], vnb_r), writes=(vn_r[cb],))
        for g in range(8):
            bank = (bank + 1) % 4
            pb, pbr = cx.psb[bank], cx.psb_r[bank]
            P.group("pe", [("matmul", (pb[:], cx.wbuf[g // 4][:, kc, (g % 4) * 128:(g % 4 + 1) * 128], h[:, kc, :]),
                            dict(start=(kc == 0), stop=(kc == KC - 1))) for kc in range(KC)], (h_r, cx.wbuf_r[g // 4]), (pbr,))
            xi ^= 1
            P.act(xv[xi], pb[:], AF.Identity, (pbr,), (xv_r[xi],), bias=bu_t[:, g:g + 1], extra_reads=(bu_r,))
            gelu_tanh(P, ug[xi], xv[xi], xv_r[xi], ug_r[xi], t1[xi], t1_r[xi])
            sb_ = 4 + g % 3
            sp, spr = cx.psb[sb_], cx.psb_r[sb_]
            calls = []
            for cb in range(4):
                calls.append(("matmul", (sp[:, cb * 128:(cb + 1) * 128], vn[cb][:, g * 128:(g + 1) * 128], ws_t[:, g, :]),
                              dict(start=True, stop=False)))
                calls.append(("matmul", (sp[:, cb * 128:(cb + 1) * 128], cx.ones[0:1, :], bs_t[0:1, g * 128:(g + 1) * 128]),
                              dict(start=False, stop=True)))
            P.group("pe", calls, tuple(vn_r) + (ws_r, cx.ones_r, bs_r), (spr,))
            P.call("dve", "tensor_tensor", cx.bufA[:, g, ts], ug[xi], sp[:], ALU.mult, reads=(ug_r[xi], spr), writes=(cx.bufA_r[g][tt],))


KINDS = ("gla", "diff", "sgu", "gla")


def fused_input_specs():
    sp = {"x": ([TL, D], F32), "ident": ([128, 128], F32), "tinc": ([64, 64], F32), "tupp": ([64, 64], F32),
          "tri": ([128, 128], F32), "gfin": ([128, KC], F32)}
    for L, kind in enumerate(KINDS):
        p = "l%d_" % L
        sp[p + "g1"] = ([128, KC], F32)
        sp[p + "g2"] = ([128, KC], F32)
        sp[p + "w_out"] = ([D, D], F32)
        sp[p + "w1"] = ([D, DFF], F32)
        sp[p + "w2"] = ([DFF, D], F32)
        if kind == "gla":
            sp[p + "wq"] = ([D, 128], F32); sp[p + "wk"] = ([D, 128], F32)
            sp[p + "wv"] = ([D, 256], F32); sp[p + "wg"] = ([D, 256], F32)
            sp[p + "gw1"] = ([D, 16], F32); sp[p + "gw2b"] = ([17, 128], F32); sp[p + "hnb"] = ([128, 256], F32)
        elif kind == "diff":
            sp[p + "wq"] = ([D, 256], F32); sp[p + "wk"] = ([D, 256], F32); sp[p + "wv"] = ([D, 256], F32)
            sp[p + "qaug"] = ([2, 3, T], F32); sp[p + "kaug"] = ([2, 3, T], F32); sp[p + "biasT"] = ([2, 128, 67], F32)
            sp[p + "lamp"] = ([1, 256], F32); sp[p + "hnc"] = ([128, 1], F32)
        else:
            sp[p + "w_in"] = ([D, 2 * D], F32); sp[p + "bu"] = ([128, KC], F32); sp[p + "bv"] = ([1, D], F32)
            sp[p + "vnb"] = ([128, D], F32); sp[p + "wsT"] = ([8, 128, 128], F32); sp[p + "bs"] = ([1, D], F32)
    return sp


def build_fused(nc):
    I = {k: nc.dram_tensor(k, shp, dt, kind="ExternalInput").ap() for k, (shp, dt) in fused_input_specs().items()}
    y_o = nc.dram_tensor("y_o", [TL, D], F32, kind="ExternalOutput").ap()
    P = Prog(nc)
    cx = Ctx(P, nc)
    emit_load_x(P, cx, I["x"], I["ident"])
    P.phase()
    keep = cx.trunk_base()
    for L, kind in enumerate(KINDS[:NL]):
        p = "l%d_" % L
        nm = "L%d" % L
        h_loc = nc.dram_tensor("h_loc" + nm, [NTT, D, 512], BF16).ap()
        h_loc_r = [Res(f"h_loc{nm}_{t}") for t in range(NTT)]
        if kind == "sgu":
            emit_norm_store(P, cx, I[p + "g1"], h_loc, h_loc_r, "g1" + nm)
            W = {k: I[p + k] for k in ("w_in", "bu", "bv", "vnb", "wsT", "bs")}
            W["tri"] = I["tri"]
            emit_sgu(P, cx, keep, h_loc, h_loc_r, W)
        else:
            h_all = nc.dram_tensor("h_all" + nm, [NTT, 4 * D, 512], BF16).ap()
            h_all_r = [Res(f"h_all{nm}_{t}") for t in range(NTT)]
            if kind == "gla":
                W = {k: I[p + k] for k in ("wq", "wk", "wv", "wg", "gw1", "gw2b", "hnb")}
                W.update(tinc=I["tinc"], tupp=I["tupp"], ident=I["ident"])
                pre = preload_gla(P, W)
            else:
                W = {k: I[p + k] for k in ("wq", "wk", "wv", "qaug", "kaug", "biasT", "lamp", "hnc")}
                W["tri"] = I["tri"]
                pre = preload_diff(P, W)
            emit_norm_store(P, cx, I[p + "g1"], h_loc, h_loc_r, "g1" + nm, h_all, h_all_r)
            o_loc = nc.dram_tensor("o_loc" + nm, [4, 256, TL], BF16).ap()
            o_loc_r = [[Res(f"o_loc{nm}_{j}_{p}") for p in range(2)] for j in range(4)]
            o_all = nc.dram_tensor("o_all" + nm, [4, D, TL], BF16).ap()
            o_all_r = [Res(f"o_all{nm}_{j}") for j in range(4)]
            if kind == "gla":
                emit_gla(P, cx, h_all, h_all_r, o_loc, o_loc_r, o_all, o_all_r, pre, nm)
            else:
                emit_diff(P, cx, h_all, h_all_r, o_loc, o_loc_r, o_all, o_all_r, W, pre, nm)
            P.release_top()
            P.phase()
            keep = cx.trunk_base()
            for kc in range(KC):
                def src(e, q, dst=cx.bufA[:, kc, :], srcv=o_all[:, kc * 128:(kc + 1) * 128, :]):
                    return dst, srcv[bass.ds(q, 1), :, :].rearrange("o p t -> p (o t)")
                P.dma_fn("sync", src, tuple(o_all_r), tuple(cx.bufA_r[kc]), key=f"ldo_{kc}")
        emit_wout_mlp(P, cx, keep, I[p + "w_out"], I[p + "g2"], I[p + "w1"], I[p + "w2"], nm)
        P.phase(keep)
        cx.gvi = 0
    emit_final(P, cx, keep, I["gfin"], y_o, I["ident"])
    P.emit()


def _pl(v):
    return np.ascontiguousarray(np.asarray(v, np.float32).reshape(-1, 128).T)


def fused_inputs(inp):
    s = np.arange(64)[:, None]
    c = np.arange(64)[None, :]
    tok = np.arange(T)
    a = tok % 512
    a_lo = (a % 256).astype(np.float32)
    a_hi = (a - a % 256).astype(np.float32)
    cc = (tok % 128).astype(np.float32)
    shared = {"ident": np.eye(128, dtype=np.float32), "tinc": (s <= c).astype(np.float32), "tupp": (s > c).astype(np.float32),
              "tri": (np.arange(128)[:, None] <= np.arange(128)[None, :]).astype(np.float32),
              "gfin": _pl(inp["final_norm"])}
    x = inp["x"].reshape(NCORES, TL, D)
    maps = []
    for cidx in range(NCORES):
        g = cidx % 4
        m = dict(shared)
        m["x"] = np.ascontiguousarray(x[cidx])
        for L, kind in enumerate(KINDS):
            p = "l%d_" % L
            m[p + "g1"] = _pl(inp[p + "norm1"])
            m[p + "g2"] = _pl(inp[p + "norm2"])
            m[p + "w_out"] = inp[p + "w_out"]
            m[p + "w1"] = inp[p + "mlp_w1"]
            m[p + "w2"] = inp[p + "mlp_w2"]
            w_in = inp[p + "w_in"]
            if kind == "gla":
                m[p + "wq"] = np.ascontiguousarray(w_in[:, g * 128:(g + 1) * 128])
                m[p + "wk"] = np.ascontiguousarray(w_in[:, 512 + g * 128:512 + (g + 1) * 128])
                m[p + "wv"] = np.ascontiguousarray(w_in[:, 1024 + g * 256:1024 + (g + 1) * 256])
                m[p + "wg"] = np.ascontiguousarray(w_in[:, 2048 + g * 256:2048 + (g + 1) * 256])
                m[p + "gw1"] = inp[p + "gate_w1"]
                m[p + "gw2b"] = np.ascontiguousarray(np.concatenate(
                    [inp[p + "gate_w2"][:, g * 128:(g + 1) * 128], inp[p + "gate_b"][None, g * 128:(g + 1) * 128]], axis=0))
                m[p + "hnb"] = np.ascontiguousarray(np.broadcast_to(inp[p + "head_norm"][None, :], (128, 256)))
            elif kind == "diff":
                m[p + "wq"] = np.ascontiguousarray(w_in[:, g * 256:(g + 1) * 256])
                m[p + "wk"] = np.ascontiguousarray(w_in[:, 1024 + g * 256:1024 + (g + 1) * 256])
                m[p + "wv"] = np.ascontiguousarray(w_in[:, 2048 + g * 256:2048 + (g + 1) * 256])
                qa = np.zeros((2, 3, T), np.float32)
                ka = np.zeros((2, 3, T), np.float32)
                bt = np.zeros((2, 128, 67), np.float32)
                for hh in range(2):
                    slope = 2.0 ** (-(2 * g + hh + 1))
                    qa[hh, 0] = -slope * a_lo
                    qa[hh, 1] = -slope * a_hi
                    qa[hh, 2] = 1.0
                    ka[hh, 0] = 1.0
                    ka[hh, 1] = 1.0
                    ka[hh, 2] = slope * cc
                    bt[hh] = (-slope * 128.0 * (np.arange(67) - 3))[None, :]
                m[p + "qaug"] = qa
                m[p + "kaug"] = ka
                m[p + "biasT"] = bt
                m[p + "lamp"] = np.ascontiguousarray(np.concatenate(
                    [inp[p + "lambda_q1"], inp[p + "lambda_k1"], inp[p + "lambda_q2"], inp[p + "lambda_k2"]])[None, :].astype(np.float32))
                m[p + "hnc"] = np.ascontiguousarray(inp[p + "head_norm"][:, None])
            else:
                b_in = inp[p + "b_in"]
                m[p + "w_in"] = w_in
                m[p + "bu"] = np.ascontiguousarray(b_in[:D].reshape(KC, 128).T)
                m[p + "bv"] = np.ascontiguousarray(b_in[None, D:])
                m[p + "vnb"] = np.ascontiguousarray(np.broadcast_to(inp[p + "v_norm"][None, :], (128, D)))
                m[p + "wsT"] = np.ascontiguousarray(inp[p + "w_s"].transpose(0, 2, 1))
                m[p + "bs"] = np.ascontiguousarray(inp[p + "b_s"].reshape(1, D))
        maps.append(m)
    return maps


_NC = {}


def kernel(**inp):
    inp = {k: np.asarray(v) for k, v in inp.items()}
    if (T, NL) not in _NC:
        nc = bass.Bass("TRN2", target_bir_lowering=False)
        build_fused(nc)
        _NC[(T, NL)] = nc
    res = run_bass_kernel_spmd(_NC[(T, NL)], fused_inputs(inp), core_ids=list(range(NCORES))).results
    y = np.stack([r["y_o"] for r in res], axis=0)
    return np.ascontiguousarray(y.reshape(2, T, D).astype(np.float32))
```

```python
import contextlib
import numpy as np
import ml_dtypes
import concourse.bass as bass
import concourse.mybir as mybir
from concourse.bass_utils import run_bass_kernel_spmd

F32 = mybir.dt.float32
BF16 = mybir.dt.bfloat16
AF = mybir.ActivationFunctionType
ALU = mybir.AluOpType
AX = mybir.AxisListType

NCORES = 8
D = 1024
KC = 8
DFF = 4096
EPS = 1e-6
T = 8192
TL = T // 4
NTT = TL // 512
GROUPS = [[0, 1, 2, 3], [4, 5, 6, 7]]


NL = 4
DEBUG_OUT = False


def configure(t, nl=4):
    global T, TL, NTT, NL
    NL = nl
    T = t
    TL = T // 4
    NTT = TL // 512


class Res:
    __slots__ = ("name", "w", "r")

    def __init__(self, name):
        self.name = name
        self.w = None
        self.r = []


class Prog:
    COMPUTE = ("act", "dve", "pool", "pe")

    def __init__(self, nc):
        self.nc = nc
        self.stack = contextlib.ExitStack()
        self.streams = {k: [] for k in ("sync", "act", "dve", "pool", "pe")}
        self.cnt = {k: 0 for k in self.COMPUTE}
        self.esem = {k: nc.alloc_semaphore(name="s_" + k) for k in self.COMPUTE}
        self.dsem = {}
        self.dcnt = {}
        self.seen = {k: {} for k in self.streams}
        self.nbuf = 0
        self.q4 = {}
        self.batch = {}
        self.cst = self.sb([128, 4], F32, "cst")
        self.cst_r = Res("cst")
        def init(e):
            e.memset(self.cst[:, 0:1], 0.0)
            e.memset(self.cst[:, 1:2], float(EPS))
            return e.memset(self.cst[:, 2:3], 1.0)
        self.op("pool", init, (), (self.cst_r,))
        self.zero = self.cst[:, 0:1]
        self.epsc = self.cst[:, 1:2]
        self.onec = self.cst[:, 2:3]

    def sb(self, shape, dtype, name=None):
        self.nbuf += 1
        return self.stack.enter_context(self.nc.sbuf_tensor("S_" + (name or f"sb{self.nbuf}"), list(shape), dtype))

    ARENA_BYTES = 136 * 1024

    def phase(self, keep=0):
        if not hasattr(self, "arena_t"):
            self.arena_t = self.sb([128, self.ARENA_BYTES // 4], F32, "arena")
            self.arena_bf = self.arena_t.bitcast(BF16)
        self.barrier()
        self.arena_off = keep
        self.batch = {}

    def tmp(self, shape, dtype, top=False):
        esz = 2 if dtype == BF16 else 4
        n = int(np.prod(shape[1:]))
        nbytes = (n * esz + 31) // 32 * 32
        top_off = getattr(self, "top_off", self.ARENA_BYTES)
        if top:
            top_off -= nbytes
            self.top_off = top_off
            start = top_off
        else:
            start = self.arena_off
            self.arena_off += nbytes
        assert self.arena_off <= top_off, ("arena overflow", self.arena_off, top_off, nbytes)
        base = self.arena_bf if dtype != F32 else self.arena_t
        o = start // esz
        ap = base[0:shape[0], o:o + n]
        if len(shape) == 3:
            ap = ap.rearrange("p (a b) -> p a b", a=shape[1])
        return ap

    def release_top(self):
        self.top_off = self.ARENA_BYTES

    def barrier(self):
        toks = [("e", k, self.cnt[k]) for k in self.COMPUTE if self.cnt[k] > 0]
        toks += [("d", k, v) for k, v in self.dcnt.items() if v > 0 and not k.startswith("ag")]
        for stream in self.streams:
            waits = []
            for kind, key, val in toks:
                if self.seen[stream].get((kind, key), 0) >= val:
                    continue
                self.seen[stream][(kind, key)] = val
                waits.append(self._sem_of((kind, key, val)))
            self.streams[stream].append((waits, None, None, 0))

    def ps(self, shape, dtype, name=None):
        self.nbuf += 1
        return self.stack.enter_context(self.nc.psum_tensor("P_" + (name or f"ps{self.nbuf}"), list(shape), dtype))

    def _sem_of(self, tok):
        kind, key, val = tok
        return (self.esem[key] if kind == "e" else self.dsem[key]), val

    def _waits(self, stream, reads, writes):
        need = {}
        def add(tok, raw):
            if tok is None:
                return
            kind, key, val = tok
            if kind == "e" and key == stream and not raw:
                return
            k = (kind, key)
            if val > need.get(k, 0):
                need[k] = val
        for r in reads:
            add(r.w, True)
        for w in writes:
            add(w.w, False)
            for t in w.r:
                add(t, False)
        out = []
        for (kind, key), val in need.items():
            if self.seen[stream].get((kind, key), 0) >= val:
                continue
            self.seen[stream][(kind, key)] = val
            out.append(self._sem_of((kind, key, val)))
        return out

    def _commit(self, tok, reads, writes):
        for r in reads:
            r.r.append(tok)
        for w in writes:
            w.w = tok
            w.r = []

    def op(self, eng, fn, reads=(), writes=()):
        waits = self._waits(eng, reads, writes)
        self.cnt[eng] += 1
        tok = ("e", eng, self.cnt[eng])
        self.streams[eng].append((waits, fn, self.esem[eng], 1))
        self._commit(tok, reads, writes)
        return tok

    def call(self, eng, meth, *args, reads=(), writes=(), **kw):
        return self.op(eng, lambda e: getattr(e, meth)(*args, **kw), reads, writes)

    def group(self, eng, calls, reads=(), writes=()):
        calls = list(calls)
        def run(e):
            ins = None
            for meth, args, kw in calls:
                ins = getattr(e, meth)(*args, **kw)
            return ins
        return self.op(eng, run, reads, writes)

    def act(self, out, in_, func, reads, writes, bias=None, scale=1.0, accum=None, extra_reads=()):
        b = self.zero if bias is None else bias
        if b.shape[0] != out.shape[0]:
            b = b[0:out.shape[0], :]
        kw = {} if accum is None else {"accum_out": accum}
        return self.op("act", lambda e: e.activation(out, in_, func, bias=b, scale=scale, **kw),
                       tuple(reads) + (self.cst_r,) + tuple(extra_reads), writes)

    def dma(self, queue, out, in_, reads=(), writes=(), key=None, batch=False):
        key = key or ("dma_" + writes[0].name)
        if key not in self.dsem:
            self.dsem[key] = self.nc.alloc_semaphore(name="d_" + key)
            self.dcnt[key] = 0
        waits = self._waits(queue, reads, writes)
        self.dcnt[key] += 16
        tok = ["d", key, self.dcnt[key]]
        if batch:
            for t in self.batch.setdefault(key, []):
                t[2] = self.dcnt[key]
            self.batch[key].append(tok)
        self.streams[queue].append((waits, lambda e: e.dma_start(out=out, in_=in_), self.dsem[key], 16))
        self._commit(tok, reads, writes)
        return tok

    def dma_fn(self, queue, fn, reads=(), writes=(), key=None, batch=False):
        key = key or ("dma_" + writes[0].name)
        if key not in self.dsem:
            self.dsem[key] = self.nc.alloc_semaphore(name="d_" + key)
            self.dcnt[key] = 0
        waits = self._waits(queue, reads, writes)
        self.dcnt[key] += 16
        tok = ["d", key, self.dcnt[key]]
        if batch:
            for t in self.batch.setdefault(key, []):
                t[2] = self.dcnt[key]
            self.batch[key].append(tok)
        def run(e):
            if queue not in self.q4:
                self.q4[queue] = e.partition_id() % 4
            o, i = fn(e, self.q4[queue])
            return e.dma_start(out=o, in_=i)
        self.streams[queue].append((waits, run, self.dsem[key], 16))
        self._commit(tok, reads, writes)
        return tok

    def collective(self, kind, src, dst, reads, writes, key):
        assert key not in self.dsem
        self.dsem[key] = self.nc.alloc_semaphore(name="c_" + key)
        self.dcnt[key] = 1
        waits = self._waits("pool", reads, writes)
        tok = ["d", key, 1]
        self.streams["pool"].append((waits, lambda e: e.collective_compute(
            kind, ALU.bypass, replica_groups=GROUPS, ins=[src], outs=[dst]), self.dsem[key], 1))
        self._commit(tok, reads, writes)
        return tok

    def wait_all(self, stream, ress):
        waits = self._waits(stream, ress, ())
        self.streams[stream].append((waits, None, None, 0))

    def emit(self):
        nc = self.nc

        def replay(name):
            def run(e):
                for waits, fn, sem, inc in self.streams[name]:
                    for s, v in waits:
                        e.wait_ge(s, v)
                    if fn is not None:
                        ins = fn(e)
                        ins.then_inc(sem, inc)
            return run

        with nc.Block() as block:
            block.sync(replay("sync"))
            block.scalar(replay("act"))
            block.vector(replay("dve"))
            block.gpsimd(replay("pool"))
            block.tensor(replay("pe"))
        self.stack.close()


class Ctx:
    def __init__(self, P, nc):
        self.P = P
        self.nc = nc
        self.xT = P.sb([128, KC, TL], F32, "xT")
        self.xT_r = [[Res(f"xT_{k}_{t}") for t in range(NTT)] for k in range(KC)]
        self.ones = P.sb([128, 128], BF16, "ones")
        self.ones_r = Res("ones")
        P.call("pool", "memset", self.ones[:], 1.0, writes=(self.ones_r,))
        self.psb = [P.ps([128, 512], F32, f"bank{i}") for i in range(8)]
        self.psb_r = [Res(f"bank{i}") for i in range(8)]
        self.sqi = 0
        self.rsi = 0

    def trunk_base(self):
        P = self.P
        self.bufA = P.tmp([128, KC, TL], BF16)
        self.bufA_r = [[Res(f"bufA_{k}_{t}") for t in range(NTT)] for k in range(KC)]
        self.wbuf = [P.tmp([128, KC, 512], BF16) for i in range(4)]
        self.wbuf_r = [Res(f"wbuf{i}") for i in range(4)]
        self.sq = [P.tmp([128, 512], BF16) for i in range(3)]
        self.sq_r = [Res(f"sq{i}") for i in range(3)]
        self.rstd = [P.tmp([128, 512], F32) for i in range(2)]
        self.rstd_r = [Res(f"rstd{i}") for i in range(2)]
        self.gv = [P.tmp([128, KC], F32) for i in range(3)]
        self.gvi = 0
        return P.arena_off

    def load_vec(self, dram_ap, name):
        P = self.P
        t = self.gv[self.gvi]
        self.gvi += 1
        r = Res(name)
        P.dma("sync", t, dram_ap, (), (r,))
        return t, r

    def rmsnorm(self, g, g_r, dst, dst_r, bank, tts=None, tcol0=None):
        P = self.P
        for tt in (range(NTT) if tts is None else tts):
            ts = slice(tt * 512, (tt + 1) * 512)
            ds_ = ts if tcol0 is None else slice(0, 512)
            pb, pbr = self.psb[bank], self.psb_r[bank]
            for kc in range(KC):
                i = self.sqi = (self.sqi + 1) % 3
                sq, sqr = self.sq[i], self.sq_r[i]
                P.act(sq, self.xT[:, kc, ts], AF.Square, (self.xT_r[kc][tt],), (sqr,))
                P.call("pe", "matmul", pb[:], self.ones[:], sq, start=(kc == 0), stop=(kc == KC - 1),
                       reads=(sqr, self.ones_r), writes=(pbr,))
            j = self.rsi = (self.rsi + 1) % 2
            rs, rsr = self.rstd[j], self.rstd_r[j]
            P.act(rs, pb[:], AF.Sqrt, (pbr,), (rsr,), bias=P.epsc, scale=1.0 / D)
            P.call("dve", "reciprocal", rs, rs, reads=(rsr,), writes=(rsr,))
            for kc in range(KC):
                P.call("dve", "scalar_tensor_tensor", dst[:, kc, ds_], self.xT[:, kc, ts], g[:, kc:kc + 1], rs,
                       ALU.mult, ALU.mult, reads=(self.xT_r[kc][tt], rsr, g_r), writes=(dst_r[kc][tt],))


def emit_load_x(P, cx, x, ident):
    P.phase()
    identf = P.tmp([128, 128], F32)
    identf_r = Res("identf")
    P.dma("sync", identf, ident, (), (identf_r,))
    xin = [P.tmp([128, D], F32) for i in range(4)]
    xin_r = [Res(f"xin{i}") for i in range(4)]
    xv = x.rearrange("(n p) d -> n p d", p=128)
    bi = 0
    for tt in range(NTT):
        for tb in range(4):
            P.dma("sync", xin[tb], xv[tt * 4 + tb], (), (xin_r[tb],))
        for kc in range(KC):
            b = bi = (bi + 1) % 4
            pb, pbr = cx.psb[b], cx.psb_r[b]
            P.group("pe", [("transpose", (pb[:, tb * 128:(tb + 1) * 128], xin[tb][:, kc * 128:(kc + 1) * 128], identf), {})
                           for tb in range(4)], tuple(xin_r) + (identf_r,), (pbr,))
            if kc % 2 == 0:
                P.act(cx.xT[:, kc, tt * 512:(tt + 1) * 512], pb[:], AF.Identity, (pbr,), (cx.xT_r[kc][tt],))
            else:
                P.call("dve", "tensor_copy", cx.xT[:, kc, tt * 512:(tt + 1) * 512], pb[:], reads=(pbr,), writes=(cx.xT_r[kc][tt],))


def emit_norm_store(P, cx, g_dram, h_loc, h_loc_r, name, h_all=None, h_all_r=None):
    g, g_r = cx.load_vec(g_dram, name)
    for tt in range(NTT):
        cx.rmsnorm(g, g_r, cx.bufA, cx.bufA_r, bank=7, tts=[tt])
        dv = h_loc[tt].rearrange("(kc p) t -> p kc t", p=128)
        P.dma("sync", dv, cx.bufA[:, :, tt * 512:(tt + 1) * 512], tuple(cx.bufA_r[kc][tt] for kc in range(KC)), (h_loc_r[tt],),
              key=f"st_h_{tt}")
        if h_all is not None:
            P.collective("AllGather", h_loc[tt], h_all[tt], (h_loc_r[tt],), (h_all_r[tt],), f"agh{name}_{tt}")


def emit_wout_mlp(P, cx, keep, w_out, g2, w1, w2, name):
    P.phase(keep)
    wo = P.tmp([128, KC, D], BF16)
    wo_r = Res("wo")
    P.dma("pool", wo, w_out.rearrange("(kc p) n -> p kc n", p=128), (), (wo_r,))
    bi = 0
    for tt in range(NTT):
        ts = slice(tt * 512, (tt + 1) * 512)
        for n in range(KC):
            b = bi = (bi + 1) % 4
            pb, pbr = cx.psb[b], cx.psb_r[b]
            P.group("pe", [("matmul", (pb[:], wo[:, fc, n * 128:(n + 1) * 128], cx.bufA[:, fc, ts]),
                            dict(start=(fc == 0), stop=(fc == KC - 1))) for fc in range(KC)],
                    (wo_r,) + tuple(cx.bufA_r[fc][tt] for fc in range(KC)), (pbr,))
            P.call("dve", "tensor_tensor", cx.xT[:, n, ts], cx.xT[:, n, ts], pb[:], ALU.add,
                   reads=(pbr, cx.xT_r[n][tt]), writes=(cx.xT_r[n][tt],))
    g2t, g2r = cx.load_vec(g2, "g2" + name)
    cx.rmsnorm(g2t, g2r, cx.bufA, cx.bufA_r, bank=7)
    P.phase(keep)
    FG = 512
    NFG = DFF // FG
    FC = FG // 128
    w1b = [cx.wbuf[0], cx.wbuf[1]]
    w1b_r = [cx.wbuf_r[0], cx.wbuf_r[1]]
    w2b = [cx.wbuf[2 + i].rearrange("p a b -> p (a b)").rearrange("p (c n) -> p c n", c=FC) for i in range(2)]
    w2b_r = [cx.wbuf_r[2], cx.wbuf_r[3]]
    a2 = [P.tmp([128, FC, 512], BF16) for i in range(2)]
    a2_r = [[Res(f"a2_{i}_{c}") for c in range(FC)] for i in range(2)]
    rl = [P.tmp([128, 512], F32) for i in range(3)]
    rl_r = [Res(f"rl{i}") for i in range(3)]
    w1v = w1.rearrange("(kc p) f -> p kc f", p=128)
    w2v = w2.rearrange("(fc p) n -> p fc n", p=128)

    def load_w(fg):
        s = fg % 2
        P.dma("pool", w1b[s], w1v[:, :, fg * FG:(fg + 1) * FG], (), (w1b_r[s],))
        P.dma("pool", w2b[s], w2v[:, fg * FC:(fg + 1) * FC, :], (), (w2b_r[s],))

    load_w(0)
    steps = [(fg, tt) for fg in range(NFG) for tt in range(NTT)]
    st = {"abank": 0, "ybank": 0}

    def stage1(i):
        fg, tt = steps[i]
        s = fg % 2
        ai = i % 2
        ts = slice(tt * 512, (tt + 1) * 512)
        for c in range(FC):
            st["abank"] = (st["abank"] + 1) % 4
            pb, pbr = cx.psb[st["abank"]], cx.psb_r[st["abank"]]
            P.group("pe", [("matmul", (pb[:], w1b[s][:, kc, c * 128:(c + 1) * 128], cx.bufA[:, kc, ts]),
                            dict(start=(kc == 0), stop=(kc == KC - 1))) for kc in range(KC)],
                    (w1b_r[s],) + tuple(cx.bufA_r[kc][tt] for kc in range(KC)), (pbr,))
            ri = (ai * FC + c) % 3
            P.act(rl[ri], pb[:], AF.Relu, (pbr,), (rl_r[ri],))
            P.call("pool", "tensor_tensor", a2[ai][:, c, :], rl[ri], rl[ri], ALU.mult, reads=(rl_r[ri],), writes=(a2_r[ai][c],))

    def stage2(i):
        fg, tt = steps[i]
        s = fg % 2
        ai = i % 2
        ts = slice(tt * 512, (tt + 1) * 512)
        for n in range(KC):
            st["ybank"] = 4 + (st["ybank"] + 1) % 4
            pb, pbr = cx.psb[st["ybank"]], cx.psb_r[st["ybank"]]
            P.group("pe", [("matmul", (pb[:], w2b[s][:, c, n * 128:(n + 1) * 128], a2[ai][:, c, :]),
                            dict(start=(c == 0), stop=(c == FC - 1))) for c in range(FC)],
                    (w2b_r[s],) + tuple(a2_r[ai]), (pbr,))
            P.call("dve", "tensor_tensor", cx.xT[:, n, ts], cx.xT[:, n, ts], pb[:], ALU.add,
                   reads=(pbr, cx.xT_r[n][tt]), writes=(cx.xT_r[n][tt],))

    stage1(0)
    for i in range(len(steps)):
        fg, tt = steps[i]
        if tt == 0 and fg + 1 < NFG:
            load_w(fg + 1)
        if i + 1 < len(steps):
            stage1(i + 1)
        stage2(i)


def emit_final(P, cx, keep, gf, y_o, ident):
    P.phase(keep)
    g3t, g3r = cx.load_vec(gf, "gf")
    identf = P.tmp([128, 128], F32)
    identf_r = Res("identf")
    P.dma("sync", identf, ident, (), (identf_r,))
    yT = P.tmp([128, KC, 512], F32)
    yT_r = [[Res(f"yT_{k}")] * NTT for k in range(KC)]
    yo = [P.tmp([128, D], F32) for i in range(2)]
    yo_r = [Res(f"yo{i}") for i in range(2)]
    yv = y_o.rearrange("(n p) d -> n p d", p=128)
    outr = Res("st_y")
    oi = 0
    bi = 0
    for tt in range(NTT):
        cx.rmsnorm(g3t, g3r, yT, yT_r, bank=7, tts=[tt], tcol0=0)
        for tb in range(4):
            oi ^= 1
            for half in range(2):
                b = bi = (bi + 1) % 4
                pb, pbr = cx.psb[b], cx.psb_r[b]
                P.group("pe", [("transpose", (pb[:, q * 128:(q + 1) * 128], yT[:, half * 4 + q, tb * 128:(tb + 1) * 128], identf), {})
                               for q in range(4)], tuple(yT_r[k][0] for k in range(KC)) + (identf_r,), (pbr,))
                if half == 0:
                    P.act(yo[oi][:, 0:512], pb[:], AF.Identity, (pbr,), (yo_r[oi],))
                else:
                    P.call("dve", "tensor_copy", yo[oi][:, 512:1024], pb[:], reads=(pbr,), writes=(yo_r[oi],))
            P.dma("sync", yv[tt * 4 + tb], yo[oi], (yo_r[oi],), (outr,), key="st_y")
    P.wait_all("sync", (outr,))


GLA_HK = 128
GLA_HV = 256


def preload_gla(P, W):
    L = {}

    def wload(ap, ncol):
        t = P.tmp([128, KC, ncol], BF16, top=True)
        r = Res("w")
        P.dma("pool", t, ap.rearrange("(kc p) n -> p kc n", p=128), (), (r,), key="glaw", batch=True)
        return t, r

    L["wq"] = wload(W["wq"], 128)
    L["wk"] = wload(W["wk"], 128)
    L["wv"] = wload(W["wv"], 256)
    L["wg"] = wload(W["wg"], 256)
    L["gw1"] = wload(W["gw1"], 16)
    gw2_s = P.tmp([17, 128], BF16, top=True); gw2_r = Res("gw2")
    P.dma("pool", gw2_s, W["gw2b"], (), (gw2_r,), key="glaw", batch=True)
    L["gw2"] = (gw2_s, gw2_r)
    idb = P.tmp([128, 128], BF16, top=True); idb_r = Res("idb")
    P.dma("pool", idb, W["ident"], (), (idb_r,), key="glaw", batch=True)
    L["idb"] = (idb, idb_r)
    hn_s = P.tmp([128, 256], F32, top=True); hn_r = Res("hn")
    P.dma("sync", hn_s, W["hnb"], (), (hn_r,), key="glac", batch=True)
    L["hn"] = (hn_s, hn_r)
    ti_s = P.tmp([64, 64], F32, top=True); ti_r = Res("ti")
    P.dma("sync", ti_s, W["tinc"], (), (ti_r,), key="glac", batch=True)
    L["ti"] = (ti_s, ti_r)
    tu_s = P.tmp([64, 64], F32, top=True); tu_r = Res("tu")
    P.dma("sync", tu_s, W["tupp"], (), (tu_r,), key="glac", batch=True)
    L["tu"] = (tu_s, tu_r)
    ti8 = P.tmp([64, 8, 64], F32, top=True); ti8_r = Res("ti8")
    for j in range(8):
        P.dma("sync", ti8[:, j, :], W["tinc"], (), (ti8_r,), key="glac", batch=True)
    L["ti8"] = (ti8, ti8_r)
    return L


def emit_gla(P, cx, h_all, h_all_r, o_loc, o_loc_r, o_all, o_all_r, L, name):
    P.phase()
    wq_s, wq_r = L["wq"]; wk_s, wk_r = L["wk"]; wv_s, wv_r = L["wv"]; wg_s, wg_r = L["wg"]
    gw1_s, gw1_r = L["gw1"]; gw2_s, gw2_r = L["gw2"]; idb, idb_r = L["idb"]
    hn_s, hn_r = L["hn"]; ti_s, ti_r = L["ti"]; tu_s, tu_r = L["tu"]; ti8, ti8_r = L["ti8"]

    hb = [P.tmp([128, KC, 512], BF16) for i in range(2)]
    hb_r = [Res(f"hb{i}") for i in range(2)]
    hvs = [h_all[j].rearrange("(r kc p) t -> r p kc t", r=4, p=128) for j in range(NTT)]

    def h_load(dst, dst_r, tile):
        P.dma("sync", dst, hvs[tile // 4][tile % 4], (h_all_r[tile // 4],), (dst_r,))

    S = P.tmp([128, 256], F32); S_r = Res("S")
    Sb = P.tmp([128, 256], BF16); Sb_r = Res("Sb")
    P.call("pool", "memset", S, 0.0, writes=(S_r,))
    P.call("pool", "memset", Sb, 0.0, writes=(Sb_r,))
    r17 = P.tmp([17, 512], BF16); r17_r = Res("r17")
    P.call("pool", "memset", r17, 1.0, writes=(r17_r,))

    bk, br = cx.psb, cx.psb_r
    bkT = bk[5].bitcast(BF16)[:, :].rearrange("p (a b) -> p a b", a=2)

    ez = P.tmp([64, 8, 128], F32); ez_r = Res("ez")
    lsp = P.tmp([64, 8, 128], F32); lsp_r = Res("lsp")
    E1 = P.tmp([128, 512], F32); E1_r = Res("E1")
    Ei = P.tmp([128, 512], F32); Ei_r = Res("Ei")
    eU = P.tmp([64, 8, 128], F32); eU_r = Res("eU")
    qd = P.tmp([128, 512], BF16); qd_r = Res("qd")
    ki = P.tmp([128, 512], BF16); ki_r = Res("ki")
    ke = P.tmp([64, 8, 128], BF16); ke_r = Res("ke")
    kf = [P.tmp([64, 8, 128], F32) for i in range(2)]; kf_r = [[Res(f"kf{i}_{j}") for j in range(8)] for i in range(2)]
    vb = [P.tmp([64, 8, 256], BF16) for i in range(2)]; vb_r = [[Res(f"vb{i}_{j}") for j in range(8)] for i in range(2)]
    sg = [P.tmp([64, 8, 256], F32) for i in range(2)]; sg_r = [[Res(f"sg{i}_{j}") for j in range(8)] for i in range(2)]
    at = P.tmp([64, 8, 64], BF16); at_r = Res("at")
    osb_ = P.tmp([64, 8, 256], F32); osb_r = [Res(f"os{i}") for i in range(8)]
    sq = [P.tmp([64, 256], F32) for i in range(2)]; sq_r = [Res(f"sq{i}") for i in range(2)]
    ss = P.tmp([64, 8], F32); ss_r = Res("ss")
    rs = P.tmp([64, 8], F32); rs_r = Res("rs")
    tt_ = [P.tmp([64, 256], F32) for i in range(2)]; tt_r = [Res(f"tt{i}") for i in range(2)]
    of = [P.tmp([64, 256], BF16) for i in range(2)]; of_r = [Res(f"of{i}") for i in range(2)]
    ob = [P.tmp([128, 2, 512], BF16) for i in range(2)]; ob_r = [Res(f"ob{i}") for i in range(2)]
    ovs = [o_loc[j].rearrange("(h p) t -> p h t", p=128) for j in range(NTT)]
    nq = 4

    def mm8(dst, lhs_fn, rhs_fn, reads, dst_r):
        P.group("pe", [("matmul", (dst, lhs_fn(kc), rhs_fn(kc)), dict(start=(kc == 0), stop=(kc == KC - 1))) for kc in range(KC)],
                reads, (dst_r,))

    def proj_chunk(tile, j):
        p = tile % 2
        h, h_r = hb[p], hb_r[p]
        cs = slice(j * 64, (j + 1) * 64)
        mm8(bk[6][0:64, 0:128], lambda kc: h[:, kc, cs], lambda kc: wk_s[:, kc, :], (wk_r, h_r), br[6])
        mm8(bk[6][0:64, 128:384], lambda kc: h[:, kc, cs], lambda kc: wv_s[:, kc, :], (wv_r, h_r), br[6])
        P.act(kf[p][:, j, :], bk[6][0:64, 0:128], AF.Identity, (br[6],), (kf_r[p][j],))
        P.act(vb[p][:, j, :], bk[6][0:64, 128:384], AF.Identity, (br[6],), (vb_r[p][j],))
        mm8(bk[7][0:64, 0:256], lambda kc: h[:, kc, cs], lambda kc: wg_s[:, kc, :], (wg_r, h_r), br[7])
        P.act(sg[p][:, j, :], bk[7][0:64, 0:256], AF.Silu, (br[7],), (sg_r[p][j],))

    h_load(hb[0], hb_r[0], 0)
    for j in range(8):
        proj_chunk(0, j)
    for tile in range(T // 512):
        if tile + 1 < T // 512:
            h_load(hb[(tile + 1) % 2], hb_r[(tile + 1) % 2], tile + 1)
        h, h_r = hb[tile % 2], hb_r[tile % 2]
        mm8(bk[0][:], lambda kc: wq_s[:, kc, :], lambda kc: h[:, kc, :], (wq_r, h_r), br[0])
        mm8(bk[1][:], lambda kc: wk_s[:, kc, :], lambda kc: h[:, kc, :], (wk_r, h_r), br[1])
        mm8(bk[2][0:16, :], lambda kc: gw1_s[:, kc, :], lambda kc: h[:, kc, :], (gw1_r, h_r), br[2])
        P.act(r17[0:16, :], bk[2][0:16, :], AF.Identity, (br[2],), (r17_r,))
        for half in range(2):
            b = 3 + half
            P.group("pe", [("matmul", (bk[b][0:64, c * 128:(c + 1) * 128], r17[0:17, (half * 4 + c) * 64:(half * 4 + c + 1) * 64], gw2_s[0:17, :]),
                            dict(start=True, stop=True)) for c in range(4)], (r17_r, gw2_r), (br[b],))
            P.act(ez[:, half * 4:half * 4 + 4, :], bk[b][0:64, :].rearrange("p (a b) -> p a b", a=4), AF.Exp, (br[b],), (ez_r,), scale=-1.0)
        P.act(lsp[:, 0:4, :], ez[:, 0:4, :], AF.Ln, (ez_r,), (lsp_r,), bias=P.onec)
        P.act(lsp[:, 4:8, :], ez[:, 4:8, :], AF.Ln, (ez_r,), (lsp_r,), bias=P.onec)
        P.group("pe", [("matmul", (bk[5][:, c * 64:(c + 1) * 64], lsp[:, c, :], ti_s), dict(start=True, stop=True)) for c in range(8)],
                (lsp_r, ti_r), (br[5],))
        for half in range(2):
            b = 6 + half
            P.group("pe", [("matmul", (bk[b][0:64, c * 128:(c + 1) * 128], tu_s, lsp[:, half * 4 + c, :]), dict(start=True, stop=True))
                           for c in range(4)], (lsp_r, tu_r), (br[b],))
        P.act(E1, bk[5][:], AF.Exp, (br[5],), (E1_r,), scale=-1.0 / 16)
        P.act(Ei, bk[5][:], AF.Exp, (br[5],), (Ei_r,), scale=1.0 / 16)
        for half in range(2):
            b = 6 + half
            P.act(eU[:, half * 4:half * 4 + 4, :], bk[b][0:64, :].rearrange("p (a b) -> p a b", a=4), AF.Exp, (br[b],), (eU_r,), scale=-1.0 / 16)
        P.call("dve", "scalar_tensor_tensor", qd, bk[0][:], float(GLA_HK ** -0.5), E1, ALU.mult, ALU.mult,
               reads=(br[0], E1_r), writes=(qd_r,))
        P.call("dve", "tensor_tensor", ki, bk[1][:], Ei, ALU.mult, reads=(br[1], Ei_r), writes=(ki_r,))
        pp = tile % 2
        P.call("dve", "tensor_tensor", ke, kf[pp], eU, ALU.mult, reads=tuple(kf_r[pp]) + (eU_r,), writes=(ke_r,))
        P.group("pe", [("matmul", (bk[0][0:64, c * 64:(c + 1) * 64], ki[:, c * 64:(c + 1) * 64], qd[:, c * 64:(c + 1) * 64]),
                        dict(start=True, stop=True)) for c in range(8)], (ki_r, qd_r), (br[0],))
        P.call("dve", "tensor_tensor", at, bk[0][0:64, :].rearrange("p (a b) -> p a b", a=8), ti8, ALU.mult,
               reads=(br[0], ti8_r), writes=(at_r,))
        for j in range(8):
            cs = slice(j * 64, (j + 1) * 64)
            bo, bkv = 1 + j % 2, 3 + j % 2
            if tile + 1 < T // 512:
                proj_chunk(tile + 1, j)
            P.call("pe", "matmul", bk[bkv][:, 0:256], ke[:, j, :], vb[pp][:, j, :], start=True, stop=True,
                   reads=(ke_r, vb_r[pp][j]), writes=(br[bkv],))
            P.group("pe", [("matmul", (bk[bo][0:64, 0:256], qd[:, cs], Sb), dict(start=True, stop=False)),
                           ("matmul", (bk[bo][0:64, 0:256], at[:, j, :], vb[pp][:, j, :]), dict(start=False, stop=True))],
                    (qd_r, Sb_r, at_r, vb_r[pp][j]), (br[bo],))
            P.call("dve", "scalar_tensor_tensor", S, S, E1[:, j * 64 + 63:j * 64 + 64], bk[bkv][:, 0:256], ALU.mult, ALU.add,
                   reads=(S_r, E1_r, br[bkv]), writes=(S_r,))
            P.act(Sb, S, AF.Identity, (S_r,), (Sb_r,))
            P.call("dve", "tensor_copy", osb_[:, j, :], bk[bo][0:64, 0:256], reads=(br[bo],), writes=(osb_r[j],))
            P.act(sq[j % 2], osb_[:, j, :], AF.Square, (osb_r[j],), (sq_r[j % 2],))
            P.call("dve", "reduce_sum", ss[:, j:j + 1], sq[j % 2], AX.X, reads=(sq_r[j % 2],), writes=(ss_r,))
        P.act(rs, ss, AF.Sqrt, (ss_r,), (rs_r,), bias=P.epsc, scale=1.0 / GLA_HV)
        P.call("dve", "reciprocal", rs, rs, reads=(rs_r,), writes=(rs_r,))
        for j in range(8):
            cs = slice(j * 64, (j + 1) * 64)
            i = j % 2
            P.call("dve", "scalar_tensor_tensor", tt_[i], osb_[:, j, :], rs[:, j:j + 1], hn_s[0:64, :], ALU.mult, ALU.mult,
                   reads=(osb_r[j], rs_r, hn_r), writes=(tt_r[i],))
            P.call("dve", "tensor_tensor", of[i], tt_[i], sg[pp][:, j, :], ALU.mult, reads=(tt_r[i], sg_r[pp][j]), writes=(of_r[i],))
            P.group("pe", [("transpose", (bkT[:, 0, cs], of[i][:, 0:128], idb[0:64, 0:64]), {}),
                           ("transpose", (bkT[:, 1, cs], of[i][:, 128:256], idb[0:64, 0:64]), {})],
                    (of_r[i], idb_r), (br[5],))
        o_, o_r = ob[tile % 2], ob_r[tile % 2]
        P.act(o_[:, 0, :], bkT[:, 0, :], AF.Identity, (br[5],), (o_r,))
        P.call("dve", "tensor_copy", o_[:, 1, :], bkT[:, 1, :], reads=(br[5],), writes=(o_r,))
        P.dma("sync", ovs[tile // nq][:, :, (tile % nq) * 512:(tile % nq + 1) * 512], o_, (o_r,), (o_loc_r[tile // nq][tile % 2],),
              key=f"st_o{name}_{tile % 2}")
        if tile % nq == nq - 1:
            P.collective("AllGather", o_loc[tile // nq], o_all[tile // nq], tuple(o_loc_r[tile // nq]), (o_all_r[tile // nq],),
                         f"ago{name}_{tile // nq}")


DIFF_LAMBDA_INIT = 0.8 - 0.6 * float(np.exp(-0.3 * 1))


def preload_diff(P, W):
    L = {}

    def wload(ap, ncol):
        t = P.tmp([128, KC, ncol], BF16, top=True)
        r = Res("w")
        P.dma("pool", t, ap.rearrange("(kc p) n -> p kc n", p=128), (), (r,), key="difw", batch=True)
        return t, r

    L["wq"] = wload(W["wq"], 256)
    L["wk"] = wload(W["wk"], 256)
    L["wv"] = wload(W["wv"], 256)
    tri_s = P.tmp([128, 128], BF16, top=True); tri_r = Res("tri")
    P.dma("pool", tri_s, W["tri"], (), (tri_r,), key="difw", batch=True)
    L["tri"] = (tri_s, tri_r)
    return L


def emit_diff(P, cx, h_all, h_all_r, o_loc, o_loc_r, o_all, o_all_r, W, L, name):
    P.phase()
    NT = T // 512
    wq_s, wq_r = L["wq"]; wk_s, wk_r = L["wk"]; wv_s, wv_r = L["wv"]; tri_s, tri_r = L["tri"]
    ones, ones_r = cx.ones, cx.ones_r
    onef = P.tmp([1, 128], F32); onef_r = Res("onef")
    P.call("pool", "memset", onef, 1.0, writes=(onef_r,))
    bias_s = P.tmp([128, 2, 67], F32); bias_r = Res("bias")
    P.dma("sync", bias_s, W["biasT"].rearrange("h p n -> p h n"), (), (bias_r,), key="difc", batch=True)
    hn_s = P.tmp([128, 1], F32); hn_r = Res("hn")
    P.dma("sync", hn_s, W["hnc"], (), (hn_r,), key="difc", batch=True)
    bk, bk_r = cx.psb, cx.psb_r

    lp = P.tmp([1, 256], F32); lp_r = Res("lp")
    P.dma("sync", lp, W["lamp"], (), (lp_r,), key="difc", batch=True)
    P.call("dve", "tensor_scalar", hn_s, hn_s, float(1.0 - DIFF_LAMBDA_INIT), None, ALU.mult, reads=(hn_r,), writes=(hn_r,))
    lw = P.tmp([1, 136], F32); lw_r = Res("lw")
    P.call("dve", "tensor_tensor", lw[:, 0:64], lp[:, 0:64], lp[:, 64:128], ALU.mult, reads=(lp_r,), writes=(lw_r,))
    P.call("dve", "tensor_tensor", lw[:, 64:128], lp[:, 128:192], lp[:, 192:256], ALU.mult, reads=(lp_r, lw_r), writes=(lw_r,))
    P.call("dve", "reduce_sum", lw[:, 128:129], lw[:, 0:64], AX.X, reads=(lw_r,), writes=(lw_r,))
    P.call("dve", "reduce_sum", lw[:, 129:130], lw[:, 64:128], AX.X, reads=(lw_r,), writes=(lw_r,))
    P.act(lw[:, 130:132], lw[:, 128:130], AF.Exp, (lw_r,), (lw_r,))
    P.call("dve", "tensor_tensor", lw[:, 132:133], lw[:, 131:132], lw[:, 130:131], ALU.subtract, reads=(lw_r,), writes=(lw_r,))
    P.call("dve", "tensor_scalar", lw[:, 133:134], lw[:, 132:133], float(-DIFF_LAMBDA_INIT), None, ALU.add, reads=(lw_r,), writes=(lw_r,))
    P.call("pe", "matmul", bk[0][:, 0:1], onef[0:1, :], lw[0:1, 133:134], start=True, stop=True,
           reads=(onef_r, lw_r), writes=(bk_r[0],))
    nlam = P.tmp([128, 1], F32); nlam_r = Res("nlam")
    P.call("dve", "tensor_copy", nlam, bk[0][:, 0:1], reads=(bk_r[0],), writes=(nlam_r,))

    Qa = [P.tmp([67, T], BF16) for r in range(2)]
    Ka = [P.tmp([67, T], BF16) for r in range(2)]
    Qa_r = [[Res(f"Qa{r}_{t}") for t in range(NT)] for r in range(2)]
    Ka_r = [[Res(f"Ka{r}_{t}") for t in range(NT)] for r in range(2)]
    Qg_r = [Res(f"Qg{r}") for r in range(2)]
    Kg_r = [Res(f"Kg{r}") for r in range(2)]
    V = P.tmp([128, T // 128, 128], BF16)
    V_r = [Res(f"V{t}") for t in range(NT)]
    hb = [P.tmp([128, KC, 512], BF16) for i in range(2)]
    hb_r = [Res(f"hb{i}") for i in range(2)]
    hvs = [h_all[j].rearrange("(r kc p) t -> r p kc t", r=4, p=128) for j in range(NTT)]

    def h_load(dst, dst_r, tile):
        P.dma("sync", dst, hvs[tile // 4][tile % 4], (h_all_r[tile // 4],), (dst_r,))

    Pt = [P.tmp([128, 512], BF16) for i in range(4)]
    Pt_r = [Res(f"Pt{i}") for i in range(4)]
    rec = [P.tmp([128, 512], F32) for i in range(2)]
    rec_r = [Res(f"rec{i}") for i in range(2)]
    on = [P.tmp([128, 512], F32) for i in range(2)]
    on_r = [Res(f"on{i}") for i in range(2)]
    oo = P.tmp([128, 512], F32); oo_r = Res("oo")
    sq = P.tmp([128, 512], BF16); sq_r = Res("sq")
    rs = P.tmp([128, 512], F32); rs_r = Res("rs")
    ofin = [P.tmp([128, 512], BF16) for i in range(2)]
    ofin_r = [Res(f"ofin{i}") for i in range(2)]
    hcnt = 0
    sbank = 0
    pti = 0
    for hh in range(2):
        for r in range(2):
            P.dma("pool", Qa[r][64:67, :], W["qaug"][hh], (), (Qg_r[r],), key="difa", batch=True)
            P.dma("pool", Ka[r][64:67, :], W["kaug"][hh], (), (Kg_r[r],), key="difa", batch=True)
        for tile in range(NT):
            s = hcnt % 2
            hcnt += 1
            h_load(hb[s], hb_r[s], tile)
            h, h_r = hb[s], hb_r[s]
            cols = slice(tile * 512, (tile + 1) * 512)
            for r in range(2):
                wc = slice((hh * 2 + r) * 64, (hh * 2 + r + 1) * 64)
                for (w_s, w_r, dst, dst_r, sc) in ((wq_s, wq_r, Qa, Qa_r, 0.125), (wk_s, wk_r, Ka, Ka_r, 1.0)):
                    sbank = (sbank + 1) % 4
                    pb, pbr = bk[sbank], bk_r[sbank]
                    P.group("pe", [("matmul", (pb[0:64, :], w_s[:, kc, wc], h[:, kc, :]), dict(start=(kc == 0), stop=(kc == KC - 1)))
                                   for kc in range(KC)], (w_r, h_r), (pbr,))
                    P.act(dst[r][0:64, cols], pb[0:64, :], AF.Identity, (pbr,), (dst_r[r][tile],), scale=sc)
            sbank = (sbank + 1) % 4
            pb, pbr = bk[sbank], bk_r[sbank]
            for tb in range(4):
                P.group("pe", [("matmul", (pb[:, tb * 128:(tb + 1) * 128], h[:, kc, tb * 128:(tb + 1) * 128], wv_s[:, kc, hh * 128:(hh + 1) * 128]),
                                dict(start=(kc == 0), stop=(kc == KC - 1))) for kc in range(KC)], (wv_r, h_r), (pbr,))
            P.call("dve", "tensor_copy", V[:, tile * 4:(tile + 1) * 4, :], pb[:].rearrange("p (a b) -> p a b", a=4),
                   reads=(pbr,), writes=(V_r[tile],))
        LA = 2
        for It in range(NT):
            nJ = 4 * It + 4
            pend = []

            def consume(item, It=It, nJ=nJ):
                Jt, r, c0, pt, ptr = item
                P.call("pe", "matmul", bk[4 + r][:, c0:512], V[:, Jt, :], pt[:, c0:512], start=(Jt == 0), stop=(Jt == nJ - 1),
                       reads=(V_r[Jt // 4], ptr), writes=(bk_r[4 + r],))
                P.call("pe", "matmul", bk[6 + r][:, c0:512], ones[:], pt[:, c0:512], start=(Jt == 0), stop=(Jt == nJ - 1),
                       reads=(ones_r, ptr), writes=(bk_r[6 + r],))

            for Jt in range(nJ):
                m = Jt - 4 * It
                c0 = 128 * m if m > 0 else 0
                idx = 4 * It - Jt + 3
                for r in range(2):
                    sbank = (sbank + 1) % 4
                    pb, pbr = bk[sbank], bk_r[sbank]
                    pti = (pti + 1) % 4
                    pt, ptr = Pt[pti], Pt_r[pti]
                    P.call("pe", "matmul", pb[:, c0:512], Ka[r][0:67, Jt * 128:(Jt + 1) * 128],
                           Qa[r][0:67, It * 512 + c0:(It + 1) * 512], start=True, stop=True,
                           reads=(Ka_r[r][Jt // 4], Kg_r[r], Qa_r[r][It], Qg_r[r]), writes=(pbr,))
                    P.act(pt[:, c0:512], pb[:, c0:512], AF.Exp, (pbr,), (ptr,), bias=bias_s[:, hh, idx:idx + 1], extra_reads=(bias_r,))
                    if m >= 0:
                        P.call("dve", "tensor_tensor", pt[:, c0:c0 + 128], pt[:, c0:c0 + 128], tri_s, ALU.mult,
                               reads=(ptr, tri_r), writes=(ptr,))
                    pend.append((Jt, r, c0, pt, ptr))
                    if len(pend) > LA:
                        consume(pend.pop(0))
            while pend:
                consume(pend.pop(0))
            for r in range(2):
                P.call("dve", "reciprocal", rec[r], bk[6 + r][:], reads=(bk_r[6 + r],), writes=(rec_r[r],))
                P.call("dve", "tensor_tensor", on[r], bk[4 + r][:], rec[r], ALU.mult, reads=(bk_r[4 + r], rec_r[r]), writes=(on_r[r],))
            P.call("dve", "scalar_tensor_tensor", oo, on[1], nlam[:, 0:1], on[0], ALU.mult, ALU.add,
                   reads=(on_r[0], on_r[1], nlam_r), writes=(oo_r,))
            P.act(sq, oo, AF.Square, (oo_r,), (sq_r,))
            sbank = (sbank + 1) % 4
            pb, pbr = bk[sbank], bk_r[sbank]
            P.call("pe", "matmul", pb[:], ones[:], sq, start=True, stop=True, reads=(ones_r, sq_r), writes=(pbr,))
            P.act(rs, pb[:], AF.Sqrt, (pbr,), (rs_r,), bias=P.epsc, scale=1.0 / 128)
            P.call("dve", "reciprocal", rs, rs, reads=(rs_r,), writes=(rs_r,))
            ob, ob_r = ofin[It % 2], ofin_r[It % 2]
            P.call("dve", "scalar_tensor_tensor", ob, oo, hn_s[:, 0:1], rs, ALU.mult, ALU.mult,
                   reads=(oo_r, hn_r, rs_r), writes=(ob_r,))
            nq = 4
            P.dma("sync", o_loc[It // nq][hh * 128:(hh + 1) * 128, (It % nq) * 512:(It % nq + 1) * 512], ob, (ob_r,),
                  (o_loc_r[It // nq][It % 2],), key=f"st_o{name}_{It % 2}")
            if hh == 1 and It % nq == nq - 1:
                P.collective("AllGather", o_loc[It // nq], o_all[It // nq], tuple(o_loc_r[It // nq]), (o_all_r[It // nq],),
                             f"ago{name}_{It // nq}")


GELU_C = 0.044715
GELU_S = 1.5957691216057308


def gelu_tanh(P, out, x, x_r, out_r, t1, t1_r):
    P.act(t1, x, AF.Square, (x_r,), (t1_r,))
    P.call("dve", "tensor_scalar", t1, t1, GELU_C, 1.0, ALU.mult, ALU.add, reads=(t1_r,), writes=(t1_r,))
    P.call("dve", "tensor_tensor", t1, t1, x, ALU.mult, reads=(t1_r, x_r), writes=(t1_r,))
    P.act(t1, t1, AF.Sigmoid, (t1_r,), (t1_r,), scale=GELU_S)
    P.call("pool", "tensor_tensor", out, x, t1, ALU.mult, reads=(x_r, t1_r), writes=(out_r,))


def emit_sgu(P, cx, keep, h_loc, h_loc_r, W):
    P.phase(keep)
    wv_ = W["w_in"].rearrange("(kc p) n -> p kc n", p=128)
    for j in range(4):
        P.dma("pool", cx.wbuf[j], wv_[:, :, j * 512:(j + 1) * 512], (), (cx.wbuf_r[j],))
    bu_t = P.tmp([128, KC], F32); bu_r = Res("bu")
    P.dma("sync", bu_t, W["bu"], (), (bu_r,), key="sguc", batch=True)
    bv_t = P.tmp([1, D], BF16); bv_r = Res("bv")
    P.dma("pool", bv_t, W["bv"], (), (bv_r,), key="sguw", batch=True)
    bs_t = P.tmp([1, D], BF16); bs_r = Res("bs")
    P.dma("pool", bs_t, W["bs"], (), (bs_r,), key="sguw", batch=True)
    vnb_t = P.tmp([128, D], F32); vnb_r = Res("vnb")
    P.dma("sync", vnb_t, W["vnb"], (), (vnb_r,), key="sguc", batch=True)
    tri_t = P.tmp([128, 128], BF16); tri_r = Res("tri")
    P.dma("pool", tri_t, W["tri"], (), (tri_r,), key="sguw", batch=True)
    ws_t = P.tmp([128, 8, 128], BF16); ws_r = Res("ws")
    P.dma("pool", ws_t, W["wsT"].rearrange("g s t -> s g t"), (), (ws_r,), key="sguw", batch=True)
    for g in range(8):
        P.call("dve", "tensor_tensor", ws_t[:, g, :], ws_t[:, g, :], tri_t, ALU.mult, reads=(ws_r, tri_r), writes=(ws_r,))
    hb = [P.tmp([128, KC, 512], BF16) for i in range(2)]
    hb_r = [Res(f"shb{i}") for i in range(2)]
    hvl = [h_loc[tt].rearrange("(kc p) t -> p kc t", p=128) for tt in range(NTT)]
    vn = [P.tmp([128, D], BF16) for i in range(4)]
    vn_r = [Res(f"vn{i}") for i in range(4)]
    xv = [P.tmp([128, 512], F32) for i in range(2)]
    xv_r = [Res(f"xv{i}") for i in range(2)]
    t1 = [P.tmp([128, 512], F32) for i in range(2)]
    t1_r = [Res(f"t1{i}") for i in range(2)]
    gv = [P.tmp([128, D], F32) for i in range(2)]
    gv_r = [Res(f"gv{i}") for i in range(2)]
    sqv = P.tmp([128, D], F32); sqv_r = Res("sqv")
    ssv = [P.tmp([128, 2], F32) for i in range(2)]
    ssv_r = [Res(f"ssv{i}") for i in range(2)]
    ug = [P.tmp([128, 512], F32) for i in range(2)]
    ug_r = [Res(f"ug{i}") for i in range(2)]
    bank = 0
    xi = 0
    P.dma("sync", hb[0], hvl[0], (h_loc_r[0],), (hb_r[0],))
    for tt in range(NTT):
        if tt + 1 < NTT:
            P.dma("sync", hb[(tt + 1) % 2], hvl[tt + 1], (h_loc_r[tt + 1],), (hb_r[(tt + 1) % 2],))
        h, h_r = hb[tt % 2], hb_r[tt % 2]
        ts = slice(tt * 512, (tt + 1) * 512)
        for cb in range(4):
            tok = slice(cb * 128, (cb + 1) * 128)
            gi = cb % 2
            for half in range(2):
                bank = (bank + 1) % 4
                pb, pbr = cx.psb[bank], cx.psb_r[bank]
                calls = [("matmul", (pb[:], h[:, kc, tok], cx.wbuf[2 + half][:, kc, :]), dict(start=(kc == 0), stop=False))
                         for kc in range(KC)]
                calls.append(("matmul", (pb[:], cx.ones[0:1, :], bv_t[0:1, half * 512:(half + 1) * 512]), dict(start=False, stop=True)))
                P.group("pe", calls, (h_r, cx.wbuf_r[2 + half], cx.ones_r, bv_r), (pbr,))
                xi ^= 1
                P.act(xv[xi], pb[:], AF.Identity, (pbr,), (xv_r[xi],))
                gelu_tanh(P, gv[gi][:, half * 512:(half + 1) * 512], xv[xi], xv_r[xi], gv_r[gi], t1[xi], t1_r[xi])
            P.act(sqv, gv[gi], AF.Square, (gv_r[gi],), (sqv_r,))
            P.call("dve", "reduce_sum", ssv[gi][:, 0:1], sqv, AX.X, reads=(sqv_r,), writes=(ssv_r[gi],))
            P.act(ssv[gi][:, 1:2], ssv[gi][:, 0:1], AF.Sqrt, (ssv_r[gi],), (ssv_r[gi],), bias=P.epsc, scale=1.0 / D)
            P.call("dve", "reciprocal", ssv[gi][:, 0:1], ssv[gi][:, 1:2], reads=(ssv_r[gi],), writes=(ssv_r[gi],))
            P.call("dve", "scalar_tensor_tensor", vn[cb], gv[gi], ssv[gi][:, 0:1], vnb_t, ALU.mult, ALU.mult,
                   reads=(gv_r[gi], ssv_r[gi], vnb_r), writes=(vn_r[cb],))
        for g in range(8):
            bank = (bank + 1) % 4
            pb, pbr = cx.psb[bank], cx.psb_r[bank]
            P.group("pe", [("matmul", (pb[:], cx.wbuf[g // 4][:, kc, (g % 4) * 128:(g % 4 + 1) * 128], h[:, kc, :]),
                            dict(start=(kc == 0), stop=(kc == KC - 1))) for kc in range(KC)], (h_r, cx.wbuf_r[g // 4]), (pbr,))
            xi ^= 1
            P.act(xv[xi], pb[:], AF.Identity, (pbr,), (xv_r[xi],), bias=bu_t[:, g:g + 1], extra_reads=(bu_r,))
            gelu_tanh(P, ug[xi], xv[xi], xv_r[xi], ug_r[xi], t1[xi], t1_r[xi])
            sb_ = 4 + g % 3
            sp, spr = cx.psb[sb_], cx.psb_r[sb_]
            calls = []
            for cb in range(4):
                calls.append(("matmul", (sp[:, cb * 128:(cb + 1) * 128], vn[cb][:, g * 128:(g + 1) * 128], ws_t[:, g, :]),
                              dict(start=True, stop=False)))
                calls.append(("matmul", (sp[:, cb * 128:(cb + 1) * 128], cx.ones[0:1, :], bs_t[0:1, g * 128:(g + 1) * 128]),
                              dict(start=False, stop=True)))
            P.group("pe", calls, tuple(vn_r) + (ws_r, cx.ones_r, bs_r), (spr,))
            P.call("dve", "tensor_tensor", cx.bufA[:, g, ts], ug[xi], sp[:], ALU.mult, reads=(ug_r[xi], spr), writes=(cx.bufA_r[g][tt],))


KINDS = ("gla", "diff", "sgu", "gla")


def fused_input_specs():
    sp = {"x": ([TL, D], F32), "ident": ([128, 128], F32), "tinc": ([64, 64], F32), "tupp": ([64, 64], F32),
          "tri": ([128, 128], F32), "gfin": ([128, KC], F32)}
    for L, kind in enumerate(KINDS):
        p = "l%d_" % L
        sp[p + "g1"] = ([128, KC], F32)
        sp[p + "g2"] = ([128, KC], F32)
        sp[p + "w_out"] = ([D, D], F32)
        sp[p + "w1"] = ([D, DFF], F32)
        sp[p + "w2"] = ([DFF, D], F32)
        if kind == "gla":
            sp[p + "wq"] = ([D, 128], F32); sp[p + "wk"] = ([D, 128], F32)
            sp[p + "wv"] = ([D, 256], F32); sp[p + "wg"] = ([D, 256], F32)
            sp[p + "gw1"] = ([D, 16], F32); sp[p + "gw2b"] = ([17, 128], F32); sp[p + "hnb"] = ([128, 256], F32)
        elif kind == "diff":
            sp[p + "wq"] = ([D, 256], F32); sp[p + "wk"] = ([D, 256], F32); sp[p + "wv"] = ([D, 256], F32)
            sp[p + "qaug"] = ([2, 3, T], F32); sp[p + "kaug"] = ([2, 3, T], F32); sp[p + "biasT"] = ([2, 128, 67], F32)
            sp[p + "lamp"] = ([1, 256], F32); sp[p + "hnc"] = ([128, 1], F32)
        else:
            sp[p + "w_in"] = ([D, 2 * D], F32); sp[p + "bu"] = ([128, KC], F32); sp[p + "bv"] = ([1, D], F32)
            sp[p + "vnb"] = ([128, D], F32); sp[p + "wsT"] = ([8, 128, 128], F32); sp[p + "bs"] = ([1, D], F32)
    return sp


def build_fused(nc):
    I = {k: nc.dram_tensor(k, shp, dt, kind="ExternalInput").ap() for k, (shp, dt) in fused_input_specs().items()}
    y_o = nc.dram_tensor("y_o", [TL, D], F32, kind="ExternalOutput").ap()
    P = Prog(nc)
    cx = Ctx(P, nc)
    emit_load_x(P, cx, I["x"], I["ident"])
    P.phase()
    keep = cx.trunk_base()
    for L, kind in enumerate(KINDS[:NL]):
        p = "l%d_" % L
        nm = "L%d" % L
        h_loc = nc.dram_tensor("h_loc" + nm, [NTT, D, 512], BF16).ap()
        h_loc_r = [Res(f"h_loc{nm}_{t}") for t in range(NTT)]
        if kind == "sgu":
            emit_norm_store(P, cx, I[p + "g1"], h_loc, h_loc_r, "g1" + nm)
            W = {k: I[p + k] for k in ("w_in", "bu", "bv", "vnb", "wsT", "bs")}
            W["tri"] = I["tri"]
            emit_sgu(P, cx, keep, h_loc, h_loc_r, W)
        else:
            h_all = nc.dram_tensor("h_all" + nm, [NTT, 4 * D, 512], BF16).ap()
            h_all_r = [Res(f"h_all{nm}_{t}") for t in range(NTT)]
            if kind == "gla":
                W = {k: I[p + k] for k in ("wq", "wk", "wv", "wg", "gw1", "gw2b", "hnb")}
                W.update(tinc=I["tinc"], tupp=I["tupp"], ident=I["ident"])
                pre = preload_gla(P, W)
            else:
                W = {k: I[p + k] for k in ("wq", "wk", "wv", "qaug", "kaug", "biasT", "lamp", "hnc")}
                W["tri"] = I["tri"]
                pre = preload_diff(P, W)
            emit_norm_store(P, cx, I[p + "g1"], h_loc, h_loc_r, "g1" + nm, h_all, h_all_r)
            o_loc = nc.dram_tensor("o_loc" + nm, [NTT, 256, 2048], BF16).ap()
            o_loc_r = [[Res(f"o_loc{nm}_{j}_{p}") for p in range(2)] for j in range(NTT)]
            o_all = nc.dram_tensor("o_all" + nm, [NTT, D, 2048], BF16).ap()
            o_all_r = [Res(f"o_all{nm}_{j}") for j in range(NTT)]
            if kind == "gla":
                emit_gla(P, cx, h_all, h_all_r, o_loc, o_loc_r, o_all, o_all_r, pre, nm)
            else:
                emit_diff(P, cx, h_all, h_all_r, o_loc, o_loc_r, o_all, o_all_r, W, pre, nm)
            P.release_top()
            P.phase()
            keep = cx.trunk_base()
            for tt in range(NTT):
                def src(e, q, dst=cx.bufA[:, :, tt * 512:(tt + 1) * 512],
                        srcv=o_all[tt].rearrange("(kc p) (r t) -> r p kc t", p=128, r=4)):
                    return dst, srcv[bass.ds(q, 1), :, :, :].rearrange("o p kc t -> p (o kc) t")
                P.dma_fn("sync", src, (o_all_r[tt],), tuple(cx.bufA_r[kc][tt] for kc in range(KC)), key=f"ldo_{tt}")
        emit_wout_mlp(P, cx, keep, I[p + "w_out"], I[p + "g2"], I[p + "w1"], I[p + "w2"], nm)
        P.phase(keep)
        cx.gvi = 0
    emit_final(P, cx, keep, I["gfin"], y_o, I["ident"])
    P.emit()


def _pl(v):
    return np.ascontiguousarray(np.asarray(v, np.float32).reshape(-1, 128).T)


def fused_inputs(inp):
    s = np.arange(64)[:, None]
    c = np.arange(64)[None, :]
    tok = np.arange(T)
    a = tok % 512
    a_lo = (a % 256).astype(np.float32)
    a_hi = (a - a % 256).astype(np.float32)
    cc = (tok % 128).astype(np.float32)
    shared = {"ident": np.eye(128, dtype=np.float32), "tinc": (s <= c).astype(np.float32), "tupp": (s > c).astype(np.float32),
              "tri": (np.arange(128)[:, None] <= np.arange(128)[None, :]).astype(np.float32),
              "gfin": _pl(inp["final_norm"])}
    x = np.ascontiguousarray(inp["x"].reshape(2, NTT, 4, 512, D).transpose(0, 2, 1, 3, 4)).reshape(NCORES, TL, D)
    maps = []
    for cidx in range(NCORES):
        g = cidx % 4
        m = dict(shared)
        m["x"] = np.ascontiguousarray(x[cidx])
        for L, kind in enumerate(KINDS):
            p = "l%d_" % L
            m[p + "g1"] = _pl(inp[p + "norm1"])
            m[p + "g2"] = _pl(inp[p + "norm2"])
            m[p + "w_out"] = inp[p + "w_out"]
            m[p + "w1"] = inp[p + "mlp_w1"]
            m[p + "w2"] = inp[p + "mlp_w2"]
            w_in = inp[p + "w_in"]
            if kind == "gla":
                m[p + "wq"] = np.ascontiguousarray(w_in[:, g * 128:(g + 1) * 128])
                m[p + "wk"] = np.ascontiguousarray(w_in[:, 512 + g * 128:512 + (g + 1) * 128])
                m[p + "wv"] = np.ascontiguousarray(w_in[:, 1024 + g * 256:1024 + (g + 1) * 256])
                m[p + "wg"] = np.ascontiguousarray(w_in[:, 2048 + g * 256:2048 + (g + 1) * 256])
                m[p + "gw1"] = inp[p + "gate_w1"]
                m[p + "gw2b"] = np.ascontiguousarray(np.concatenate(
                    [inp[p + "gate_w2"][:, g * 128:(g + 1) * 128], inp[p + "gate_b"][None, g * 128:(g + 1) * 128]], axis=0))
                m[p + "hnb"] = np.ascontiguousarray(np.broadcast_to(inp[p + "head_norm"][None, :], (128, 256)))
            elif kind == "diff":
                m[p + "wq"] = np.ascontiguousarray(w_in[:, g * 256:(g + 1) * 256])
                m[p + "wk"] = np.ascontiguousarray(w_in[:, 1024 + g * 256:1024 + (g + 1) * 256])
                m[p + "wv"] = np.ascontiguousarray(w_in[:, 2048 + g * 256:2048 + (g + 1) * 256])
                qa = np.zeros((2, 3, T), np.float32)
                ka = np.zeros((2, 3, T), np.float32)
                bt = np.zeros((2, 128, 67), np.float32)
                for hh in range(2):
                    slope = 2.0 ** (-(2 * g + hh + 1))
                    qa[hh, 0] = -slope * a_lo
                    qa[hh, 1] = -slope * a_hi
                    qa[hh, 2] = 1.0
                    ka[hh, 0] = 1.0
                    ka[hh, 1] = 1.0
                    ka[hh, 2] = slope * cc
                    bt[hh] = (-slope * 128.0 * (np.arange(67) - 3))[None, :]
                m[p + "qaug"] = qa
                m[p + "kaug"] = ka
                m[p + "biasT"] = bt
                m[p + "lamp"] = np.ascontiguousarray(np.concatenate(
                    [inp[p + "lambda_q1"], inp[p + "lambda_k1"], inp[p + "lambda_q2"], inp[p + "lambda_k2"]])[None, :].astype(np.float32))
                m[p + "hnc"] = np.ascontiguousarray(inp[p + "head_norm"][:, None])
            else:
                b_in = inp[p + "b_in"]
                m[p + "w_in"] = w_in
                m[p + "bu"] = np.ascontiguousarray(b_in[:D].reshape(KC, 128).T)
                m[p + "bv"] = np.ascontiguousarray(b_in[None, D:])
                m[p + "vnb"] = np.ascontiguousarray(np.broadcast_to(inp[p + "v_norm"][None, :], (128, D)))
                m[p + "wsT"] = np.ascontiguousarray(inp[p + "w_s"].transpose(0, 2, 1))
                m[p + "bs"] = np.ascontiguousarray(inp[p + "b_s"].reshape(1, D))
        maps.append(m)
    return maps


_NC = {}


def kernel(**inp):
    inp = {k: np.asarray(v) for k, v in inp.items()}
    if (T, NL) not in _NC:
        nc = bass.Bass("TRN2", target_bir_lowering=False)
        build_fused(nc)
        _NC[(T, NL)] = nc
    res = run_bass_kernel_spmd(_NC[(T, NL)], fused_inputs(inp), core_ids=list(range(NCORES))).results
    y = np.stack([r["y_o"] for r in res], axis=0).reshape(2, 4, NTT, 512, D).transpose(0, 2, 1, 3, 4)
    return np.ascontiguousarray(y.reshape(2, T, D).astype(np.float32))
```

```python
import contextlib
import numpy as np
import ml_dtypes
import concourse.bass as bass
import concourse.mybir as mybir
from concourse.bass_utils import run_bass_kernel_spmd

F32 = mybir.dt.float32
BF16 = mybir.dt.bfloat16
AF = mybir.ActivationFunctionType
ALU = mybir.AluOpType
AX = mybir.AxisListType

NCORES = 8
D = 1024
KC = 8
DFF = 4096
EPS = 1e-6
T = 8192
TL = T // 4
NTT = TL // 512
GROUPS = [[0, 1, 2, 3], [4, 5, 6, 7]]


NL = 4
DEBUG_OUT = False


def configure(t, nl=4):
    global T, TL, NTT, NL
    NL = nl
    T = t
    TL = T // 4
    NTT = TL // 512


class Res:
    __slots__ = ("name", "w", "r")

    def __init__(self, name):
        self.name = name
        self.w = None
        self.r = []


class Prog:
    COMPUTE = ("act", "dve", "pool", "pe")

    def __init__(self, nc):
        self.nc = nc
        self.stack = contextlib.ExitStack()
        self.streams = {k: [] for k in ("sync", "act", "dve", "pool", "pe")}
        self.cnt = {k: 0 for k in self.COMPUTE}
        self.esem = {k: nc.alloc_semaphore(name="s_" + k) for k in self.COMPUTE}
        self.dsem = {}
        self.dcnt = {}
        self.seen = {k: {} for k in self.streams}
        self.nbuf = 0
        self.q4 = {}
        self.batch = {}
        self.cst = self.sb([128, 4], F32, "cst")
        self.cst_r = Res("cst")
        def init(e):
            e.memset(self.cst[:, 0:1], 0.0)
            e.memset(self.cst[:, 1:2], float(EPS))
            return e.memset(self.cst[:, 2:3], 1.0)
        self.op("pool", init, (), (self.cst_r,))
        self.zero = self.cst[:, 0:1]
        self.epsc = self.cst[:, 1:2]
        self.onec = self.cst[:, 2:3]

    def sb(self, shape, dtype, name=None):
        self.nbuf += 1
        return self.stack.enter_context(self.nc.sbuf_tensor("S_" + (name or f"sb{self.nbuf}"), list(shape), dtype))

    ARENA_BYTES = 136 * 1024

    def phase(self, keep=0):
        if not hasattr(self, "arena_t"):
            self.arena_t = self.sb([128, self.ARENA_BYTES // 4], F32, "arena")
            self.arena_bf = self.arena_t.bitcast(BF16)
        self.barrier()
        self.arena_off = keep
        self.batch = {}

    def tmp(self, shape, dtype, top=False):
        esz = 2 if dtype == BF16 else 4
        n = int(np.prod(shape[1:]))
        nbytes = (n * esz + 31) // 32 * 32
        top_off = getattr(self, "top_off", self.ARENA_BYTES)
        if top:
            top_off -= nbytes
            self.top_off = top_off
            start = top_off
        else:
            start = self.arena_off
            self.arena_off += nbytes
        assert self.arena_off <= top_off, ("arena overflow", self.arena_off, top_off, nbytes)
        base = self.arena_bf if dtype != F32 else self.arena_t
        o = start // esz
        ap = base[0:shape[0], o:o + n]
        if len(shape) == 3:
            ap = ap.rearrange("p (a b) -> p a b", a=shape[1])
        return ap

    def release_top(self):
        self.top_off = self.ARENA_BYTES

    def barrier(self):
        toks = [("e", k, self.cnt[k]) for k in self.COMPUTE if self.cnt[k] > 0]
        toks += [("d", k, v) for k, v in self.dcnt.items() if v > 0 and not k.startswith("ag")]
        for stream in self.streams:
            waits = []
            for kind, key, val in toks:
                if self.seen[stream].get((kind, key), 0) >= val:
                    continue
                self.seen[stream][(kind, key)] = val
                waits.append(self._sem_of((kind, key, val)))
            self.streams[stream].append((waits, None, None, 0))

    def ps(self, shape, dtype, name=None):
        self.nbuf += 1
        return self.stack.enter_context(self.nc.psum_tensor("P_" + (name or f"ps{self.nbuf}"), list(shape), dtype))

    def _sem_of(self, tok):
        kind, key, val = tok
        return (self.esem[key] if kind == "e" else self.dsem[key]), val

    def _waits(self, stream, reads, writes):
        need = {}
        def add(tok, raw):
            if tok is None:
                return
            kind, key, val = tok
            if kind == "e" and key == stream and not raw:
                return
            k = (kind, key)
            if val > need.get(k, 0):
                need[k] = val
        for r in reads:
            add(r.w, True)
        for w in writes:
            add(w.w, False)
            for t in w.r:
                add(t, False)
        out = []
        for (kind, key), val in need.items():
            if self.seen[stream].get((kind, key), 0) >= val:
                continue
            self.seen[stream][(kind, key)] = val
            out.append(self._sem_of((kind, key, val)))
        return out

    def _commit(self, tok, reads, writes):
        for r in reads:
            r.r.append(tok)
        for w in writes:
            w.w = tok
            w.r = []

    def op(self, eng, fn, reads=(), writes=()):
        waits = self._waits(eng, reads, writes)
        self.cnt[eng] += 1
        tok = ("e", eng, self.cnt[eng])
        self.streams[eng].append((waits, fn, self.esem[eng], 1))
        self._commit(tok, reads, writes)
        return tok

    def call(self, eng, meth, *args, reads=(), writes=(), **kw):
        return self.op(eng, lambda e: getattr(e, meth)(*args, **kw), reads, writes)

    def group(self, eng, calls, reads=(), writes=()):
        calls = list(calls)
        def run(e):
            ins = None
            for meth, args, kw in calls:
                ins = getattr(e, meth)(*args, **kw)
            return ins
        return self.op(eng, run, reads, writes)

    def act(self, out, in_, func, reads, writes, bias=None, scale=1.0, accum=None, extra_reads=()):
        b = self.zero if bias is None else bias
        if b.shape[0] != out.shape[0]:
            b = b[0:out.shape[0], :]
        kw = {} if accum is None else {"accum_out": accum}
        return self.op("act", lambda e: e.activation(out, in_, func, bias=b, scale=scale, **kw),
                       tuple(reads) + (self.cst_r,) + tuple(extra_reads), writes)

    def dma(self, queue, out, in_, reads=(), writes=(), key=None, batch=False):
        key = key or ("dma_" + writes[0].name)
        if key not in self.dsem:
            self.dsem[key] = self.nc.alloc_semaphore(name="d_" + key)
            self.dcnt[key] = 0
        waits = self._waits(queue, reads, writes)
        self.dcnt[key] += 16
        tok = ["d", key, self.dcnt[key]]
        if batch:
            for t in self.batch.setdefault(key, []):
                t[2] = self.dcnt[key]
            self.batch[key].append(tok)
        self.streams[queue].append((waits, lambda e: e.dma_start(out=out, in_=in_), self.dsem[key], 16))
        self._commit(tok, reads, writes)
        return tok

    def dma_fn(self, queue, fn, reads=(), writes=(), key=None, batch=False):
        key = key or ("dma_" + writes[0].name)
        if key not in self.dsem:
            self.dsem[key] = self.nc.alloc_semaphore(name="d_" + key)
            self.dcnt[key] = 0
        waits = self._waits(queue, reads, writes)
        self.dcnt[key] += 16
        tok = ["d", key, self.dcnt[key]]
        if batch:
            for t in self.batch.setdefault(key, []):
                t[2] = self.dcnt[key]
            self.batch[key].append(tok)
        def run(e):
            if queue not in self.q4:
                self.q4[queue] = e.partition_id() % 4
            o, i = fn(e, self.q4[queue])
            return e.dma_start(out=o, in_=i)
        self.streams[queue].append((waits, run, self.dsem[key], 16))
        self._commit(tok, reads, writes)
        return tok

    def collective(self, kind, src, dst, reads, writes, key):
        assert key not in self.dsem
        self.dsem[key] = self.nc.alloc_semaphore(name="c_" + key)
        self.dcnt[key] = 1
        waits = self._waits("pool", reads, writes)
        tok = ["d", key, 1]
        self.streams["pool"].append((waits, lambda e: e.collective_compute(
            kind, ALU.bypass, replica_groups=GROUPS, ins=[src], outs=[dst]), self.dsem[key], 1))
        self._commit(tok, reads, writes)
        return tok

    def wait_all(self, stream, ress):
        waits = self._waits(stream, ress, ())
        self.streams[stream].append((waits, None, None, 0))

    def emit(self):
        nc = self.nc

        def replay(name):
            def run(e):
                for waits, fn, sem, inc in self.streams[name]:
                    for s, v in waits:
                        e.wait_ge(s, v)
                    if fn is not None:
                        ins = fn(e)
                        ins.then_inc(sem, inc)
            return run

        with nc.Block() as block:
            block.sync(replay("sync"))
            block.scalar(replay("act"))
            block.vector(replay("dve"))
            block.gpsimd(replay("pool"))
            block.tensor(replay("pe"))
        self.stack.close()


class Ctx:
    def __init__(self, P, nc):
        self.P = P
        self.nc = nc
        self.xT = P.sb([128, KC, TL], F32, "xT")
        self.xT_r = [[Res(f"xT_{k}_{t}") for t in range(NTT)] for k in range(KC)]
        self.ones = P.sb([128, 128], BF16, "ones")
        self.ones_r = Res("ones")
        P.call("pool", "memset", self.ones[:], 1.0, writes=(self.ones_r,))
        self.psb = [P.ps([128, 512], F32, f"bank{i}") for i in range(8)]
        self.psb_r = [Res(f"bank{i}") for i in range(8)]
        self.sqi = 0
        self.rsi = 0

    def trunk_base(self):
        P = self.P
        self.bufA = P.tmp([128, KC, TL], BF16)
        self.bufA_r = [[Res(f"bufA_{k}_{t}") for t in range(NTT)] for k in range(KC)]
        self.wbuf = [P.tmp([128, KC, 512], BF16) for i in range(4)]
        self.wbuf_r = [Res(f"wbuf{i}") for i in range(4)]
        self.sq = [P.tmp([128, 512], BF16) for i in range(3)]
        self.sq_r = [Res(f"sq{i}") for i in range(3)]
        self.rstd = [P.tmp([128, 512], F32) for i in range(2)]
        self.rstd_r = [Res(f"rstd{i}") for i in range(2)]
        self.gv = [P.tmp([128, KC], F32) for i in range(3)]
        self.gvi = 0
        return P.arena_off

    def load_vec(self, dram_ap, name):
        P = self.P
        t = self.gv[self.gvi]
        self.gvi += 1
        r = Res(name)
        P.dma("sync", t, dram_ap, (), (r,))
        return t, r

    def rmsnorm(self, g, g_r, dst, dst_r, bank, tts=None, tcol0=None):
        P = self.P
        for tt in (range(NTT) if tts is None else tts):
            ts = slice(tt * 512, (tt + 1) * 512)
            ds_ = ts if tcol0 is None else slice(0, 512)
            pb, pbr = self.psb[bank], self.psb_r[bank]
            for kc in range(KC):
                i = self.sqi = (self.sqi + 1) % 3
                sq, sqr = self.sq[i], self.sq_r[i]
                P.act(sq, self.xT[:, kc, ts], AF.Square, (self.xT_r[kc][tt],), (sqr,))
                P.call("pe", "matmul", pb[:], self.ones[:], sq, start=(kc == 0), stop=(kc == KC - 1),
                       reads=(sqr, self.ones_r), writes=(pbr,))
            j = self.rsi = (self.rsi + 1) % 2
            rs, rsr = self.rstd[j], self.rstd_r[j]
            P.act(rs, pb[:], AF.Sqrt, (pbr,), (rsr,), bias=P.epsc, scale=1.0 / D)
            P.call("dve", "reciprocal", rs, rs, reads=(rsr,), writes=(rsr,))
            for kc in range(KC):
                P.call("dve", "scalar_tensor_tensor", dst[:, kc, ds_], self.xT[:, kc, ts], g[:, kc:kc + 1], rs,
                       ALU.mult, ALU.mult, reads=(self.xT_r[kc][tt], rsr, g_r), writes=(dst_r[kc][tt],))


def emit_load_x(P, cx, x, ident):
    P.phase()
    identf = P.tmp([128, 128], F32)
    identf_r = Res("identf")
    P.dma("sync", identf, ident, (), (identf_r,))
    xin = [P.tmp([128, D], F32) for i in range(4)]
    xin_r = [Res(f"xin{i}") for i in range(4)]
    xv = x.rearrange("(n p) d -> n p d", p=128)
    bi = 0
    for tt in range(NTT):
        for tb in range(4):
            P.dma("sync", xin[tb], xv[tt * 4 + tb], (), (xin_r[tb],))
        for kc in range(KC):
            b = bi = (bi + 1) % 4
            pb, pbr = cx.psb[b], cx.psb_r[b]
            P.group("pe", [("transpose", (pb[:, tb * 128:(tb + 1) * 128], xin[tb][:, kc * 128:(kc + 1) * 128], identf), {})
                           for tb in range(4)], tuple(xin_r) + (identf_r,), (pbr,))
            if kc % 2 == 0:
                P.act(cx.xT[:, kc, tt * 512:(tt + 1) * 512], pb[:], AF.Identity, (pbr,), (cx.xT_r[kc][tt],))
            else:
                P.call("dve", "tensor_copy", cx.xT[:, kc, tt * 512:(tt + 1) * 512], pb[:], reads=(pbr,), writes=(cx.xT_r[kc][tt],))


def emit_norm_store(P, cx, g_dram, h_loc, h_loc_r, name, h_all=None, h_all_r=None):
    g, g_r = cx.load_vec(g_dram, name)
    for tt in range(NTT):
        cx.rmsnorm(g, g_r, cx.bufA, cx.bufA_r, bank=7, tts=[tt])
        dv = h_loc[tt].rearrange("(kc p) t -> p kc t", p=128)
        P.dma("sync", dv, cx.bufA[:, :, tt * 512:(tt + 1) * 512], tuple(cx.bufA_r[kc][tt] for kc in range(KC)), (h_loc_r[tt],),
              key=f"st_h_{tt}")
        if h_all is not None:
            P.collective("AllGather", h_loc[tt], h_all[tt], (h_loc_r[tt],), (h_all_r[tt],), f"agh{name}_{tt}")


def emit_wout_mlp(P, cx, keep, w_out, g2, w1, w2, name):
    P.phase(keep)
    wo = P.tmp([128, KC, D], BF16)
    wo_r = Res("wo")
    P.dma("pool", wo, w_out.rearrange("(kc p) n -> p kc n", p=128), (), (wo_r,))
    bi = 0
    for tt in range(NTT):
        ts = slice(tt * 512, (tt + 1) * 512)
        for n in range(KC):
            b = bi = (bi + 1) % 4
            pb, pbr = cx.psb[b], cx.psb_r[b]
            P.group("pe", [("matmul", (pb[:], wo[:, fc, n * 128:(n + 1) * 128], cx.bufA[:, fc, ts]),
                            dict(start=(fc == 0), stop=(fc == KC - 1))) for fc in range(KC)],
                    (wo_r,) + tuple(cx.bufA_r[fc][tt] for fc in range(KC)), (pbr,))
            P.call("dve", "tensor_tensor", cx.xT[:, n, ts], cx.xT[:, n, ts], pb[:], ALU.add,
                   reads=(pbr, cx.xT_r[n][tt]), writes=(cx.xT_r[n][tt],))
    g2t, g2r = cx.load_vec(g2, "g2" + name)
    cx.rmsnorm(g2t, g2r, cx.bufA, cx.bufA_r, bank=7)
    P.phase(keep)
    FG = 512
    NFG = DFF // FG
    FC = FG // 128
    w1b = [cx.wbuf[0], cx.wbuf[1]]
    w1b_r = [cx.wbuf_r[0], cx.wbuf_r[1]]
    w2b = [cx.wbuf[2 + i].rearrange("p a b -> p (a b)").rearrange("p (c n) -> p c n", c=FC) for i in range(2)]
    w2b_r = [cx.wbuf_r[2], cx.wbuf_r[3]]
    a2 = [P.tmp([128, FC, 512], BF16) for i in range(2)]
    a2_r = [[Res(f"a2_{i}_{c}") for c in range(FC)] for i in range(2)]
    rl = [P.tmp([128, 512], F32) for i in range(3)]
    rl_r = [Res(f"rl{i}") for i in range(3)]
    w1v = w1.rearrange("(kc p) f -> p kc f", p=128)
    w2v = w2.rearrange("(fc p) n -> p fc n", p=128)

    def load_w(fg):
        s = fg % 2
        P.dma("pool", w1b[s], w1v[:, :, fg * FG:(fg + 1) * FG], (), (w1b_r[s],))
        P.dma("pool", w2b[s], w2v[:, fg * FC:(fg + 1) * FC, :], (), (w2b_r[s],))

    load_w(0)
    steps = [(fg, tt) for fg in range(NFG) for tt in range(NTT)]
    st = {"abank": 0, "ybank": 0}

    def stage1(i):
        fg, tt = steps[i]
        s = fg % 2
        ai = i % 2
        ts = slice(tt * 512, (tt + 1) * 512)
        for c in range(FC):
            st["abank"] = (st["abank"] + 1) % 4
            pb, pbr = cx.psb[st["abank"]], cx.psb_r[st["abank"]]
            P.group("pe", [("matmul", (pb[:], w1b[s][:, kc, c * 128:(c + 1) * 128], cx.bufA[:, kc, ts]),
                            dict(start=(kc == 0), stop=(kc == KC - 1))) for kc in range(KC)],
                    (w1b_r[s],) + tuple(cx.bufA_r[kc][tt] for kc in range(KC)), (pbr,))
            ri = (ai * FC + c) % 3
            P.act(rl[ri], pb[:], AF.Relu, (pbr,), (rl_r[ri],))
            P.call("pool", "tensor_tensor", a2[ai][:, c, :], rl[ri], rl[ri], ALU.mult, reads=(rl_r[ri],), writes=(a2_r[ai][c],))

    def stage2(i):
        fg, tt = steps[i]
        s = fg % 2
        ai = i % 2
        ts = slice(tt * 512, (tt + 1) * 512)
        for n in range(KC):
            st["ybank"] = 4 + (st["ybank"] + 1) % 4
            pb, pbr = cx.psb[st["ybank"]], cx.psb_r[st["ybank"]]
            P.group("pe", [("matmul", (pb[:], w2b[s][:, c, n * 128:(n + 1) * 128], a2[ai][:, c, :]),
                            dict(start=(c == 0), stop=(c == FC - 1))) for c in range(FC)],
                    (w2b_r[s],) + tuple(a2_r[ai]), (pbr,))
            P.call("dve", "tensor_tensor", cx.xT[:, n, ts], cx.xT[:, n, ts], pb[:], ALU.add,
                   reads=(pbr, cx.xT_r[n][tt]), writes=(cx.xT_r[n][tt],))

    stage1(0)
    for i in range(len(steps)):
        fg, tt = steps[i]
        if tt == 0 and fg + 1 < NFG:
            load_w(fg + 1)
        if i + 1 < len(steps):
            stage1(i + 1)
        stage2(i)


def emit_final(P, cx, keep, gf, y_o, ident):
    P.phase(keep)
    g3t, g3r = cx.load_vec(gf, "gf")
    identf = P.tmp([128, 128], F32)
    identf_r = Res("identf")
    P.dma("sync", identf, ident, (), (identf_r,))
    yT = P.tmp([128, KC, 512], F32)
    yT_r = [[Res(f"yT_{k}")] * NTT for k in range(KC)]
    yo = [P.tmp([128, D], F32) for i in range(2)]
    yo_r = [Res(f"yo{i}") for i in range(2)]
    yv = y_o.rearrange("(n p) d -> n p d", p=128)
    outr = Res("st_y")
    oi = 0
    bi = 0
    for tt in range(NTT):
        cx.rmsnorm(g3t, g3r, yT, yT_r, bank=7, tts=[tt], tcol0=0)
        for tb in range(4):
            oi ^= 1
            for half in range(2):
                b = bi = (bi + 1) % 4
                pb, pbr = cx.psb[b], cx.psb_r[b]
                P.group("pe", [("transpose", (pb[:, q * 128:(q + 1) * 128], yT[:, half * 4 + q, tb * 128:(tb + 1) * 128], identf), {})
                               for q in range(4)], tuple(yT_r[k][0] for k in range(KC)) + (identf_r,), (pbr,))
                if half == 0:
                    P.act(yo[oi][:, 0:512], pb[:], AF.Identity, (pbr,), (yo_r[oi],))
                else:
                    P.call("dve", "tensor_copy", yo[oi][:, 512:1024], pb[:], reads=(pbr,), writes=(yo_r[oi],))
            P.dma("sync", yv[tt * 4 + tb], yo[oi], (yo_r[oi],), (outr,), key="st_y")
    P.wait_all("sync", (outr,))


GLA_HK = 128
GLA_HV = 256


def preload_gla(P, W):
    L = {}

    def wload(ap, ncol):
        t = P.tmp([128, KC, ncol], BF16, top=True)
        r = Res("w")
        P.dma("pool", t, ap.rearrange("(kc p) n -> p kc n", p=128), (), (r,), key="glaw", batch=True)
        return t, r

    L["wq"] = wload(W["wq"], 128)
    L["wk"] = wload(W["wk"], 128)
    L["wv"] = wload(W["wv"], 256)
    L["wg"] = wload(W["wg"], 256)
    L["gw1"] = wload(W["gw1"], 16)
    gw2_s = P.tmp([17, 128], BF16, top=True); gw2_r = Res("gw2")
    P.dma("pool", gw2_s, W["gw2b"], (), (gw2_r,), key="glaw", batch=True)
    L["gw2"] = (gw2_s, gw2_r)
    idb = P.tmp([128, 128], BF16, top=True); idb_r = Res("idb")
    P.dma("pool", idb, W["ident"], (), (idb_r,), key="glaw", batch=True)
    L["idb"] = (idb, idb_r)
    hn_s = P.tmp([128, 256], F32, top=True); hn_r = Res("hn")
    P.dma("sync", hn_s, W["hnb"], (), (hn_r,), key="glac", batch=True)
    L["hn"] = (hn_s, hn_r)
    ti_s = P.tmp([64, 64], F32, top=True); ti_r = Res("ti")
    P.dma("sync", ti_s, W["tinc"], (), (ti_r,), key="glac", batch=True)
    L["ti"] = (ti_s, ti_r)
    tu_s = P.tmp([64, 64], F32, top=True); tu_r = Res("tu")
    P.dma("sync", tu_s, W["tupp"], (), (tu_r,), key="glac", batch=True)
    L["tu"] = (tu_s, tu_r)
    ti8 = P.tmp([64, 8, 64], F32, top=True); ti8_r = Res("ti8")
    for j in range(8):
        P.dma("sync", ti8[:, j, :], W["tinc"], (), (ti8_r,), key="glac", batch=True)
    L["ti8"] = (ti8, ti8_r)
    return L


def emit_gla(P, cx, h_all, h_all_r, o_loc, o_loc_r, o_all, o_all_r, L, name):
    P.phase()
    wq_s, wq_r = L["wq"]; wk_s, wk_r = L["wk"]; wv_s, wv_r = L["wv"]; wg_s, wg_r = L["wg"]
    gw1_s, gw1_r = L["gw1"]; gw2_s, gw2_r = L["gw2"]; idb, idb_r = L["idb"]
    hn_s, hn_r = L["hn"]; ti_s, ti_r = L["ti"]; tu_s, tu_r = L["tu"]; ti8, ti8_r = L["ti8"]

    hb = [P.tmp([128, KC, 512], BF16) for i in range(2)]
    hb_r = [Res(f"hb{i}") for i in range(2)]
    hvs = [h_all[j].rearrange("(r kc p) t -> r p kc t", r=4, p=128) for j in range(NTT)]

    def h_load(dst, dst_r, tile):
        P.dma("sync", dst, hvs[tile // 4][tile % 4], (h_all_r[tile // 4],), (dst_r,))

    S = P.tmp([128, 256], F32); S_r = Res("S")
    Sb = P.tmp([128, 256], BF16); Sb_r = Res("Sb")
    P.call("pool", "memset", S, 0.0, writes=(S_r,))
    P.call("pool", "memset", Sb, 0.0, writes=(Sb_r,))
    r17 = P.tmp([17, 512], BF16); r17_r = Res("r17")
    P.call("pool", "memset", r17, 1.0, writes=(r17_r,))

    bk, br = cx.psb, cx.psb_r
    bkT = bk[5].bitcast(BF16)[:, :].rearrange("p (a b) -> p a b", a=2)

    ez = P.tmp([64, 8, 128], F32); ez_r = Res("ez")
    lsp = P.tmp([64, 8, 128], F32); lsp_r = Res("lsp")
    E1 = P.tmp([128, 512], F32); E1_r = Res("E1")
    Ei = P.tmp([128, 512], F32); Ei_r = Res("Ei")
    eU = P.tmp([64, 8, 128], F32); eU_r = Res("eU")
    qd = P.tmp([128, 512], BF16); qd_r = Res("qd")
    ki = P.tmp([128, 512], BF16); ki_r = Res("ki")
    ke = P.tmp([64, 8, 128], BF16); ke_r = Res("ke")
    kf = [P.tmp([64, 8, 128], F32) for i in range(2)]; kf_r = [[Res(f"kf{i}_{j}") for j in range(8)] for i in range(2)]
    vb = [P.tmp([64, 8, 256], BF16) for i in range(2)]; vb_r = [[Res(f"vb{i}_{j}") for j in range(8)] for i in range(2)]
    sg = [P.tmp([64, 8, 256], F32) for i in range(2)]; sg_r = [[Res(f"sg{i}_{j}") for j in range(8)] for i in range(2)]
    at = P.tmp([64, 8, 64], BF16); at_r = Res("at")
    osb_ = P.tmp([64, 8, 256], F32); osb_r = [Res(f"os{i}") for i in range(8)]
    sq = [P.tmp([64, 256], F32) for i in range(2)]; sq_r = [Res(f"sq{i}") for i in range(2)]
    ss = P.tmp([64, 8], F32); ss_r = Res("ss")
    rs = P.tmp([64, 8], F32); rs_r = Res("rs")
    tt_ = [P.tmp([64, 256], F32) for i in range(2)]; tt_r = [Res(f"tt{i}") for i in range(2)]
    of = [P.tmp([64, 256], BF16) for i in range(2)]; of_r = [Res(f"of{i}") for i in range(2)]
    ob = [P.tmp([128, 2, 512], BF16) for i in range(2)]; ob_r = [Res(f"ob{i}") for i in range(2)]
    ovs = [o_loc[j].rearrange("(h p) t -> p h t", p=128) for j in range(NTT)]
    nq = 4

    def mm8(dst, lhs_fn, rhs_fn, reads, dst_r):
        P.group("pe", [("matmul", (dst, lhs_fn(kc), rhs_fn(kc)), dict(start=(kc == 0), stop=(kc == KC - 1))) for kc in range(KC)],
                reads, (dst_r,))

    def proj_chunk(tile, j):
        p = tile % 2
        h, h_r = hb[p], hb_r[p]
        cs = slice(j * 64, (j + 1) * 64)
        mm8(bk[6][0:64, 0:128], lambda kc: h[:, kc, cs], lambda kc: wk_s[:, kc, :], (wk_r, h_r), br[6])
        mm8(bk[6][0:64, 128:384], lambda kc: h[:, kc, cs], lambda kc: wv_s[:, kc, :], (wv_r, h_r), br[6])
        P.act(kf[p][:, j, :], bk[6][0:64, 0:128], AF.Identity, (br[6],), (kf_r[p][j],))
        P.act(vb[p][:, j, :], bk[6][0:64, 128:384], AF.Identity, (br[6],), (vb_r[p][j],))
        mm8(bk[7][0:64, 0:256], lambda kc: h[:, kc, cs], lambda kc: wg_s[:, kc, :], (wg_r, h_r), br[7])
        P.act(sg[p][:, j, :], bk[7][0:64, 0:256], AF.Silu, (br[7],), (sg_r[p][j],))

    h_load(hb[0], hb_r[0], 0)
    for j in range(8):
        proj_chunk(0, j)
    for tile in range(T // 512):
        if tile + 1 < T // 512:
            h_load(hb[(tile + 1) % 2], hb_r[(tile + 1) % 2], tile + 1)
        h, h_r = hb[tile % 2], hb_r[tile % 2]
        mm8(bk[0][:], lambda kc: wq_s[:, kc, :], lambda kc: h[:, kc, :], (wq_r, h_r), br[0])
        mm8(bk[1][:], lambda kc: wk_s[:, kc, :], lambda kc: h[:, kc, :], (wk_r, h_r), br[1])
        mm8(bk[2][0:16, :], lambda kc: gw1_s[:, kc, :], lambda kc: h[:, kc, :], (gw1_r, h_r), br[2])
        P.act(r17[0:16, :], bk[2][0:16, :], AF.Identity, (br[2],), (r17_r,))
        for half in range(2):
            b = 3 + half
            P.group("pe", [("matmul", (bk[b][0:64, c * 128:(c + 1) * 128], r17[0:17, (half * 4 + c) * 64:(half * 4 + c + 1) * 64], gw2_s[0:17, :]),
                            dict(start=True, stop=True)) for c in range(4)], (r17_r, gw2_r), (br[b],))
            P.act(ez[:, half * 4:half * 4 + 4, :], bk[b][0:64, :].rearrange("p (a b) -> p a b", a=4), AF.Exp, (br[b],), (ez_r,), scale=-1.0)
        P.act(lsp[:, 0:4, :], ez[:, 0:4, :], AF.Ln, (ez_r,), (lsp_r,), bias=P.onec)
        P.act(lsp[:, 4:8, :], ez[:, 4:8, :], AF.Ln, (ez_r,), (lsp_r,), bias=P.onec)
        P.group("pe", [("matmul", (bk[5][:, c * 64:(c + 1) * 64], lsp[:, c, :], ti_s), dict(start=True, stop=True)) for c in range(8)],
                (lsp_r, ti_r), (br[5],))
        for half in range(2):
            b = 6 + half
            P.group("pe", [("matmul", (bk[b][0:64, c * 128:(c + 1) * 128], tu_s, lsp[:, half * 4 + c, :]), dict(start=True, stop=True))
                           for c in range(4)], (lsp_r, tu_r), (br[b],))
        P.act(E1, bk[5][:], AF.Exp, (br[5],), (E1_r,), scale=-1.0 / 16)
        P.act(Ei, bk[5][:], AF.Exp, (br[5],), (Ei_r,), scale=1.0 / 16)
        for half in range(2):
            b = 6 + half
            P.act(eU[:, half * 4:half * 4 + 4, :], bk[b][0:64, :].rearrange("p (a b) -> p a b", a=4), AF.Exp, (br[b],), (eU_r,), scale=-1.0 / 16)
        P.call("dve", "scalar_tensor_tensor", qd, bk[0][:], float(GLA_HK ** -0.5), E1, ALU.mult, ALU.mult,
               reads=(br[0], E1_r), writes=(qd_r,))
        P.call("dve", "tensor_tensor", ki, bk[1][:], Ei, ALU.mult, reads=(br[1], Ei_r), writes=(ki_r,))
        pp = tile % 2
        P.call("dve", "tensor_tensor", ke, kf[pp], eU, ALU.mult, reads=tuple(kf_r[pp]) + (eU_r,), writes=(ke_r,))
        P.group("pe", [("matmul", (bk[0][0:64, c * 64:(c + 1) * 64], ki[:, c * 64:(c + 1) * 64], qd[:, c * 64:(c + 1) * 64]),
                        dict(start=True, stop=True)) for c in range(8)], (ki_r, qd_r), (br[0],))
        P.call("dve", "tensor_tensor", at, bk[0][0:64, :].rearrange("p (a b) -> p a b", a=8), ti8, ALU.mult,
               reads=(br[0], ti8_r), writes=(at_r,))
        for j in range(8):
            cs = slice(j * 64, (j + 1) * 64)
            bo, bkv = 1 + j % 2, 3 + j % 2
            if tile + 1 < T // 512:
                proj_chunk(tile + 1, j)
            P.call("pe", "matmul", bk[bkv][:, 0:256], ke[:, j, :], vb[pp][:, j, :], start=True, stop=True,
                   reads=(ke_r, vb_r[pp][j]), writes=(br[bkv],))
            P.group("pe", [("matmul", (bk[bo][0:64, 0:256], qd[:, cs], Sb), dict(start=True, stop=False)),
                           ("matmul", (bk[bo][0:64, 0:256], at[:, j, :], vb[pp][:, j, :]), dict(start=False, stop=True))],
                    (qd_r, Sb_r, at_r, vb_r[pp][j]), (br[bo],))
            P.call("dve", "scalar_tensor_tensor", S, S, E1[:, j * 64 + 63:j * 64 + 64], bk[bkv][:, 0:256], ALU.mult, ALU.add,
                   reads=(S_r, E1_r, br[bkv]), writes=(S_r,))
            P.act(Sb, S, AF.Identity, (S_r,), (Sb_r,))
            P.call("dve", "tensor_copy", osb_[:, j, :], bk[bo][0:64, 0:256], reads=(br[bo],), writes=(osb_r[j],))
            P.act(sq[j % 2], osb_[:, j, :], AF.Square, (osb_r[j],), (sq_r[j % 2],))
            P.call("dve", "reduce_sum", ss[:, j:j + 1], sq[j % 2], AX.X, reads=(sq_r[j % 2],), writes=(ss_r,))
        P.act(rs, ss, AF.Sqrt, (ss_r,), (rs_r,), bias=P.epsc, scale=1.0 / GLA_HV)
        P.call("dve", "reciprocal", rs, rs, reads=(rs_r,), writes=(rs_r,))
        for j in range(8):
            cs = slice(j * 64, (j + 1) * 64)
            i = j % 2
            P.call("dve", "scalar_tensor_tensor", tt_[i], osb_[:, j, :], rs[:, j:j + 1], hn_s[0:64, :], ALU.mult, ALU.mult,
                   reads=(osb_r[j], rs_r, hn_r), writes=(tt_r[i],))
            P.call("dve", "tensor_tensor", of[i], tt_[i], sg[pp][:, j, :], ALU.mult, reads=(tt_r[i], sg_r[pp][j]), writes=(of_r[i],))
            P.group("pe", [("transpose", (bkT[:, 0, cs], of[i][:, 0:128], idb[0:64, 0:64]), {}),
                           ("transpose", (bkT[:, 1, cs], of[i][:, 128:256], idb[0:64, 0:64]), {})],
                    (of_r[i], idb_r), (br[5],))
        o_, o_r = ob[tile % 2], ob_r[tile % 2]
        P.act(o_[:, 0, :], bkT[:, 0, :], AF.Identity, (br[5],), (o_r,))
        P.call("dve", "tensor_copy", o_[:, 1, :], bkT[:, 1, :], reads=(br[5],), writes=(o_r,))
        P.dma("sync", ovs[tile // nq][:, :, (tile % nq) * 512:(tile % nq + 1) * 512], o_, (o_r,), (o_loc_r[tile // nq][tile % 2],),
              key=f"st_o{name}_{tile % 2}")
        if tile % nq == nq - 1:
            P.collective("AllGather", o_loc[tile // nq], o_all[tile // nq], tuple(o_loc_r[tile // nq]), (o_all_r[tile // nq],),
                         f"ago{name}_{tile // nq}")


DIFF_LAMBDA_INIT = 0.8 - 0.6 * float(np.exp(-0.3 * 1))


def preload_diff(P, W):
    L = {}

    def wload(ap, ncol):
        t = P.tmp([128, KC, ncol], BF16, top=True)
        r = Res("w")
        P.dma("pool", t, ap.rearrange("(kc p) n -> p kc n", p=128), (), (r,), key="difw", batch=True)
        return t, r

    L["wq"] = wload(W["wq"], 256)
    L["wk"] = wload(W["wk"], 256)
    L["wv"] = wload(W["wv"], 256)
    tri_s = P.tmp([128, 128], BF16, top=True); tri_r = Res("tri")
    P.dma("pool", tri_s, W["tri"], (), (tri_r,), key="difw", batch=True)
    L["tri"] = (tri_s, tri_r)
    return L


def emit_diff(P, cx, h_all, h_all_r, o_loc, o_loc_r, o_all, o_all_r, W, L, name):
    P.phase()
    NT = T // 512
    wq_s, wq_r = L["wq"]; wk_s, wk_r = L["wk"]; wv_s, wv_r = L["wv"]; tri_s, tri_r = L["tri"]
    ones, ones_r = cx.ones, cx.ones_r
    onef = P.tmp([1, 128], F32); onef_r = Res("onef")
    P.call("pool", "memset", onef, 1.0, writes=(onef_r,))
    bias_s = P.tmp([128, 2, 67], F32); bias_r = Res("bias")
    P.dma("sync", bias_s, W["biasT"].rearrange("h p n -> p h n"), (), (bias_r,), key="difc", batch=True)
    hn_s = P.tmp([128, 1], F32); hn_r = Res("hn")
    P.dma("sync", hn_s, W["hnc"], (), (hn_r,), key="difc", batch=True)
    bk, bk_r = cx.psb, cx.psb_r

    lp = P.tmp([1, 256], F32); lp_r = Res("lp")
    P.dma("sync", lp, W["lamp"], (), (lp_r,), key="difc", batch=True)
    P.call("dve", "tensor_scalar", hn_s, hn_s, float(1.0 - DIFF_LAMBDA_INIT), None, ALU.mult, reads=(hn_r,), writes=(hn_r,))
    lw = P.tmp([1, 136], F32); lw_r = Res("lw")
    P.call("dve", "tensor_tensor", lw[:, 0:64], lp[:, 0:64], lp[:, 64:128], ALU.mult, reads=(lp_r,), writes=(lw_r,))
    P.call("dve", "tensor_tensor", lw[:, 64:128], lp[:, 128:192], lp[:, 192:256], ALU.mult, reads=(lp_r, lw_r), writes=(lw_r,))
    P.call("dve", "reduce_sum", lw[:, 128:129], lw[:, 0:64], AX.X, reads=(lw_r,), writes=(lw_r,))
    P.call("dve", "reduce_sum", lw[:, 129:130], lw[:, 64:128], AX.X, reads=(lw_r,), writes=(lw_r,))
    P.act(lw[:, 130:132], lw[:, 128:130], AF.Exp, (lw_r,), (lw_r,))
    P.call("dve", "tensor_tensor", lw[:, 132:133], lw[:, 131:132], lw[:, 130:131], ALU.subtract, reads=(lw_r,), writes=(lw_r,))
    P.call("dve", "tensor_scalar", lw[:, 133:134], lw[:, 132:133], float(-DIFF_LAMBDA_INIT), None, ALU.add, reads=(lw_r,), writes=(lw_r,))
    P.call("pe", "matmul", bk[0][:, 0:1], onef[0:1, :], lw[0:1, 133:134], start=True, stop=True,
           reads=(onef_r, lw_r), writes=(bk_r[0],))
    nlam = P.tmp([128, 1], F32); nlam_r = Res("nlam")
    P.call("dve", "tensor_copy", nlam, bk[0][:, 0:1], reads=(bk_r[0],), writes=(nlam_r,))

    Qa = [P.tmp([67, T], BF16) for r in range(2)]
    Ka = [P.tmp([67, T], BF16) for r in range(2)]
    Qa_r = [[Res(f"Qa{r}_{t}") for t in range(NT)] for r in range(2)]
    Ka_r = [[Res(f"Ka{r}_{t}") for t in range(NT)] for r in range(2)]
    Qg_r = [Res(f"Qg{r}") for r in range(2)]
    Kg_r = [Res(f"Kg{r}") for r in range(2)]
    V = P.tmp([128, T // 128, 128], BF16)
    V_r = [Res(f"V{t}") for t in range(NT)]
    hb = [P.tmp([128, KC, 512], BF16) for i in range(2)]
    hb_r = [Res(f"hb{i}") for i in range(2)]
    hvs = [h_all[j].rearrange("(r kc p) t -> r p kc t", r=4, p=128) for j in range(NTT)]

    def h_load(dst, dst_r, tile):
        P.dma("sync", dst, hvs[tile // 4][tile % 4], (h_all_r[tile // 4],), (dst_r,))

    Pt = [P.tmp([128, 512], BF16) for i in range(4)]
    Pt_r = [Res(f"Pt{i}") for i in range(4)]
    rec = [P.tmp([128, 512], F32) for i in range(2)]
    rec_r = [Res(f"rec{i}") for i in range(2)]
    on = [P.tmp([128, 512], F32) for i in range(2)]
    on_r = [Res(f"on{i}") for i in range(2)]
    oo = P.tmp([128, 512], F32); oo_r = Res("oo")
    sq = P.tmp([128, 512], BF16); sq_r = Res("sq")
    rs = P.tmp([128, 512], F32); rs_r = Res("rs")
    ofin = [P.tmp([128, 512], BF16) for i in range(2)]
    ofin_r = [Res(f"ofin{i}") for i in range(2)]
    hcnt = 0
    sbank = 0
    pti = 0
    for hh in range(2):
        for r in range(2):
            P.dma("sync", Qa[r][64:67, :], W["qaug"][hh], (), (Qg_r[r],), key="difa", batch=True)
            P.dma("sync", Ka[r][64:67, :], W["kaug"][hh], (), (Kg_r[r],), key="difa", batch=True)
        for tile in range(NT):
            s = hcnt % 2
            hcnt += 1
            h_load(hb[s], hb_r[s], tile)
            h, h_r = hb[s], hb_r[s]
            cols = slice(tile * 512, (tile + 1) * 512)
            for r in range(2):
                wc = slice((hh * 2 + r) * 64, (hh * 2 + r + 1) * 64)
                for (w_s, w_r, dst, dst_r, sc) in ((wq_s, wq_r, Qa, Qa_r, 0.125), (wk_s, wk_r, Ka, Ka_r, 1.0)):
                    sbank = (sbank + 1) % 4
                    pb, pbr = bk[sbank], bk_r[sbank]
                    P.group("pe", [("matmul", (pb[0:64, :], w_s[:, kc, wc], h[:, kc, :]), dict(start=(kc == 0), stop=(kc == KC - 1)))
                                   for kc in range(KC)], (w_r, h_r), (pbr,))
                    P.act(dst[r][0:64, cols], pb[0:64, :], AF.Identity, (pbr,), (dst_r[r][tile],), scale=sc)
            sbank = (sbank + 1) % 4
            pb, pbr = bk[sbank], bk_r[sbank]
            for tb in range(4):
                P.group("pe", [("matmul", (pb[:, tb * 128:(tb + 1) * 128], h[:, kc, tb * 128:(tb + 1) * 128], wv_s[:, kc, hh * 128:(hh + 1) * 128]),
                                dict(start=(kc == 0), stop=(kc == KC - 1))) for kc in range(KC)], (wv_r, h_r), (pbr,))
            P.call("dve", "tensor_copy", V[:, tile * 4:(tile + 1) * 4, :], pb[:].rearrange("p (a b) -> p a b", a=4),
                   reads=(pbr,), writes=(V_r[tile],))
        LA = 2
        for It in range(NT):
            nJ = 4 * It + 4
            pend = []

            def consume(item, It=It, nJ=nJ):
                Jt, r, c0, pt, ptr = item
                P.call("pe", "matmul", bk[4 + r][:, c0:512], V[:, Jt, :], pt[:, c0:512], start=(Jt == 0), stop=(Jt == nJ - 1),
                       reads=(V_r[Jt // 4], ptr), writes=(bk_r[4 + r],))
                P.call("pe", "matmul", bk[6 + r][:, c0:512], ones[:], pt[:, c0:512], start=(Jt == 0), stop=(Jt == nJ - 1),
                       reads=(ones_r, ptr), writes=(bk_r[6 + r],))

            for Jt in range(nJ):
                m = Jt - 4 * It
                c0 = 128 * m if m > 0 else 0
                idx = 4 * It - Jt + 3
                for r in range(2):
                    sbank = (sbank + 1) % 4
                    pb, pbr = bk[sbank], bk_r[sbank]
                    pti = (pti + 1) % 4
                    pt, ptr = Pt[pti], Pt_r[pti]
                    P.call("pe", "matmul", pb[:, c0:512], Ka[r][0:67, Jt * 128:(Jt + 1) * 128],
                           Qa[r][0:67, It * 512 + c0:(It + 1) * 512], start=True, stop=True,
                           reads=(Ka_r[r][Jt // 4], Kg_r[r], Qa_r[r][It], Qg_r[r]), writes=(pbr,))
                    P.act(pt[:, c0:512], pb[:, c0:512], AF.Exp, (pbr,), (ptr,), bias=bias_s[:, hh, idx:idx + 1], extra_reads=(bias_r,))
                    if m >= 0:
                        P.call("dve", "tensor_tensor", pt[:, c0:c0 + 128], pt[:, c0:c0 + 128], tri_s, ALU.mult,
                               reads=(ptr, tri_r), writes=(ptr,))
                    pend.append((Jt, r, c0, pt, ptr))
                    if len(pend) > LA:
                        consume(pend.pop(0))
            while pend:
                consume(pend.pop(0))
            for r in range(2):
                P.call("dve", "reciprocal", rec[r], bk[6 + r][:], reads=(bk_r[6 + r],), writes=(rec_r[r],))
                P.call("dve", "tensor_tensor", on[r], bk[4 + r][:], rec[r], ALU.mult, reads=(bk_r[4 + r], rec_r[r]), writes=(on_r[r],))
            P.call("dve", "scalar_tensor_tensor", oo, on[1], nlam[:, 0:1], on[0], ALU.mult, ALU.add,
                   reads=(on_r[0], on_r[1], nlam_r), writes=(oo_r,))
            P.act(sq, oo, AF.Square, (oo_r,), (sq_r,))
            sbank = (sbank + 1) % 4
            pb, pbr = bk[sbank], bk_r[sbank]
            P.call("pe", "matmul", pb[:], ones[:], sq, start=True, stop=True, reads=(ones_r, sq_r), writes=(pbr,))
            P.act(rs, pb[:], AF.Sqrt, (pbr,), (rs_r,), bias=P.epsc, scale=1.0 / 128)
            P.call("dve", "reciprocal", rs, rs, reads=(rs_r,), writes=(rs_r,))
            ob, ob_r = ofin[It % 2], ofin_r[It % 2]
            P.call("dve", "scalar_tensor_tensor", ob, oo, hn_s[:, 0:1], rs, ALU.mult, ALU.mult,
                   reads=(oo_r, hn_r, rs_r), writes=(ob_r,))
            nq = 4
            P.dma("sync", o_loc[It // nq][hh * 128:(hh + 1) * 128, (It % nq) * 512:(It % nq + 1) * 512], ob, (ob_r,),
                  (o_loc_r[It // nq][It % 2],), key=f"st_o{name}_{It % 2}")
            if hh == 1 and It % nq == nq - 1:
                P.collective("AllGather", o_loc[It // nq], o_all[It // nq], tuple(o_loc_r[It // nq]), (o_all_r[It // nq],),
                             f"ago{name}_{It // nq}")


GELU_C = 0.044715
GELU_S = 1.5957691216057308


def gelu_tanh(P, out, x, x_r, out_r, t1, t1_r):
    P.act(t1, x, AF.Square, (x_r,), (t1_r,))
    P.call("dve", "tensor_scalar", t1, t1, GELU_C, 1.0, ALU.mult, ALU.add, reads=(t1_r,), writes=(t1_r,))
    P.call("dve", "tensor_tensor", t1, t1, x, ALU.mult, reads=(t1_r, x_r), writes=(t1_r,))
    P.act(t1, t1, AF.Sigmoid, (t1_r,), (t1_r,), scale=GELU_S)
    P.call("pool", "tensor_tensor", out, x, t1, ALU.mult, reads=(x_r, t1_r), writes=(out_r,))


def emit_sgu(P, cx, keep, h_loc, h_loc_r, W):
    P.phase(keep)
    wv_ = W["w_in"].rearrange("(kc p) n -> p kc n", p=128)
    for j in range(4):
        P.dma("pool", cx.wbuf[j], wv_[:, :, j * 512:(j + 1) * 512], (), (cx.wbuf_r[j],))
    bu_t = P.tmp([128, KC], F32); bu_r = Res("bu")
    P.dma("sync", bu_t, W["bu"], (), (bu_r,), key="sguc", batch=True)
    bv_t = P.tmp([1, D], BF16); bv_r = Res("bv")
    P.dma("pool", bv_t, W["bv"], (), (bv_r,), key="sguw", batch=True)
    bs_t = P.tmp([1, D], BF16); bs_r = Res("bs")
    P.dma("pool", bs_t, W["bs"], (), (bs_r,), key="sguw", batch=True)
    vnb_t = P.tmp([128, D], F32); vnb_r = Res("vnb")
    P.dma("sync", vnb_t, W["vnb"], (), (vnb_r,), key="sguc", batch=True)
    tri_t = P.tmp([128, 128], BF16); tri_r = Res("tri")
    P.dma("pool", tri_t, W["tri"], (), (tri_r,), key="sguw", batch=True)
    ws_t = P.tmp([128, 8, 128], BF16); ws_r = Res("ws")
    P.dma("pool", ws_t, W["wsT"].rearrange("g s t -> s g t"), (), (ws_r,), key="sguw", batch=True)
    for g in range(8):
        P.call("dve", "tensor_tensor", ws_t[:, g, :], ws_t[:, g, :], tri_t, ALU.mult, reads=(ws_r, tri_r), writes=(ws_r,))
    hb = [P.tmp([128, KC, 512], BF16) for i in range(2)]
    hb_r = [Res(f"shb{i}") for i in range(2)]
    hvl = [h_loc[tt].rearrange("(kc p) t -> p kc t", p=128) for tt in range(NTT)]
    vn = [P.tmp([128, D], BF16) for i in range(4)]
    vn_r = [Res(f"vn{i}") for i in range(4)]
    xv = [P.tmp([128, 512], F32) for i in range(2)]
    xv_r = [Res(f"xv{i}") for i in range(2)]
    t1 = [P.tmp([128, 512], F32) for i in range(2)]
    t1_r = [Res(f"t1{i}") for i in range(2)]
    gv = [P.tmp([128, D], F32) for i in range(2)]
    gv_r = [Res(f"gv{i}") for i in range(2)]
    sqv = P.tmp([128, D], F32); sqv_r = Res("sqv")
    ssv = [P.tmp([128, 2], F32) for i in range(2)]
    ssv_r = [Res(f"ssv{i}") for i in range(2)]
    ug = [P.tmp([128, 512], F32) for i in range(2)]
    ug_r = [Res(f"ug{i}") for i in range(2)]
    bank = 0
    xi = 0
    P.dma("sync", hb[0], hvl[0], (h_loc_r[0],), (hb_r[0],))
    for tt in range(NTT):
        if tt + 1 < NTT:
            P.dma("sync", hb[(tt + 1) % 2], hvl[tt + 1], (h_loc_r[tt + 1],), (hb_r[(tt + 1) % 2],))
        h, h_r = hb[tt % 2], hb_r[tt % 2]
        ts = slice(tt * 512, (tt + 1) * 512)
        for cb in range(4):
            tok = slice(cb * 128, (cb + 1) * 128)
            gi = cb % 2
            for half in range(2):
                bank = (bank + 1) % 4
                pb, pbr = cx.psb[bank], cx.psb_r[bank]
                calls = [("matmul", (pb[:], h[:, kc, tok], cx.wbuf[2 + half][:, kc, :]), dict(start=(kc == 0), stop=False))
                         for kc in range(KC)]
                calls.append(("matmul", (pb[:], cx.ones[0:1, :], bv_t[0:1, half * 512:(half + 1) * 512]), dict(start=False, stop=True)))
                P.group("pe", calls, (h_r, cx.wbuf_r[2 + half], cx.ones_r, bv_r), (pbr,))
                xi ^= 1
                P.act(xv[xi], pb[:], AF.Identity, (pbr,), (xv_r[xi],))
                gelu_tanh(P, gv[gi][:, half * 512:(half + 1) * 512], xv[xi], xv_r[xi], gv_r[gi], t1[xi], t1_r[xi])
            P.act(sqv, gv[gi], AF.Square, (gv_r[gi],), (sqv_r,))
            P.call("dve", "reduce_sum", ssv[gi][:, 0:1], sqv, AX.X, reads=(sqv_r,), writes=(ssv_r[gi],))
            P.act(ssv[gi][:, 1:2], ssv[gi][:, 0:1], AF.Sqrt, (ssv_r[gi],), (ssv_r[gi],), bias=P.epsc, scale=1.0 / D)
            P.call("dve", "reciprocal", ssv[gi][:, 0:1], ssv[gi][:, 1:2], reads=(ssv_r[gi],), writes=(ssv_r[gi],))
            P.call("dve", "scalar_tensor_tensor", vn[cb], gv[gi], ssv[gi][:, 0:1], vnb_t, ALU.mult, ALU.mult,
                   reads=(gv_r[gi], ssv_r[gi], vnb_r), writes=(vn_r[cb],))
        for g in range(8):
            bank = (bank + 1) % 4
            pb, pbr = cx.psb[bank], cx.psb_r[bank]
            P.group("pe", [("matmul", (pb[:], cx.wbuf[g // 4][:, kc, (g % 4) * 128:(g % 4 + 1) * 128], h[:, kc, :]),
                            dict(start=(kc == 0), stop=(kc == KC - 1))) for kc in range(KC)], (h_r, cx.wbuf_r[g // 4]), (pbr,))
            xi ^= 1
            P.act(xv[xi], pb[:], AF.Identity, (pbr,), (xv_r[xi],), bias=bu_t[:, g:g + 1], extra_reads=(bu_r,))
            gelu_tanh(P, ug[xi], xv[xi], xv_r[xi], ug_r[xi], t1[xi], t1_r[xi])
            sb_ = 4 + g % 3
            sp, spr = cx.psb[sb_], cx.psb_r[sb_]
            calls = []
            for cb in range(4):
                calls.append(("matmul", (sp[:, cb * 128:(cb + 1) * 128], vn[cb][:, g * 128:(g + 1) * 128], ws_t[:, g, :]),
                              dict(start=True, stop=False)))
                calls.append(("matmul", (sp[:, cb * 128:(cb + 1) * 128], cx.ones[0:1, :], bs_t[0:1, g * 128:(g + 1) * 128]),
                              dict(start=False, stop=True)))
            P.group("pe", calls, tuple(vn_r) + (ws_r, cx.ones_r, bs_r), (spr,))
            P.call("dve", "tensor_tensor", cx.bufA[:, g, ts], ug[xi], sp[:], ALU.mult, reads=(ug_r[xi], spr), writes=(cx.bufA_r[g][tt],))


KINDS = ("gla", "diff", "sgu", "gla")


def fused_input_specs():
    sp = {"x": ([TL, D], F32), "ident": ([128, 128], F32), "tinc": ([64, 64], F32), "tupp": ([64, 64], F32),
          "tri": ([128, 128], F32), "gfin": ([128, KC], F32)}
    for L, kind in enumerate(KINDS):
        p = "l%d_" % L
        sp[p + "g1"] = ([128, KC], F32)
        sp[p + "g2"] = ([128, KC], F32)
        sp[p + "w_out"] = ([D, D], F32)
        sp[p + "w1"] = ([D, DFF], F32)
        sp[p + "w2"] = ([DFF, D], F32)
        if kind == "gla":
            sp[p + "wq"] = ([D, 128], F32); sp[p + "wk"] = ([D, 128], F32)
            sp[p + "wv"] = ([D, 256], F32); sp[p + "wg"] = ([D, 256], F32)
            sp[p + "gw1"] = ([D, 16], F32); sp[p + "gw2b"] = ([17, 128], F32); sp[p + "hnb"] = ([128, 256], F32)
        elif kind == "diff":
            sp[p + "wq"] = ([D, 256], F32); sp[p + "wk"] = ([D, 256], F32); sp[p + "wv"] = ([D, 256], F32)
            sp[p + "qaug"] = ([2, 3, T], BF16); sp[p + "kaug"] = ([2, 3, T], BF16); sp[p + "biasT"] = ([2, 128, 67], F32)
            sp[p + "lamp"] = ([1, 256], F32); sp[p + "hnc"] = ([128, 1], F32)
        else:
            sp[p + "w_in"] = ([D, 2 * D], F32); sp[p + "bu"] = ([128, KC], F32); sp[p + "bv"] = ([1, D], F32)
            sp[p + "vnb"] = ([128, D], F32); sp[p + "wsT"] = ([8, 128, 128], F32); sp[p + "bs"] = ([1, D], F32)
    return sp


def build_fused(nc):
    I = {k: nc.dram_tensor(k, shp, dt, kind="ExternalInput").ap() for k, (shp, dt) in fused_input_specs().items()}
    y_o = nc.dram_tensor("y_o", [TL, D], F32, kind="ExternalOutput").ap()
    P = Prog(nc)
    cx = Ctx(P, nc)
    emit_load_x(P, cx, I["x"], I["ident"])
    P.phase()
    keep = cx.trunk_base()
    for L, kind in enumerate(KINDS[:NL]):
        p = "l%d_" % L
        nm = "L%d" % L
        h_loc = nc.dram_tensor("h_loc" + nm, [NTT, D, 512], BF16).ap()
        h_loc_r = [Res(f"h_loc{nm}_{t}") for t in range(NTT)]
        if kind == "sgu":
            emit_norm_store(P, cx, I[p + "g1"], h_loc, h_loc_r, "g1" + nm)
            W = {k: I[p + k] for k in ("w_in", "bu", "bv", "vnb", "wsT", "bs")}
            W["tri"] = I["tri"]
            emit_sgu(P, cx, keep, h_loc, h_loc_r, W)
        else:
            h_all = nc.dram_tensor("h_all" + nm, [NTT, 4 * D, 512], BF16).ap()
            h_all_r = [Res(f"h_all{nm}_{t}") for t in range(NTT)]
            if kind == "gla":
                W = {k: I[p + k] for k in ("wq", "wk", "wv", "wg", "gw1", "gw2b", "hnb")}
                W.update(tinc=I["tinc"], tupp=I["tupp"], ident=I["ident"])
                pre = preload_gla(P, W)
            else:
                W = {k: I[p + k] for k in ("wq", "wk", "wv", "qaug", "kaug", "biasT", "lamp", "hnc")}
                W["tri"] = I["tri"]
                pre = preload_diff(P, W)
            emit_norm_store(P, cx, I[p + "g1"], h_loc, h_loc_r, "g1" + nm, h_all, h_all_r)
            o_loc = nc.dram_tensor("o_loc" + nm, [NTT, 256, 2048], BF16).ap()
            o_loc_r = [[Res(f"o_loc{nm}_{j}_{p}") for p in range(2)] for j in range(NTT)]
            o_all = nc.dram_tensor("o_all" + nm, [NTT, D, 2048], BF16).ap()
            o_all_r = [Res(f"o_all{nm}_{j}") for j in range(NTT)]
            if kind == "gla":
                emit_gla(P, cx, h_all, h_all_r, o_loc, o_loc_r, o_all, o_all_r, pre, nm)
            else:
                emit_diff(P, cx, h_all, h_all_r, o_loc, o_loc_r, o_all, o_all_r, W, pre, nm)
            P.release_top()
            P.phase()
            keep = cx.trunk_base()
            for tt in range(NTT):
                def src(e, q, dst=cx.bufA[:, :, tt * 512:(tt + 1) * 512],
                        srcv=o_all[tt].rearrange("(kc p) (r t) -> r p kc t", p=128, r=4)):
                    return dst, srcv[bass.ds(q, 1), :, :, :].rearrange("o p kc t -> p (o kc) t")
                P.dma_fn("sync", src, (o_all_r[tt],), tuple(cx.bufA_r[kc][tt] for kc in range(KC)), key=f"ldo_{tt}")
        emit_wout_mlp(P, cx, keep, I[p + "w_out"], I[p + "g2"], I[p + "w1"], I[p + "w2"], nm)
        P.phase(keep)
        cx.gvi = 0
    emit_final(P, cx, keep, I["gfin"], y_o, I["ident"])
    P.emit()


def _pl(v):
    return np.ascontiguousarray(np.asarray(v, np.float32).reshape(-1, 128).T)


def fused_inputs(inp):
    s = np.arange(64)[:, None]
    c = np.arange(64)[None, :]
    tok = np.arange(T)
    a = tok % 512
    a_lo = (a % 256).astype(np.float32)
    a_hi = (a - a % 256).astype(np.float32)
    cc = (tok % 128).astype(np.float32)
    shared = {"ident": np.eye(128, dtype=np.float32), "tinc": (s <= c).astype(np.float32), "tupp": (s > c).astype(np.float32),
              "tri": (np.arange(128)[:, None] <= np.arange(128)[None, :]).astype(np.float32),
              "gfin": _pl(inp["final_norm"])}
    x = np.ascontiguousarray(inp["x"].reshape(2, NTT, 4, 512, D).transpose(0, 2, 1, 3, 4)).reshape(NCORES, TL, D)
    maps = []
    for cidx in range(NCORES):
        g = cidx % 4
        m = dict(shared)
        m["x"] = np.ascontiguousarray(x[cidx])
        for L, kind in enumerate(KINDS):
            p = "l%d_" % L
            m[p + "g1"] = _pl(inp[p + "norm1"])
            m[p + "g2"] = _pl(inp[p + "norm2"])
            m[p + "w_out"] = inp[p + "w_out"]
            m[p + "w1"] = inp[p + "mlp_w1"]
            m[p + "w2"] = inp[p + "mlp_w2"]
            w_in = inp[p + "w_in"]
            if kind == "gla":
                m[p + "wq"] = np.ascontiguousarray(w_in[:, g * 128:(g + 1) * 128])
                m[p + "wk"] = np.ascontiguousarray(w_in[:, 512 + g * 128:512 + (g + 1) * 128])
                m[p + "wv"] = np.ascontiguousarray(w_in[:, 1024 + g * 256:1024 + (g + 1) * 256])
                m[p + "wg"] = np.ascontiguousarray(w_in[:, 2048 + g * 256:2048 + (g + 1) * 256])
                m[p + "gw1"] = inp[p + "gate_w1"]
                m[p + "gw2b"] = np.ascontiguousarray(np.concatenate(
                    [inp[p + "gate_w2"][:, g * 128:(g + 1) * 128], inp[p + "gate_b"][None, g * 128:(g + 1) * 128]], axis=0))
                m[p + "hnb"] = np.ascontiguousarray(np.broadcast_to(inp[p + "head_norm"][None, :], (128, 256)))
            elif kind == "diff":
                m[p + "wq"] = np.ascontiguousarray(w_in[:, g * 256:(g + 1) * 256])
                m[p + "wk"] = np.ascontiguousarray(w_in[:, 1024 + g * 256:1024 + (g + 1) * 256])
                m[p + "wv"] = np.ascontiguousarray(w_in[:, 2048 + g * 256:2048 + (g + 1) * 256])
                qa = np.zeros((2, 3, T), np.float32)
                ka = np.zeros((2, 3, T), np.float32)
                bt = np.zeros((2, 128, 67), np.float32)
                for hh in range(2):
                    slope = 2.0 ** (-(2 * g + hh + 1))
                    qa[hh, 0] = -slope * a_lo
                    qa[hh, 1] = -slope * a_hi
                    qa[hh, 2] = 1.0
                    ka[hh, 0] = 1.0
                    ka[hh, 1] = 1.0
                    ka[hh, 2] = slope * cc
                    bt[hh] = (-slope * 128.0 * (np.arange(67) - 3))[None, :]
                m[p + "qaug"] = qa.astype(ml_dtypes.bfloat16)
                m[p + "kaug"] = ka.astype(ml_dtypes.bfloat16)
                m[p + "biasT"] = bt
                m[p + "lamp"] = np.ascontiguousarray(np.concatenate(
                    [inp[p + "lambda_q1"], inp[p + "lambda_k1"], inp[p + "lambda_q2"], inp[p + "lambda_k2"]])[None, :].astype(np.float32))
                m[p + "hnc"] = np.ascontiguousarray(inp[p + "head_norm"][:, None])
            else:
                b_in = inp[p + "b_in"]
                m[p + "w_in"] = w_in
                m[p + "bu"] = np.ascontiguousarray(b_in[:D].reshape(KC, 128).T)
                m[p + "bv"] = np.ascontiguousarray(b_in[None, D:])
                m[p + "vnb"] = np.ascontiguousarray(np.broadcast_to(inp[p + "v_norm"][None, :], (128, D)))
                m[p + "wsT"] = np.ascontiguousarray(inp[p + "w_s"].transpose(0, 2, 1))
                m[p + "bs"] = np.ascontiguousarray(inp[p + "b_s"].reshape(1, D))
        maps.append(m)
    return maps


_NC = {}


def kernel(**inp):
    inp = {k: np.asarray(v) for k, v in inp.items()}
    if (T, NL) not in _NC:
        nc = bass.Bass("TRN2", target_bir_lowering=False)
        build_fused(nc)
        _NC[(T, NL)] = nc
    res = run_bass_kernel_spmd(_NC[(T, NL)], fused_inputs(inp), core_ids=list(range(NCORES))).results
    y = np.stack([r["y_o"] for r in res], axis=0).reshape(2, 4, NTT, 512, D).transpose(0, 2, 1, 3, 4)
    return np.ascontiguousarray(y.reshape(2, T, D).astype(np.float32))
```
